# Optimizing a Trainium2 kernel written in Bass

```python
import math
import jax, jax.numpy as jnp
from jax import lax
import numpy as np

D_MODEL = 4096
BATCH = 8
SEQ = 2048
DEPTH = 1

CHUNK = 64
Q_BLOCK = 128
D_MIX = D_MODEL
A_HEADS = 16
A_HEAD_DIM = D_MIX // 2 // A_HEADS
IDX_HEADS = 32
IDX_DIM = 64
TOPK_MAX = 256
T5_BUCKETS = 32
T5_MAX_DIST = 128
B_HEADS = 4
B_HEAD_V = D_MIX // 2 // B_HEADS
B_HEAD_K = B_HEAD_V // 2
GATE_RANK = 16
GATE_TAU = 16.0
D_FF = 11008
CONV_W = 3
ALPHA = (2 * DEPTH) ** 0.25
BETA = (8 * DEPTH) ** -0.25
EPS = 1e-6

A_Q = A_HEADS * A_HEAD_DIM
A_K = A_HEAD_DIM
A_V = A_HEAD_DIM
IDX_Q = IDX_HEADS * IDX_DIM
IDX_K = IDX_DIM
IDX_W = IDX_HEADS
B_Q = B_HEADS * B_HEAD_K
B_K = B_HEADS * B_HEAD_K
B_V = B_HEADS * B_HEAD_V
B_G = GATE_RANK
B_R = B_HEADS * B_HEAD_V
IN_SPLITS = (A_Q, A_K, A_V, IDX_Q, IDX_K, IDX_W, B_Q, B_K, B_V, B_G, B_R)
D_IN = sum(IN_SPLITS)
VALUE_SLOTS = (2, 8)

kernel_name = "hybrid_dsa_gla_convffn_deepnorm_adaln"


def layer_norm(x):
    xf = x.astype(jnp.float32)
    mu = jnp.mean(xf, axis=-1, keepdims=True)
    var = jnp.mean(jnp.square(xf - mu), axis=-1, keepdims=True)
    return ((xf - mu) * lax.rsqrt(var + EPS)).astype(x.dtype)


def layer_norm_affine(x, g, b):
    return layer_norm(x) * g + b


def modulate(x, shift, scale):
    return layer_norm(x) * (1 + scale) + shift


def t5_bucket(rel):
    half = T5_BUCKETS // 2
    max_exact = half // 2
    ret = jnp.where(rel > 0, half, 0)
    n = jnp.abs(rel)
    nf = jnp.maximum(n, 1).astype(jnp.float32)
    large = max_exact + (jnp.log(nf / max_exact) / math.log(T5_MAX_DIST / max_exact)
                         * (half - max_exact)).astype(jnp.int32)
    large = jnp.minimum(large, half - 1)
    return ret + jnp.where(n < max_exact, n, large)


def dsa_mixer(q, k, v, q_idx, k_idx, w_idx, t5_table):
    B, S = q.shape[0], q.shape[1]
    topk = min(TOPK_MAX, S // 4)
    nb = S // Q_BLOCK
    key_chunk = jnp.arange(S, dtype=jnp.int32) // CHUNK
    pos_blocks = jnp.arange(S, dtype=jnp.int32).reshape(nb, Q_BLOCK)
    gather = jax.vmap(lambda a, i: a[i])

    def to_blocks(a):
        return jnp.moveaxis(a.reshape((B, nb, Q_BLOCK) + a.shape[2:]), 1, 0)

    def block(args):
        qb, qib, wb, tb = args
        s_head = jax.nn.relu(jnp.einsum('bqhd,bsd->bqhs', qib, k_idx) * IDX_DIM ** -0.5)
        score = jnp.einsum('bqh,bqhs->bqs', wb, s_head).astype(jnp.float32)
        q_chunk = tb // CHUNK
        admissible = key_chunk[None, :] <= q_chunk[:, None]
        score = jnp.where(admissible[None], score, -jnp.inf)
        _, sel = lax.top_k(score, topk)
        valid = (sel // CHUNK) <= q_chunk[None, :, None]
        k_sel = gather(k, sel)
        v_sel = gather(v, sel)
        bias = t5_table[t5_bucket(sel - tb[None, :, None])]
        logits = (jnp.einsum('bqhd,bqkd->bhqk', qb, k_sel).astype(jnp.float32) * A_HEAD_DIM ** -0.5
                  + jnp.moveaxis(bias, 3, 1).astype(jnp.float32))
        logits = jnp.where(valid[:, None], logits, -1e30)
        p = jax.nn.softmax(logits, axis=-1).astype(v.dtype)
        return jnp.einsum('bhqk,bqkd->bqhd', p, v_sel)

    out = lax.map(block, (to_blocks(q), to_blocks(q_idx), to_blocks(w_idx), pos_blocks))
    return jnp.moveaxis(out, 0, 1).reshape(B, S, A_HEADS * A_HEAD_DIM)


def gla_mixer(q, k, v, g):
    B, S, H, DK = q.shape
    DV = v.shape[-1]
    N = S // CHUNK

    def to_chunks(a):
        return a.reshape(B, N, CHUNK, H, a.shape[-1]).transpose(1, 0, 3, 2, 4).astype(jnp.float32)

    qc, kc, vc, gc = to_chunks(q) * DK ** -0.5, to_chunks(k), to_chunks(v), to_chunks(g)
    b = jnp.cumsum(gc, axis=3)
    b_last = b[:, :, :, -1:, :]
    qe = qc * jnp.exp(b)
    ke = kc * jnp.exp(-b)
    kd = kc * jnp.exp(b_last - b)
    tril = jnp.tril(jnp.ones((CHUNK, CHUNK), jnp.float32))
    a_intra = jnp.einsum('nbhcd,nbhsd->nbhcs', qe, ke) * tril
    o_intra = jnp.einsum('nbhcs,nbhsv->nbhcv', a_intra, vc)
    decay = jnp.exp(b_last[:, :, :, 0, :])

    def step(state, inp):
        qe_n, kd_n, v_n, dec_n = inp
        o = jnp.einsum('bhcd,bhdv->bhcv', qe_n, state)
        state = state * dec_n[..., None] + jnp.einsum('bhcd,bhcv->bhdv', kd_n, v_n)
        return state, o

    s0 = jnp.zeros((B, H, DK, DV), jnp.float32)
    _, o_inter = lax.scan(step, s0, (qe, kd, vc, decay))
    o = (o_intra + o_inter).transpose(1, 0, 3, 2, 4).reshape(B, S, H, DV)
    return o


def causal_dwconv(u, w, b):
    out = lax.conv_general_dilated(
        u, w[:, None, :].astype(u.dtype), window_strides=(1,), padding=[(CONV_W - 1, 0)],
        dimension_numbers=('NWC', 'WIO', 'NWC'), feature_group_count=u.shape[-1])
    return out + b


def setup_inputs(seed: int = 0) -> dict:
    key = jax.random.key(seed)
    ks = jax.random.split(key, 20)

    def nrm(k, shape, scale):
        return scale * jax.random.normal(k, shape, jnp.float32)

    offs = np.concatenate([[0], np.cumsum(IN_SPLITS)])
    col_scale = np.ones((D_IN,), np.float32)
    for slot in VALUE_SLOTS:
        col_scale[offs[slot]:offs[slot + 1]] = BETA
    col_scale = jnp.asarray(col_scale)

    return {
        "x": nrm(ks[0], (BATCH, SEQ, D_MODEL), 1.0),
        "c": nrm(ks[1], (BATCH, D_MODEL), 1.0),
        "t5_table": nrm(ks[2], (T5_BUCKETS, A_HEADS), 0.5),
        "w_ada": nrm(ks[3], (DEPTH, D_MODEL, 6 * D_MODEL), 0.5 * D_MODEL ** -0.5),
        "b_ada": nrm(ks[4], (DEPTH, 6 * D_MODEL), 0.02),
        "w_in": nrm(ks[5], (DEPTH, D_MODEL, D_IN), D_MODEL ** -0.5) * col_scale,
        "w_g2": nrm(ks[6], (DEPTH, GATE_RANK, B_HEADS * B_HEAD_K), GATE_RANK ** -0.5),
        "b_g2": nrm(ks[7], (DEPTH, B_HEADS * B_HEAD_K), 0.1),
        "gla_norm": 1.0 + nrm(ks[8], (DEPTH, B_HEAD_V), 0.02),
        "w_out": nrm(ks[9], (DEPTH, D_MIX, D_MODEL), BETA * D_MIX ** -0.5),
        "ln1_g": 1.0 + nrm(ks[10], (DEPTH, D_MODEL), 0.02),
        "ln1_b": nrm(ks[11], (DEPTH, D_MODEL), 0.02),
        "w_up": nrm(ks[12], (DEPTH, D_MODEL, D_FF), BETA * D_MODEL ** -0.5),
        "w_gate": nrm(ks[13], (DEPTH, D_MODEL, D_FF), BETA * D_MODEL ** -0.5),
        "conv_w": nrm(ks[14], (DEPTH, CONV_W, D_FF), CONV_W ** -0.5),
        "conv_b": nrm(ks[15], (DEPTH, D_FF), 0.02),
        "w_down": nrm(ks[16], (DEPTH, D_FF, D_MODEL), BETA * D_FF ** -0.5),
        "ln2_g": 1.0 + nrm(ks[17], (DEPTH, D_MODEL), 0.02),
        "ln2_b": nrm(ks[18], (DEPTH, D_MODEL), 0.02),
    }


def reference(x, c, t5_table, w_ada, b_ada, w_in, w_g2, b_g2, gla_norm, w_out,
              ln1_g, ln1_b, w_up, w_gate, conv_w, conv_b, w_down, ln2_g, ln2_b):
    B, S, _ = x.shape
    split_points = np.cumsum(IN_SPLITS)[:-1].tolist()
    c_act = jax.nn.silu(c)
    for l in range(DEPTH):
        mod = (jnp.einsum('bd,dm->bm', c_act, w_ada[l]) + b_ada[l])[:, None, :]
        shift1, scale1, gate1, shift2, scale2, gate2 = jnp.split(mod, 6, axis=-1)

        h = modulate(x, shift1, scale1)
        proj = jnp.einsum('bsd,de->bse', h, w_in[l])
        (a_q, a_k, a_v, i_q, i_k, i_w, g_q, g_k, g_v, g_lr, g_r) = jnp.split(proj, split_points, axis=-1)

        o_a = dsa_mixer(a_q.reshape(B, S, A_HEADS, A_HEAD_DIM), a_k, a_v,
                        i_q.reshape(B, S, IDX_HEADS, IDX_DIM), i_k,
                        i_w * IDX_HEADS ** -0.5, t5_table)

        log_gate = jax.nn.log_sigmoid(
            (jnp.einsum('bsr,rk->bsk', g_lr, w_g2[l]) + b_g2[l]).astype(jnp.float32)) / GATE_TAU
        o_b = gla_mixer(g_q.reshape(B, S, B_HEADS, B_HEAD_K), g_k.reshape(B, S, B_HEADS, B_HEAD_K),
                        g_v.reshape(B, S, B_HEADS, B_HEAD_V), log_gate.reshape(B, S, B_HEADS, B_HEAD_K))
        o_b = o_b * lax.rsqrt(jnp.mean(jnp.square(o_b), axis=-1, keepdims=True) + EPS) * gla_norm[l]
        o_b = (o_b.reshape(B, S, B_V) * jax.nn.silu(g_r.astype(jnp.float32))).astype(x.dtype)

        y = jnp.einsum('bse,ed->bsd', jnp.concatenate([o_a, o_b], axis=-1), w_out[l])
        x = layer_norm_affine(ALPHA * x + gate1 * y, ln1_g[l], ln1_b[l])

        h = modulate(x, shift2, scale2)
        u = causal_dwconv(jnp.einsum('bsd,df->bsf', h, w_up[l]), conv_w[l], conv_b[l])
        f = jax.nn.gelu(u) * jnp.einsum('bsd,df->bsf', h, w_gate[l])
        y = jnp.einsum('bsf,fd->bsd', f, w_down[l])
        x = layer_norm_affine(ALPHA * x + gate2 * y, ln2_g[l], ln2_b[l])
    return x
```

```python
import math
from contextlib import ExitStack

import numpy as np
import concourse.bass as bass
import concourse.mybir as mybir
from concourse.ap import AP as RawAP
from concourse.bass_utils import run_bass_kernel_spmd

F32, BF16 = mybir.dt.float32, mybir.dt.bfloat16
AF = mybir.ActivationFunctionType
ALU = mybir.AluOpType

D = 4096
KC = 32
CH = 64
A_HEADS, A_DH = 16, 128
I_HEADS, I_D = 32, 64
B_HEADS, B_DV, B_DK = 4, 512, 256
GR = 16
D_FF = 11008
FC = D_FF // 128
ALPHA = 2.0 ** 0.25
EPS = 1e-6
BIG = 1.0e30
NEG = -30000.0
O_AQ = 0
O_AK = 2048
O_AV = 2176
O_IQ = 2304
O_IK = 4352
O_IW = 4416
O_BQ = 4448
O_BK = 5472
O_BV = 6496
O_BG = 8544
O_BR = 8560
D_IN = 10608
TL = 384


class Tok:
    __slots__ = ("si", "val")

    def __init__(self, si, val):
        self.si, self.val = si, val


class Pending:
    __slots__ = ("eng", "tok")

    def __init__(self, eng):
        self.eng, self.tok = eng, None


class Buf:
    def __init__(self, name):
        self.name = name
        self.w = None
        self.r = {}
        self.dsi = None
        self.dcnt = 0


class Eng:
    def __init__(self, K, name, e):
        self.K, self.name, self.e = K, name, e
        self.si = K.newsem("e_" + name)
        self.cnt = 0
        self.waited = {}
        self.pending = []

    def wait(self, tok):
        if tok is None:
            return
        if isinstance(tok, Pending):
            if tok.tok is None:
                assert tok.eng is self, "unresolved pending token from other engine"
                return
            tok = tok.tok
        if self.waited.get(tok.si, 0) >= tok.val:
            return
        self.e.wait_ge(self.K.sems[tok.si], tok.val)
        self.waited[tok.si] = tok.val


class KB:
    def __init__(self, nc, es):
        self.nc, self.es = nc, es
        self.sems = []
        self.pe = Eng(self, "pe", nc.tensor)
        self.act = Eng(self, "act", nc.scalar)
        self.dve = Eng(self, "dve", nc.vector)
        self.pool = Eng(self, "pool", nc.gpsimd)
        self.sp = Eng(self, "sp", nc.sync)
        self.engs = [self.pe, self.act, self.dve, self.pool, self.sp]
        self.dbufs = []
        self.rr = 0

    def newsem(self, name):
        s = self.es.enter_context(self.nc.semaphore("%s_%d" % (name, len(self.sems))))
        self.sems.append(s)
        return len(self.sems) - 1

    def sb(self, name, shape, dt, es=None):
        t = (es or self.es).enter_context(self.nc.sbuf_tensor(name, list(shape), dt))
        return t, Buf(name)

    def ps(self, name, shape, dt, es=None):
        t = (es or self.es).enter_context(self.nc.psum_tensor(name, list(shape), dt))
        return t, Buf(name)

    def op(self, eng, fn, r=(), w=(), mark=True):
        for b in r:
            eng.wait(b.w)
        for b in w:
            eng.wait(b.w)
            for t in list(b.r.values()):
                eng.wait(t)
        ins = fn(eng.e)
        if mark:
            ins.then_inc(self.sems[eng.si], 1)
            eng.cnt += 1
            tok = Tok(eng.si, eng.cnt)
            for p in eng.pending:
                p.tok = tok
            eng.pending = []
        else:
            tok = Pending(eng)
            eng.pending.append(tok)
        for b in r:
            b.r[eng.si] = tok
        for b in w:
            b.w = tok
            b.r = {}
        return tok

    def dma(self, eng, out, in_, owner, r=(), w=()):
        if owner.dsi is None:
            owner.dsi = self.newsem("d_" + owner.name)
            self.dbufs.append(owner)
        for b in r:
            eng.wait(b.w)
        for b in w:
            if not (isinstance(b.w, Tok) and b.w.si == owner.dsi):
                eng.wait(b.w)
            for t in list(b.r.values()):
                eng.wait(t)
        eng.e.dma_start(out=out, in_=in_).then_inc(self.sems[owner.dsi], 16)
        owner.dcnt += 16
        tok = Tok(owner.dsi, owner.dcnt)
        for b in r:
            b.r[owner.dsi] = tok
        for b in w:
            b.w = tok
            b.r = {}
        return tok

    def barrier(self):
        for e in self.engs:
            assert not e.pending, e.name
        for e in self.engs:
            for e2 in self.engs:
                if e2 is not e and e2.cnt > 0:
                    e.wait(Tok(e2.si, e2.cnt))
            for b in self.dbufs:
                if b.dcnt > 0:
                    e.wait(Tok(b.dsi, b.dcnt))

    def ew(self):
        self.rr += 1
        return self.dve if self.rr % 2 else self.pool


def t5_bucket_np(rel):
    half, max_exact = 16, 8
    ret = np.where(rel > 0, half, 0)
    n = np.abs(rel)
    nf = np.maximum(n, 1).astype(np.float32)
    large = max_exact + (np.log(nf / np.float32(max_exact)) / np.float32(math.log(128 / 8)) * np.float32(8)).astype(np.int32)
    large = np.minimum(large, half - 1)
    return ret + np.where(n < max_exact, n, large)


def host_consts():
    c = {}
    i = np.arange(128)
    same = (i[:, None] // CH) == (i[None, :] // CH)
    c["ident"] = np.eye(128, dtype=np.float32)
    c["utri"] = np.where(same & (i[:, None] <= i[None, :]), -1.0 / 16, 0.0).astype(np.float32)
    c["ustrict"] = np.where(same & (i[:, None] > i[None, :]), -1.0 / 16, 0.0).astype(np.float32)
    c["trimask"] = np.where(same & (i[:, None] <= i[None, :]), 1.0, 0.0).astype(np.float32)
    m = np.arange(TL)
    rel = 127 - m
    bk = t5_bucket_np(rel)
    oh = np.zeros((32, TL), np.float32)
    oh[bk, m] = 1.0
    c["onehot"] = oh
    return np.concatenate([c["ident"], c["utri"], c["ustrict"], c["trimask"]], axis=1), oh


def build(S, dbg=False):
    NT = S // 512
    NQ = S // 128
    TOPK = min(256, S // 4)
    nc = bass.Bass("TRN2", target_bir_lowering=False)

    def din(name, shape, dt=F32):
        return nc.dram_tensor(name, list(shape), dt, kind="ExternalInput")

    x_d = din("x", [S, D])
    cT_d = din("cT", [128, KC])
    t5_d = din("t5", [32, A_HEADS])
    wada_d = din("w_ada", [D, 6 * D])
    bada_d = din("b_ada", [1, 6 * D])
    win_d = din("w_in", [D, D_IN])
    wg2_d = din("w_g2", [GR, 1024])
    bg2_d = din("b_g2", [1, 1024])
    gn_d = din("gla_norm", [1, B_DV])
    wout_d = din("w_out", [D, D])
    ln1g_d = din("ln1_g", [1, D])
    ln1b_d = din("ln1_b", [1, D])
    wup_d = din("w_up", [D, D_FF])
    wgate_d = din("w_gate", [D, D_FF])
    convw_d = din("conv_wT", [128, FC, 3])
    convb_d = din("conv_bT", [128, FC])
    wdown_d = din("w_down", [D_FF, D])
    ln2g_d = din("ln2_g", [1, D])
    ln2b_d = din("ln2_b", [1, D])
    cm_d = din("cmask", [128, 512])
    oh_d = din("onehot", [32, TL])
    out_d = nc.dram_tensor("out", [S, D], F32, kind="ExternalOutput")

    def dscr(name, shape, dt=F32):
        return nc.dram_tensor(name, list(shape), dt, kind=("ExternalOutput" if dbg else "Internal"))

    g1b_s = dscr("g1b_s", [128, D])
    g2b_s = dscr("g2b_s", [128, D])
    gsc_s = dscr("gsc_s", [A_HEADS, TL])
    grep_s = dscr("grep_s", [128, A_HEADS * TL])
    o_s = dscr("o_s", [S, D], BF16)
    hT_s = nc.dram_tensor("hT_s", [NT, 128, KC * 512], BF16, kind="Internal")
    q_s = nc.dram_tensor("q_s", [NT, 128, 16 * 512], BF16, kind="Internal")
    qi_s = nc.dram_tensor("qi_s", [NT, 128, 16 * 512], BF16, kind="Internal")
    xp_s = dscr("xp_s", [S, D])
    x1_s = dscr("x1_s", [S, D])
    dbg_d = {}

    x_a, out_a = x_d.ap(), out_d.ap()
    win_a = win_d.ap().rearrange("(kc p) n -> p kc n", p=128)
    wout_a = wout_d.ap().rearrange("(kc p) n -> p kc n", p=128)
    wup_a = wup_d.ap().rearrange("(kc p) n -> p kc n", p=128)
    wgate_a = wgate_d.ap().rearrange("(kc p) n -> p kc n", p=128)
    wdown_a = wdown_d.ap().rearrange("(fc p) n -> p fc n", p=128)
    wada_a = wada_d.ap().rearrange("(kc p) n -> p kc n", p=128)

    with ExitStack() as es:
        K = KB(nc, es)
        es.enter_context(nc.Block())
        pe, act, dve, pool, sp = K.pe, K.act, K.dve, K.pool, K.sp

        cmf, cmf_b = K.sb("cmf", [128, 512], F32)
        identf = cmf[:, 0:128]
        cmb, cmb_b = K.sb("cmb", [128, 512], BF16)
        identb, utri, ustrict = cmb[:, 0:128], cmb[:, 128:256], cmb[:, 256:384]
        trimf = cmf[:, 384:512]
        modT, modT_b = K.sb("modT", [128, 4, KC], F32)
        onesf, onesf_b = K.sb("onesf", [128, 128], F32)
        onesb, onesb_b = K.sb("onesb", [1, 128], BF16)
        psb = [K.ps("ps%d" % i, [128, 512], F32) for i in range(8)]
        wslots = [K.sb("wbuf%d" % i, [128, KC, 256], BF16) for i in range(3)]
        wctr = [0]

        K.dma(sp, cmf[:], cm_d.ap(), cmf_b, w=[cmf_b])
        K.op(dve, lambda e: e.tensor_copy(out=cmb[:], in_=cmf[:]), r=[cmf_b], w=[cmb_b])
        K.op(dve, lambda e: e.memset(onesf[:], 1.0), w=[onesf_b])
        K.op(dve, lambda e: e.memset(onesb[:], 1.0), w=[onesb_b])
        epsc, epsc_b = K.sb("epsc", [128, 2], F32)
        K.op(dve, lambda e: e.memset(epsc[:, 0:1], EPS), w=[epsc_b])
        K.op(dve, lambda e: e.memset(epsc[:, 1:2], float(B_DV) * EPS), w=[epsc_b])

        def pst(i):
            return psb[i][0], psb[i][1]

        def nextw():
            s = wslots[wctr[0] % 3]
            wctr[0] += 1
            return s

        def ln_stats(xt, xt_b, mv, mv_b, st, st_b, rstd, rstd_b):
            for c8 in range(8):
                K.op(dve, lambda e, c8=c8: e.bn_stats(out=st[:, c8, :], in_=xt[:, c8 * 512:(c8 + 1) * 512]), r=[xt_b], w=[st_b])
            K.op(dve, lambda e: e.bn_aggr(out=mv[:], in_=st[:].rearrange("p a b -> p (a b)")), r=[st_b], w=[mv_b])
            K.op(act, lambda e: e.activation(out=rstd[:], in_=mv[:, 1:2], func=AF.Sqrt, bias=epsc[:, 0:1]), r=[mv_b, epsc_b], w=[rstd_b])
            K.op(dve, lambda e: e.reciprocal(out=rstd[:], in_=rstd[:]), r=[rstd_b], w=[rstd_b])

        def make_hT(es2, src_a, t0, which, hT, hT_b, tiles):
            (xts, xh, xh_b, st, st_b, mv, mv_b, rstd, rstd_b) = tiles
            for ts in range(4):
                xt, xt_b = xts[ts % len(xts)]
                K.dma(act, xt[:], src_a[t0 + ts * 128:t0 + (ts + 1) * 128, :], xt_b, w=[xt_b])
                ln_stats(xt, xt_b, mv, mv_b, st, st_b, rstd, rstd_b)
                K.op(dve, lambda e: e.tensor_scalar(out=xh[:], in0=xt[:], scalar1=mv[:, 0:1], scalar2=rstd[:, 0:1], op0=ALU.subtract, op1=ALU.mult),
                     r=[xt_b, mv_b, rstd_b], w=[xh_b])
                for g in range(8):
                    pt, pt_b = pst(4 + g % 2)
                    ptb = pt[:].bitcast(BF16)
                    for j in range(4):
                        kc = g * 4 + j
                        K.op(pe, lambda e, kc=kc, j=j: e.transpose(out=ptb[:, j * 128:(j + 1) * 128], in_=xh[:, kc * 128:(kc + 1) * 128], identity=identb),
                             r=[xh_b, cmb_b], w=[pt_b], mark=(j == 3))
                    for j in range(4):
                        kc = g * 4 + j
                        if j % 2 == 0:
                            K.op(act, lambda e, kc=kc, j=j: e.activation(out=hT[:, kc, ts * 128:(ts + 1) * 128], in_=ptb[:, j * 128:(j + 1) * 128], func=AF.Identity,
                                                                          scale=modT[:, which + 1, kc:kc + 1], bias=modT[:, which, kc:kc + 1]),
                                 r=[pt_b, modT_b], w=[hT_b])
                        else:
                            K.op(dve, lambda e, kc=kc, j=j: e.tensor_scalar(out=hT[:, kc, ts * 128:(ts + 1) * 128], in0=ptb[:, j * 128:(j + 1) * 128],
                                                                             scalar1=modT[:, which + 1, kc:kc + 1], scalar2=modT[:, which, kc:kc + 1], op0=ALU.mult, op1=ALU.add),
                                 r=[pt_b, modT_b], w=[hT_b])

        hctr = [0]

        def alloc_ht_tiles(es2, xt_alias=None, xh_alias=None):
            if xt_alias is None:
                xts = [K.sb("xt0", [128, D], F32, es2)]
            else:
                xts = [(xt_alias, Buf("xt_alias"))]
            if xh_alias is None:
                xh, xh_b = K.sb("xh", [128, D], BF16, es2)
            else:
                xh, xh_b = xh_alias, Buf("xh_alias")
            hctr[0] += 1
            st, st_b = K.sb("st%d" % hctr[0], [128, 8, 6], F32, es2)
            mv, mv_b = K.sb("mv%d" % hctr[0], [128, 2], F32, es2)
            rstd, rstd_b = K.sb("rstd%d" % hctr[0], [128, 1], F32, es2)
            return (xts, xh, xh_b, st, st_b, mv, mv_b, rstd, rstd_b)

        def load_w(w_a, c0, w, k0=0, nk=KC, dup=False):
            wt, wt_b = nextw()
            K.dma(pool, wt[:, 0:nk, 0:w], w_a[:, k0:k0 + nk, c0:c0 + w], wt_b, w=[wt_b])
            if dup:
                K.dma(pool, wt[:, 0:nk, w:2 * w], w_a[:, k0:k0 + nk, c0:c0 + w], wt_b, w=[wt_b])
            return wt, wt_b

        pctr = [0]

        def proj_S(hT, hT_b, w_a, c0, ncols, evac, T=512, dup=False):
            for cs in range(c0, c0 + ncols, 256):
                w = min(256, c0 + ncols - cs)
                wt, wt_b = load_w(w_a, cs, w, dup=dup)
                weff = 2 * w if dup else w
                for j in range(0, weff, 128):
                    m = min(128, weff - j)
                    pt, pt_b = pst(pctr[0] % 4)
                    pctr[0] += 1
                    for kc in range(KC):
                        K.op(pe, lambda e, kc=kc: e.matmul(pt[0:m, 0:T], lhsT=wt[:, kc, j:j + m], rhs=hT[:, kc, 0:T], start=(kc == 0), stop=(kc == KC - 1)),
                             r=[wt_b, hT_b], w=[pt_b], mark=(kc == KC - 1))
                    evac(pt[0:m, 0:T], pt_b, cs - c0 + j, m)

        pctr2 = [0]

        def proj_M(hT, hT_b, w_a, c0, ncols, evac, T=512):
            if ncols % 512 == 0:
                for cs in range(c0, c0 + ncols, 512):
                    base = (pctr2[0] % 2) * 4
                    pctr2[0] += 1
                    for half in range(2):
                        wt, wt_b = nextw()
                        wtv = wt[:].rearrange("p a b -> p (a b)").rearrange("p (a b) -> p a b", b=512)
                        K.dma(pool, wtv[:, :, :], w_a[:, half * 16:(half + 1) * 16, cs:cs + 512], wt_b, w=[wt_b])
                        for ts in range(T // 128):
                            pt, pt_b = pst(base + ts)
                            for kl in range(16):
                                kc = half * 16 + kl
                                K.op(pe, lambda e, kc=kc, kl=kl: e.matmul(pt[:, :], lhsT=hT[:, kc, ts * 128:(ts + 1) * 128], rhs=wtv[:, kl, :], start=(kc == 0), stop=(kc == KC - 1)),
                                     r=[wt_b, hT_b], w=[pt_b], mark=(kl == 15))
                    for ts in range(T // 128):
                        pt, pt_b = pst(base + ts)
                        evac(pt[:, :], pt_b, ts, cs - c0, 512)
                return
            for cs in range(c0, c0 + ncols, 256):
                w = min(256, c0 + ncols - cs)
                wt, wt_b = load_w(w_a, cs, w)
                for ts in range(T // 128):
                    pt, pt_b = pst(pctr[0] % 4)
                    pctr[0] += 1
                    for kc in range(KC):
                        K.op(pe, lambda e, kc=kc: e.matmul(pt[:, 0:w], lhsT=hT[:, kc, ts * 128:(ts + 1) * 128], rhs=wt[:, kc, 0:w], start=(kc == 0), stop=(kc == KC - 1)),
                             r=[wt_b, hT_b], w=[pt_b], mark=(kc == KC - 1))
                    evac(pt[:, 0:w], pt_b, ts, cs - c0, w)

        ectr = [0]

        def evac_copy(dst_fn, dst_b, scale=None):
            def f(ps_ap, ps_b, *a):
                d = dst_fn(*a)
                ectr[0] += 1
                if ectr[0] % 2:
                    if scale is None:
                        K.op(act, lambda e: e.copy(out=d, in_=ps_ap), r=[ps_b], w=[dst_b])
                    else:
                        K.op(act, lambda e: e.activation(out=d, in_=ps_ap, func=AF.Copy, scale=scale), r=[ps_b], w=[dst_b])
                else:
                    if scale is None:
                        K.op(dve, lambda e: e.tensor_copy(out=d, in_=ps_ap), r=[ps_b], w=[dst_b])
                    else:
                        K.op(dve, lambda e: e.tensor_scalar(out=d, in0=ps_ap, scalar1=scale, scalar2=None, op0=ALU.mult), r=[ps_b], w=[dst_b])
            return f

        actr = [0]

        def adaln_gen(es2, cb_list, ldq=None):
            ldq = ldq or pool
            actr[0] += 1
            tg = "A%d" % actr[0]
            cTt, cT_b = K.sb("cTt" + tg, [128, KC], F32, es2)
            crep, crep_b = K.sb("crep" + tg, [128, KC, 128], BF16, es2)
            was = [K.sb("wa%s%d" % (tg, i), [128, 16, 512], BF16, es2) for i in range(2)]
            bat, bat_b = K.sb("bat" + tg, [1, 512], BF16, es2)
            stg = [K.sb("stg%s%d" % (tg, i), [128, 512], F32, es2) for i in range(2)]
            junk, junk_b = K.sb("junkA" + tg, [128, 128], F32, es2)
            K.dma(sp, cTt[:], cT_d.ap(), cT_b, w=[cT_b])
            K.op(act, lambda e: e.activation(out=cTt[:], in_=cTt[:], func=AF.Silu), r=[cT_b], w=[cT_b])
            for kc in range(KC):
                K.op(dve, lambda e, kc=kc: e.tensor_scalar(out=crep[:, kc, :], in0=onesf[:], scalar1=cTt[:, kc:kc + 1], scalar2=None, op0=ALU.mult),
                     r=[onesf_b, cT_b], w=[crep_b])
            wi = 0
            for cb in cb_list:
                which, cbl = cb // 8, cb % 8
                pt, pt_b = pst(cb % 4)
                K.dma(pool, bat[:], bada_d.ap()[0:1, cb * 512:(cb + 1) * 512], bat_b, w=[bat_b])
                K.op(pe, lambda e: e.matmul(pt[:, :], lhsT=onesb[0:1, :], rhs=bat[0:1, :], start=True, stop=False), r=[onesb_b, bat_b], w=[pt_b], mark=False)
                for kg in range(2):
                    wa, wa_b = was[wi % 2]
                    wi += 1
                    K.dma(pool, wa[:], wada_a[:, kg * 16:(kg + 1) * 16, cb * 512:(cb + 1) * 512], wa_b, w=[wa_b])
                    for kl in range(16):
                        kc = kg * 16 + kl
                        K.op(pe, lambda e, kc=kc, kl=kl: e.matmul(pt[:, :], lhsT=crep[:, kc, :], rhs=wa[:, kl, :], start=False, stop=(kc == KC - 1)),
                             r=[crep_b, wa_b], w=[pt_b], mark=(kl == 15))
                if which in (2, 5):
                    sg, sg_b = stg[cb % 2]
                    K.op(act, lambda e: e.copy(out=sg[:], in_=pt[:, :]), r=[pt_b], w=[sg_b])
                    dst = (g1b_s if which == 2 else g2b_s).ap()[:, cbl * 512:(cbl + 1) * 512]
                    K.dma(sp, dst, sg[:], sg_b, r=[sg_b])
                else:
                    slot = {0: 0, 1: 1, 3: 2, 4: 3}[which]
                    for c in range(4):
                        col = cbl * 4 + c
                        K.op(dve, lambda e, c=c, col=col: e.scalar_tensor_tensor(out=junk[:], in0=pt[:, c * 128:(c + 1) * 128], scalar=1.0, in1=identf, op0=ALU.mult, op1=ALU.mult,
                                                                                   accum_out=modT[:, slot, col:col + 1]),
                             r=[pt_b, cmf_b], w=[junk_b, modT_b])
                    if cbl == 7 and which in (1, 4):
                        K.op(dve, lambda e, slot=slot: e.tensor_scalar(out=modT[:, slot, :], in0=modT[:, slot, :], scalar1=1.0, scalar2=None, op0=ALU.add), r=[modT_b], w=[modT_b])
                yield

        K.op(dve, lambda e: e.memset(modT[:], 0.0), w=[modT_b])
        with ExitStack() as es2:
            for _ in adaln_gen(es2, list(range(16))):
                pass
            K.barrier()

        with ExitStack() as esB:
            SW = max(S, 2048)
            kT, kT_b = K.sb("kT", [128, S], BF16, esB)
            kiT, kiT_b = K.sb("kiT", [128, S], BF16, esB)
            vaug, vaug_b = K.sb("vaug", [128, NQ, 132], BF16, esB)
            wabs, wabs_b = K.sb("wabs", [128, NQ, 32], F32, esB)
            wsgn, wsgn_b = K.sb("wsgn", [128, NQ, 32], F32, esB)
            K.op(dve, lambda e: e.memset(vaug[:], 1.0), w=[vaug_b])

            with ExitStack() as es2:
                hTs = [K.sb("hT_B%d" % i, [128, KC, 512], BF16, es2) for i in range(2)]
                qT, qT_b = K.sb("qT", [128, 4, A_HEADS, 128], BF16, es2)
                qiT, qiT_b = K.sb("qiT", [128, 4, 16, 128], BF16, es2)
                wraw, wraw_b = K.sb("wraw", [128, 4, 32], F32, es2)
                xtB = [K.sb("xtB%d" % i, [128, D], F32, es2) for i in range(1)]
                xh4 = [K.sb("xh4_%d" % i, [128, D], BF16, es2) for i in range(2)]
                stB, stB_b = K.sb("stB", [128, 8, 6], F32, es2)
                mvB, mvB_b = K.sb("mvB", [128, 2], F32, es2)
                rsB, rsB_b = K.sb("rsB", [128, 1], F32, es2)

                def ln_part(t0, tss=(0, 1, 2, 3)):
                    for ts in tss:
                        xt, xt_b = xtB[ts % len(xtB)]
                        xh, xh_b = xh4[ts % 2]
                        K.dma(act, xt[:], x_a[t0 + ts * 128:t0 + (ts + 1) * 128, :], xt_b, w=[xt_b])
                        ln_stats(xt, xt_b, mvB, mvB_b, stB, stB_b, rsB, rsB_b)
                        K.op(dve, lambda e: e.tensor_scalar(out=xh[:], in0=xt[:], scalar1=mvB[:, 0:1], scalar2=rsB[:, 0:1], op0=ALU.subtract, op1=ALU.mult),
                             r=[xt_b, mvB_b, rsB_b], w=[xh_b])

                def tr_part(hT, hT_b, tss=(0, 1, 2, 3)):
                    for ts in tss:
                        xh, xh_b = xh4[ts % 2]
                        for g in range(8):
                            pt, pt_b = pst(4 + g % 2)
                            ptb = pt[:].bitcast(BF16)
                            for j in range(4):
                                kc = g * 4 + j
                                K.op(pe, lambda e, kc=kc, j=j: e.transpose(out=ptb[:, j * 128:(j + 1) * 128], in_=xh[:, kc * 128:(kc + 1) * 128], identity=identb),
                                     r=[xh_b, cmb_b], w=[pt_b], mark=(j == 3))
                            for j in range(4):
                                kc = g * 4 + j
                                if j % 2 == 0:
                                    K.op(act, lambda e, kc=kc, j=j: e.activation(out=hT[:, kc, ts * 128:(ts + 1) * 128], in_=ptb[:, j * 128:(j + 1) * 128], func=AF.Identity,
                                                                                  scale=modT[:, 1, kc:kc + 1], bias=modT[:, 0, kc:kc + 1]), r=[pt_b, modT_b], w=[hT_b])
                                else:
                                    K.op(dve, lambda e, kc=kc, j=j: e.tensor_scalar(out=hT[:, kc, ts * 128:(ts + 1) * 128], in0=ptb[:, j * 128:(j + 1) * 128],
                                                                                     scalar1=modT[:, 1, kc:kc + 1], scalar2=modT[:, 0, kc:kc + 1], op0=ALU.mult, op1=ALU.add),
                                         r=[pt_b, modT_b], w=[hT_b])

                def evq(dst, dst_b, scale):
                    def f(ps_ap, ps_b, col, m):
                        d = dst[:, :, col // 128, :]
                        src = ps_ap.rearrange("p (a b) -> p a b", b=128)
                        ectr[0] += 1
                        if ectr[0] % 2:
                            K.op(act, lambda e: e.activation(out=d, in_=src, func=AF.Copy, scale=scale), r=[ps_b], w=[dst_b])
                        else:
                            K.op(dve, lambda e: e.tensor_scalar(out=d, in0=src, scalar1=scale, scalar2=None, op0=ALU.mult), r=[ps_b], w=[dst_b])
                    return f

                for ts_ in range(4):
                    ln_part(0, (ts_,))
                    tr_part(*hTs[0], tss=(ts_,))
                for tt in range(NT):
                    t0 = tt * 512
                    hT, hT_b = hTs[tt % 2]
                    K.dma(sp, hT_s.ap()[tt], hT[:].rearrange("p a b -> p (a b)"), hT_b, r=[hT_b])
                    proj_S(hT, hT_b, win_a, O_AK, 128, evac_copy(lambda col, m: kT[:, t0:t0 + 512], kT_b))
                    proj_S(hT, hT_b, win_a, O_IK, 64, evac_copy(lambda col, m: kiT[:, t0:t0 + 512], kiT_b), dup=True)
                    proj_M(hT, hT_b, win_a, O_AV, 128, evac_copy(lambda ts, col, w: vaug[:, tt * 4 + ts, 0:128], vaug_b))
                    proj_M(hT, hT_b, win_a, O_IW, 32, evac_copy(lambda ts, col, w: wraw[:, ts, :], wraw_b))
                    K.op(act, lambda e: e.activation(out=wabs[:, tt * 4:tt * 4 + 4, :], in_=wraw[:], func=AF.Abs, scale=I_HEADS ** -0.5), r=[wraw_b], w=[wabs_b])
                    K.op(act, lambda e: e.activation(out=wsgn[:, tt * 4:tt * 4 + 4, :], in_=wraw[:], func=AF.Sign), r=[wraw_b], w=[wsgn_b])
                    proj_S(hT, hT_b, win_a, O_IQ, 2048, evq(qiT, qiT_b, I_D ** -0.5))
                    K.dma(sp, qi_s.ap()[tt], qiT[:].rearrange("p a b c -> p (a b c)"), qiT_b, r=[qiT_b])
                    if tt + 1 < NT:
                        ln_part(t0 + 512, (0, 1))
                    proj_S(hT, hT_b, win_a, O_AQ, 2048, evq(qT, qT_b, A_DH ** -0.5))
                    K.dma(sp, q_s.ap()[tt], qT[:].rearrange("p a b c -> p (a b c)"), qT_b, r=[qT_b])
                    if tt + 1 < NT:
                        nh = hTs[(tt + 1) % 2]
                        tr_part(*nh, tss=(0,))
                        ln_part(t0 + 512, (2,))
                        tr_part(*nh, tss=(1,))
                        ln_part(t0 + 512, (3,))
                        tr_part(*nh, tss=(2, 3))
                K.barrier()

            biasM, biasM_b = K.sb("biasM", [128, A_HEADS, 2, 128], BF16, esB)
            b15, b15_b = K.sb("b15", [128, A_HEADS], F32, esB)
            es3 = ExitStack()
            biasF, biasF_b = K.sb("biasF", [128, A_HEADS, 2, 128], F32, es3)
            t5t, t5_b = K.sb("t5t", [32, A_HEADS], F32, es3)
            oht, oh_b = K.sb("oht", [32, TL], F32, es3)
            gsb, gsb_b = K.sb("gsb", [A_HEADS, TL], F32, es3)
            gsc_b, grep_b = Buf("gsc"), Buf("grep")

            K.dma(sp, t5t[:], t5_d.ap(), t5_b, w=[t5_b])
            K.dma(sp, oht[:], oh_d.ap(), oh_b, w=[oh_b])
            K.dma(sp, b15[:], t5_d.ap()[15:16, :].partition_broadcast(128).rearrange("p a h -> p (a h)"), b15_b, w=[b15_b])
            pt, pt_b = pst(7)
            K.op(pe, lambda e: e.matmul(pt[0:A_HEADS, 0:TL], lhsT=t5t[:, :], rhs=oht[:, :], start=True, stop=True), r=[t5_b, oh_b], w=[pt_b])
            K.op(act, lambda e: e.copy(out=gsb[:], in_=pt[0:A_HEADS, 0:TL]), r=[pt_b], w=[gsb_b])
            K.dma(sp, gsc_s.ap(), gsb[:], gsb_b, r=[gsb_b], w=[gsc_b])
            K.dma(sp, grep_s.ap(), gsc_s.ap().rearrange("h l -> (h l)").partition_broadcast(128), gsb_b, r=[gsc_b], w=[grep_b])
            for hh in range(A_HEADS):
                for a_ in range(2):
                    src = RawAP(grep_s, hh * TL + 127 + 128 * (1 - a_), [[A_HEADS * TL - 1, 128], [1, 128]])
                    K.dma(sp, biasF[:, hh, a_, :], src, biasF_b, r=[grep_b], w=[biasF_b])
            K.op(dve, lambda e: e.tensor_copy(out=biasM[:], in_=biasF[:]), r=[biasF_b], w=[biasM_b])
            K.barrier()
            es3.close()

            es2 = esB
            acc, acc_b = K.sb("acc", [128, SW], F32, es2)
            work, work_b = K.sb("work", [128, SW], F32, es2)
            rts = [K.sb("rt%d" % i, [128, 512], F32, es2) for i in range(4)]
            m8, m8_b = K.sb("m8", [128, 8], F32, es2)
            thr, thr_b = K.sb("thr", [128, 1], F32, es2)
            madd, madd_b = K.sb("madd", [128, S], BF16, es2)
            maskTs = [K.sb("maskT%d" % i, [128, NQ, 128], BF16, es2) for i in range(2)]
            pts = [K.sb("pt%d" % i, [128, 512], BF16, es2) for i in range(3)]
            oa, oa_b = K.sb("oa", [128, 2048], BF16, es2)
            oraw, oraw_b = K.sb("oraw", [128, A_HEADS, 132], F32, es2)
            rinv16, rinv16_b = K.sb("rinv16", [128, A_HEADS], F32, es2)
            qTqs = [K.sb("qTq%d" % i, [128, A_HEADS, 128], BF16, es2) for i in range(2)]
            qiTqs = [K.sb("qiTq%d" % i, [128, 16, 128], BF16, es2) for i in range(2)]

            def load_q(qt):
                tt, ts = qt // 4, qt % 4
                qq, qq_b = qTqs[qt % 2]
                qi, qi_b = qiTqs[qt % 2]
                K.dma(act, qi[:].rearrange("p a b -> p (a b)"), qi_s.ap()[tt][:, ts * 2048:(ts + 1) * 2048], qi_b, w=[qi_b])
                K.dma(act, qq[:].rearrange("p a b -> p (a b)"), q_s.ap()[tt][:, ts * 2048:(ts + 1) * 2048], qq_b, w=[qq_b])

            def S1(qt):
                qiT, qiT_b = qiTqs[qt % 2]
                ts = qt
                sadm = 128 * (qt + 1)
                nkb = (sadm + 511) // 512
                tq = slice(0, 128)
                ri = 0
                for h in range(I_HEADS):
                    rows = slice((h % 2) * 64, (h % 2) * 64 + 64)
                    for kb in range(nkb):
                        kw = min(512, sadm - kb * 512)
                        ks_ = slice(kb * 512, kb * 512 + kw)
                        pt, pt_b = pst(ri % 4)
                        rt, rt_b = rts[ri % 4]
                        ri += 1
                        K.op(pe, lambda e: e.matmul(pt[:, 0:kw], lhsT=qiT[rows, h // 2, tq], rhs=kiT[rows, ks_], start=True, stop=True), r=[qiT_b, kiT_b], w=[pt_b])
                        K.op(act, lambda e: e.activation(out=rt[:, 0:kw], in_=pt[:, 0:kw], func=AF.Relu, scale=wabs[:, ts, h:h + 1]), r=[pt_b, wabs_b], w=[rt_b])
                        if h == 0:
                            K.op(dve, lambda e: e.tensor_scalar(out=acc[:, ks_], in0=rt[:, 0:kw], scalar1=wsgn[:, ts, h:h + 1], scalar2=None, op0=ALU.mult), r=[rt_b, wsgn_b], w=[acc_b])
                        else:
                            K.op(dve, lambda e: e.scalar_tensor_tensor(out=acc[:, ks_], in0=rt[:, 0:kw], scalar=wsgn[:, ts, h:h + 1], in1=acc[:, ks_], op0=ALU.mult, op1=ALU.add),
                                 r=[rt_b, wsgn_b, acc_b], w=[acc_b])
                K.op(dve, lambda e: e.memset(acc[0:64, sadm - 64:sadm], -BIG), w=[acc_b])
                if sadm > TOPK:
                    K.op(act, lambda e: e.copy(out=work[:, 0:sadm], in_=acc[:, 0:sadm]), r=[acc_b], w=[work_b])
                    nr = TOPK // 8
                    for r_ in range(nr):
                        K.op(dve, lambda e: e.max(out=m8[:], in_=work[:, 0:sadm]), r=[work_b], w=[m8_b])
                        if r_ < nr - 1:
                            K.op(dve, lambda e: e.match_replace(out=work[:, 0:sadm], in_to_replace=m8[:], in_values=work[:, 0:sadm], imm_value=-BIG), r=[m8_b, work_b], w=[work_b])
                    K.op(dve, lambda e: e.tensor_copy(out=thr[:], in_=m8[:, 7:8]), r=[m8_b], w=[thr_b])
                else:
                    K.op(dve, lambda e: e.memset(thr[:], -0.5 * BIG), w=[thr_b])
                K.op(dve, lambda e: e.tensor_scalar(out=madd[:, 0:sadm], in0=acc[:, 0:sadm], scalar1=thr[:, 0:1], scalar2=NEG, op0=ALU.is_lt, op1=ALU.mult), r=[acc_b, thr_b], w=[madd_b])

            def S2(qt):
                mT, mT_b = maskTs[qt % 2]
                for g in range((qt + 4) // 4):
                    pt, pt_b = pst(4 + g % 2)
                    ptb = pt[:].bitcast(BF16)
                    n = min(4, qt + 1 - g * 4)
                    for j in range(n):
                        ks = g * 4 + j
                        K.op(pe, lambda e, ks=ks, j=j: e.transpose(out=ptb[:, j * 128:(j + 1) * 128], in_=madd[:, ks * 128:(ks + 1) * 128], identity=identb),
                             r=[madd_b, cmb_b], w=[pt_b], mark=(j == n - 1))
                    K.op(act, lambda e, g=g, n=n: e.copy(out=mT[:, g * 4:g * 4 + n, :], in_=ptb[:, 0:n * 128].rearrange("p (a b) -> p a b", b=128)), r=[pt_b], w=[mT_b])

            def S3(qt):
                qT, qT_b = qTqs[qt % 2]
                tq = slice(0, 128)
                mT, mT_b = maskTs[qt % 2]
                nfar = max(0, qt - 1)
                groups = [(list(range(g, min(g + 4, nfar))), False) for g in range(0, nfar, 4)]
                groups.append(([ks for ks in (qt - 1, qt) if ks >= 0], True))
                pi = 0
                for h in range(A_HEADS):
                    po, po_b = pst(6 + h % 2)
                    for gi, (grp, near) in enumerate(groups):
                        n = len(grp)
                        ks0 = grp[0]
                        ps_, ps_b = pst(4 + pi % 2)
                        pT_, pT_b = pts[pi % 3]
                        pi += 1
                        K.op(pe, lambda e: e.matmul(ps_[:, 0:n * 128], lhsT=identb, rhs=mT[:, ks0:ks0 + n, :].rearrange("p a b -> p (a b)"), start=True, stop=False),
                             r=[cmb_b, mT_b], w=[ps_b], mark=False)
                        if near:
                            brhs = biasM[:, h, 2 - n:2, :].rearrange("p a b -> p (a b)")
                            K.op(pe, lambda e: e.matmul(ps_[:, 0:n * 128], lhsT=identb, rhs=brhs, start=False, stop=False), r=[cmb_b, biasM_b], w=[ps_b], mark=False)
                        for j, ks in enumerate(grp):
                            K.op(pe, lambda e, j=j, ks=ks: e.matmul(ps_[:, j * 128:(j + 1) * 128], lhsT=kT[:, ks * 128:(ks + 1) * 128], rhs=qT[:, h, tq], start=False, stop=(j == n - 1)),
                                 r=[kT_b, qT_b], w=[ps_b], mark=(j == n - 1))
                        if near:
                            K.op(act, lambda e: e.activation(out=pT_[:, 0:n * 128], in_=ps_[:, 0:n * 128], func=AF.Exp), r=[ps_b], w=[pT_b])
                        else:
                            K.op(act, lambda e: e.activation(out=pT_[:, 0:n * 128], in_=ps_[:, 0:n * 128], func=AF.Exp, bias=b15[:, h:h + 1]), r=[ps_b, b15_b], w=[pT_b])
                        for j, ks in enumerate(grp):
                            last = (gi == len(groups) - 1) and (j == n - 1)
                            K.op(pe, lambda e, j=j, ks=ks: e.matmul(po[:, 0:129], lhsT=pT_[:, j * 128:(j + 1) * 128], rhs=vaug[:, ks, 0:129], start=(gi == 0 and j == 0), stop=last),
                                 r=[pT_b, vaug_b], w=[po_b], mark=(j == n - 1))
                    K.op(act, lambda e: e.copy(out=oraw[:, h, 0:129], in_=po[:, 0:129]), r=[po_b], w=[oraw_b])
                K.op(dve, lambda e: e.reciprocal(out=rinv16[:], in_=oraw[:, :, 128]), r=[oraw_b], w=[rinv16_b])
                for h in range(A_HEADS):
                    K.op(dve, lambda e: e.tensor_scalar(out=oa[:, h * 128:(h + 1) * 128], in0=oraw[:, h, 0:128], scalar1=rinv16[:, h:h + 1], scalar2=None, op0=ALU.mult),
                         r=[oraw_b, rinv16_b], w=[oa_b])
                K.dma(sp, o_s.ap()[qt * 128:(qt + 1) * 128, 0:2048], oa[:], oa_b, r=[oa_b])

            with ExitStack() as esS:
                side = adaln_gen(esS, list(range(16, 40)), ldq=pool)
                load_q(0)
                if NQ > 1:
                    load_q(1)
                S1(0)
                S2(0)
                for qt in range(NQ):
                    if qt + 1 < NQ:
                        S1(qt + 1)
                    S3(qt)
                    if qt + 2 < NQ:
                        load_q(qt + 2)
                    if qt + 1 < NQ:
                        S2(qt + 1)
                    for _ in range(2 if qt % 2 == 0 else 1):
                        next(side, None)
                for _ in side:
                    pass
                K.barrier()

        with ExitStack() as es2:
            hT, hT_b = K.sb("hT_C", [128, KC, 512], BF16, es2)
            qTg, qTg_b = K.sb("qTg", [128, 8, 512], BF16, es2)
            kTg, kTg_b = K.sb("kTg", [128, 8, 512], BF16, es2)
            ktok, ktok_b = K.sb("ktok", [128, 4, 1024], BF16, es2)
            vtok, vtok_b = K.sb("vtok", [128, 4, 2048], BF16, es2)
            Gt, Gt_b = K.sb("Gt", [128, 4, 2048], BF16, es2)
            lrT, lrT_b = K.sb("lrT", [GR, 512], BF16, es2)
            wg2, wg2_b = K.sb("wg2", [GR, 1024], BF16, es2)
            bg2, bg2_b = K.sb("bg2", [1, 1024], BF16, es2)
            gnb, gnb_b = K.sb("gnb", [128, B_DV], F32, es2)
            Sf, Sf_b = K.sb("Sf", [128, 8, 512], F32, es2)
            Sb, Sb_b = K.sb("Sb", [128, 8, 512], BF16, es2)
            e1, e1_b = K.sb("e1", [128, 1024], F32, es2)
            spl, spl_b = K.sb("spl", [128, 1024], BF16, es2)
            EbT, EbT_b = K.sb("EbT", [128, 8, 128], F32, es2)
            EnbT, EnbT_b = K.sb("EnbT", [128, 8, 128], BF16, es2)
            qe, qe_b = K.sb("qe", [128, 8, 128], BF16, es2)
            qeP, qeP_b = K.sb("qeP", [128, 8, 2, 128], BF16, es2)
            ke, ke_b = K.sb("ke", [128, 8, 128], BF16, es2)
            Ec, Ec_b = e1, e1_b
            sq, sq_b = e1[:, 0:256].bitcast(BF16), e1_b
            kd, kd_b = K.sb("kd", [128, 1024], BF16, es2)
            Am4, _ = K.sb("Am4", [128, 4, 128], BF16, es2)
            Am_bs = [Buf("Am%d" % i) for i in range(4)]
            pa_bs = [Buf("pa%d" % i) for i in range(4)]
            Sf_bs = [Buf("Sf%d" % i) for i in range(8)]
            Sb_bs = [Buf("Sb%d" % i) for i in range(8)]
            ss_bs = [Buf("ss%d" % i) for i in range(4)]
            rs_bs = [Buf("rs%d" % i) for i in range(4)]
            ssq4, _ = K.sb("ssq4", [128, 4], F32, es2)
            rs4, _ = K.sb("rs4", [128, 4], F32, es2)
            uctr = [0]

            ssq, ssq_b = K.sb("ssq", [128, 1], F32, es2)
            rs2, rs2_b = K.sb("rs2", [128, 1], F32, es2)
            ob, ob_b = K.sb("ob", [128, 2048], BF16, es2)
            silt = [K.sb("silt%d" % i, [128, 256], F32, es2) for i in range(2)]

            es3 = ExitStack()
            K.dma(pool, wg2[:], wg2_d.ap(), wg2_b, w=[wg2_b])
            K.dma(pool, bg2[:], bg2_d.ap(), bg2_b, w=[bg2_b])
            K.dma(sp, gnb[:], gn_d.ap()[0:1, :].partition_broadcast(128).rearrange("p a h -> p (a h)"), gnb_b, w=[gnb_b])
            K.op(dve, lambda e: e.tensor_scalar(out=gnb[:], in0=gnb[:], scalar1=float(B_DV) ** 0.5, scalar2=None, op0=ALU.mult), r=[gnb_b], w=[gnb_b])
            K.op(dve, lambda e: e.memset(Sf[:], 0.0), w=[Sf_b])
            K.op(pool, lambda e: e.memset(Sb[:], 0.0), w=[Sb_b])
            for b_ in Sf_bs:
                b_.w = Sf_b.w
            for b_ in Sb_bs:
                b_.w = Sb_b.w
            K.op(pool, lambda e: e.memset(qeP[:], 0.0), w=[qeP_b])
            K.barrier()
            es3.close()

            sctr = [0]

            def evac_silu(ps_ap, ps_b, ts, col, w):
                for o_ in range(0, w, 256):
                    sl, sl_b = silt[sctr[0] % 2]
                    sctr[0] += 1
                    K.op(act, lambda e: e.activation(out=sl[:, 0:256], in_=ps_ap[:, o_:o_ + 256], func=AF.Silu), r=[ps_b], w=[sl_b])
                    hcol = (col + o_) % B_DV
                    K.op(pool, lambda e: e.tensor_tensor(out=Gt[:, ts, col + o_:col + o_ + 256], in0=sl[:, 0:256], in1=gnb[:, hcol:hcol + 256], op=ALU.mult), r=[sl_b, gnb_b], w=[Gt_b])

            for tt in range(NT):
                t0 = tt * 512
                K.dma(act, hT[:].rearrange("p a b -> p (a b)"), hT_s.ap()[tt], hT_b, w=[hT_b])
                proj_S(hT, hT_b, win_a, O_BQ, 1024, evac_copy(lambda col, m: qTg[:, col // 128, :], qTg_b, scale=B_DK ** -0.5))
                proj_S(hT, hT_b, win_a, O_BK, 1024, evac_copy(lambda col, m: kTg[:, col // 128, :], kTg_b))
                proj_M(hT, hT_b, win_a, O_BK, 1024, evac_copy(lambda ts, col, w: ktok[:, ts, col:col + w], ktok_b))
                proj_M(hT, hT_b, win_a, O_BV, 2048, evac_copy(lambda ts, col, w: vtok[:, ts, col:col + w], vtok_b))
                proj_S(hT, hT_b, win_a, O_BG, GR, evac_copy(lambda col, m: lrT[:, :], lrT_b))
                proj_M(hT, hT_b, win_a, O_BR, 2048, evac_silu)
                for ts in range(4):
                    tq = slice(ts * 128, (ts + 1) * 128)
                    for hf in range(2):
                        pt, pt_b = pst(4 + hf)
                        cs_ = slice(hf * 512, (hf + 1) * 512)
                        K.op(pe, lambda e: e.matmul(pt[:, :], lhsT=onesb[0:1, :], rhs=bg2[0:1, cs_], start=True, stop=False), r=[onesb_b, bg2_b], w=[pt_b], mark=False)
                        K.op(pe, lambda e: e.matmul(pt[:, :], lhsT=lrT[:, tq], rhs=wg2[:, cs_], start=False, stop=True), r=[lrT_b, wg2_b], w=[pt_b])
                        K.op(act, lambda e: e.activation(out=e1[:, cs_], in_=pt[:, :], func=AF.Exp, scale=-1.0), r=[pt_b], w=[e1_b])
                    K.op(act, lambda e: e.activation(out=spl[:], in_=e1[:], func=AF.Ln, bias=1.0), r=[e1_b], w=[spl_b])
                    for hf in range(2):
                        pt, pt_b = pst(4 + hf)
                        for j in range(4):
                            c8 = hf * 4 + j
                            K.op(pe, lambda e, c8=c8, j=j: e.matmul(pt[:, j * 128:(j + 1) * 128], lhsT=spl[:, c8 * 128:(c8 + 1) * 128], rhs=utri, start=True, stop=True),
                                 r=[spl_b, cmb_b], w=[pt_b], mark=(j == 3))
                        bsl = slice(hf * 4, hf * 4 + 4)
                        K.op(act, lambda e: e.activation(out=EbT[:, bsl, :], in_=pt[:, :].rearrange("p (a b) -> p a b", b=128), func=AF.Exp), r=[pt_b], w=[EbT_b])
                        K.op(act, lambda e: e.activation(out=EnbT[:, bsl, :], in_=pt[:, :].rearrange("p (a b) -> p a b", b=128), func=AF.Exp, scale=-1.0), r=[pt_b], w=[EnbT_b])
                    K.op(dve, lambda e: e.tensor_tensor(out=qe[:], in0=qTg[:, :, tq], in1=EbT[:], op=ALU.mult), r=[qTg_b, EbT_b], w=[qe_b])
                    K.op(pool, lambda e: e.tensor_copy(out=qeP[:, :, 0, 0:64], in_=qe[:, :, 0:64]), r=[qe_b], w=[qeP_b])
                    K.op(pool, lambda e: e.tensor_copy(out=qeP[:, :, 1, 64:128], in_=qe[:, :, 64:128]), r=[qe_b], w=[qeP_b])
                    K.op(dve, lambda e: e.tensor_tensor(out=ke[:], in0=kTg[:, :, tq], in1=EnbT[:], op=ALU.mult), r=[kTg_b, EnbT_b], w=[ke_b])
                    for hf in range(2):
                        pt, pt_b = pst(4 + hf)
                        cs_ = slice(hf * 512, (hf + 1) * 512)
                        K.op(pe, lambda e: e.matmul(pt[:, :], lhsT=ustrict, rhs=spl[:, cs_], start=True, stop=True), r=[spl_b, cmb_b], w=[pt_b])
                        K.op(act, lambda e: e.activation(out=Ec[:, cs_], in_=pt[:, :], func=AF.Exp), r=[pt_b], w=[Ec_b])
                    K.op(dve, lambda e: e.tensor_tensor(out=kd[:], in0=ktok[:, ts, :], in1=Ec[:], op=ALU.mult), r=[ktok_b, Ec_b], w=[kd_b])
                    pa, _ = pst(6)
                    for hh in range(B_HEADS):
                        for c2 in range(2):
                            c8 = hh * 2 + c2
                            K.op(pe, lambda e, c8=c8, c2=c2: e.matmul(pa[:, hh * 128:(hh + 1) * 128], lhsT=ke[:, c8, :], rhs=qe[:, c8, :], start=(c2 == 0), stop=(c2 == 1)),
                                 r=[ke_b, qe_b], w=[pa_bs[hh]], mark=(c2 == 1))
                        K.op(dve, lambda e: e.tensor_tensor(out=Am4[:, hh, :], in0=pa[:, hh * 128:(hh + 1) * 128], in1=trimf, op=ALU.mult), r=[pa_bs[hh], cmf_b], w=[Am_bs[hh]])
                    for hh in range(B_HEADS):
                        vs = slice(hh * B_DV, (hh + 1) * B_DV)
                        po, po_b = pst(hh)
                        K.op(pe, lambda e: e.matmul(po[:, :], lhsT=Am4[:, hh, :], rhs=vtok[:, ts, vs], start=True, stop=False), r=[Am_bs[hh], vtok_b], w=[po_b], mark=False)
                        for c2 in range(2):
                            c8 = hh * 2 + c2
                            K.op(pe, lambda e, c8=c8: e.matmul(po[:, :], lhsT=qeP[:, c8, 0, :], rhs=Sb[:, c8, :], start=False, stop=False),
                                 r=[qeP_b, Sb_bs[c8]], w=[po_b], mark=(c2 == 1))
                    for ck in range(2):
                        rows = slice(ck * 64, ck * 64 + 64)
                        if ck == 1:
                            for hh in range(B_HEADS):
                                po, po_b = pst(hh)
                                for c2 in range(2):
                                    c8 = hh * 2 + c2
                                    K.op(pe, lambda e, c8=c8: e.matmul(po[:, :], lhsT=qeP[:, c8, 1, :], rhs=Sb[:, c8, :], start=False, stop=(c2 == 1)),
                                         r=[qeP_b, Sb_bs[c8]], w=[po_b], mark=(c2 == 1))
                        for hh in range(B_HEADS):
                            vs = slice(hh * B_DV, (hh + 1) * B_DV)
                            for c2 in range(2):
                                c8 = hh * 2 + c2
                                pu, pu_b = pst((4, 5, 7)[uctr[0] % 3])
                                uctr[0] += 1
                                K.op(pe, lambda e, c8=c8: e.matmul(pu[:, :], lhsT=kd[rows, c8 * 128:(c8 + 1) * 128], rhs=vtok[rows, ts, vs], start=True, stop=True),
                                     r=[kd_b, vtok_b], w=[pu_b])
                                K.op(dve, lambda e, c8=c8, ck=ck: e.scalar_tensor_tensor(out=Sf[:, c8, :], in0=Sf[:, c8, :], scalar=EbT[:, c8, ck * 64 + 63:ck * 64 + 64], in1=pu[:, :],
                                                                                          op0=ALU.mult, op1=ALU.add), r=[Sf_bs[c8], EbT_b, pu_b], w=[Sf_bs[c8]])
                                K.op(act, lambda e, c8=c8: e.copy(out=Sb[:, c8, :], in_=Sf[:, c8, :]), r=[Sf_bs[c8]], w=[Sb_bs[c8]])
                    for hh in range(B_HEADS):
                        vs = slice(hh * B_DV, (hh + 1) * B_DV)
                        po, po_b = pst(hh)
                        K.op(act, lambda e: e.activation(out=sq[:], in_=po[:, :], func=AF.Square, accum_out=ssq4[:, hh:hh + 1]), r=[po_b], w=[sq_b, ss_bs[hh]])
                        K.op(act, lambda e: e.activation(out=rs4[:, hh:hh + 1], in_=ssq4[:, hh:hh + 1], func=AF.Sqrt, bias=epsc[:, 1:2]), r=[ss_bs[hh], epsc_b], w=[rs_bs[hh]])
                        K.op(dve, lambda e: e.reciprocal(out=rs4[:, hh:hh + 1], in_=rs4[:, hh:hh + 1]), r=[rs_bs[hh]], w=[rs_bs[hh]])
                        K.op(dve, lambda e: e.scalar_tensor_tensor(out=ob[:, vs], in0=po[:, :], scalar=rs4[:, hh:hh + 1], in1=Gt[:, ts, vs], op0=ALU.mult, op1=ALU.mult),
                             r=[po_b, rs_bs[hh], Gt_b], w=[ob_b])
                    r0 = t0 + ts * 128
                    K.dma(sp, o_s.ap()[r0:r0 + 128, 2048:4096], ob[:], ob_b, r=[ob_b])
            K.barrier()

        with ExitStack() as es2:
            oTs = [K.sb("oT%d" % i, [128, KC, 512], BF16, es2) for i in range(2)]
            ots = [K.sb("ot%d" % i, [128, D], BF16, es2) for i in range(2)]
            g1b, g1b_b = K.sb("g1b", [128, D], F32, es2)
            xbs = [K.sb("xb%d" % i, [128, 512], F32, es2) for i in range(3)]
            tms = [K.sb("tm%d" % i, [128, 512], F32, es2) for i in range(3)]
            K.dma(sp, g1b[:], g1b_s.ap(), g1b_b, w=[g1b_b])
            dctr = [0]

            def prepD(tt):
                t0 = tt * 512
                oT, oT_b = oTs[tt % 2]
                for ts in range(4):
                    ot, ot_b = ots[ts % 2]
                    K.dma(act, ot[:], o_s.ap()[t0 + ts * 128:t0 + (ts + 1) * 128, :], ot_b, w=[ot_b])
                    for g in range(8):
                        pt, pt_b = pst(4 + g % 2)
                        ptb = pt[:].bitcast(BF16)
                        for j in range(4):
                            kc = g * 4 + j
                            K.op(pe, lambda e, kc=kc, j=j: e.transpose(out=ptb[:, j * 128:(j + 1) * 128], in_=ot[:, kc * 128:(kc + 1) * 128], identity=identb),
                                 r=[ot_b, cmb_b], w=[pt_b], mark=(j == 3))
                        if g % 2:
                            K.op(act, lambda e, g=g: e.copy(out=oT[:, g * 4:g * 4 + 4, ts * 128:(ts + 1) * 128], in_=ptb[:, 0:512].rearrange("p (a b) -> p a b", b=128)), r=[pt_b], w=[oT_b])
                        else:
                            K.op(dve, lambda e, g=g: e.tensor_copy(out=oT[:, g * 4:g * 4 + 4, ts * 128:(ts + 1) * 128], in_=ptb[:, 0:512].rearrange("p (a b) -> p a b", b=128)), r=[pt_b], w=[oT_b])

            prepD(0)
            for tt in range(NT):
                t0 = tt * 512
                oT, oT_b = oTs[tt % 2]
                if tt + 1 < NT:
                    prepD(tt + 1)

                def evac_res(ps_ap, ps_b, ts, col, w):
                    i = dctr[0] % 3
                    dctr[0] += 1
                    xb, xb_b = xbs[i]
                    tm, tm_b = tms[i]
                    r0 = t0 + ts * 128
                    K.dma(act, xb[:, 0:w], x_a[r0:r0 + 128, col:col + w], xb_b, w=[xb_b])
                    K.op(dve, lambda e: e.tensor_tensor(out=tm[:, 0:w], in0=ps_ap, in1=g1b[:, col:col + w], op=ALU.mult), r=[ps_b, g1b_b], w=[tm_b])
                    K.op(dve, lambda e: e.scalar_tensor_tensor(out=tm[:, 0:w], in0=xb[:, 0:w], scalar=ALPHA, in1=tm[:, 0:w], op0=ALU.mult, op1=ALU.add), r=[xb_b, tm_b], w=[tm_b])
                    K.dma(sp, xp_s.ap()[r0:r0 + 128, col:col + w], tm[:, 0:w], tm_b, r=[tm_b])

                proj_M(oT, oT_b, wout_a, 0, D, evac_res)
            K.barrier()

        def ln_pass(src_a, dst_a, g_d, b_d, tag, side=None, nside=0):
            with ExitStack() as es2:
                sgen = side(es2) if side is not None else None
                gb, gb_b = K.sb("lng" + tag, [128, D], F32, es2)
                bb, bb_b = K.sb("lnb" + tag, [128, D], F32, es2)
                xts = [K.sb("lx%s%d" % (tag, i), [128, D], F32, es2) for i in range(2)]
                ys = [K.sb("ly%s%d" % (tag, i), [128, D], F32, es2) for i in range(2)]
                st, st_b = K.sb("lst" + tag, [128, 8, 6], F32, es2)
                mv, mv_b = K.sb("lmv" + tag, [128, 2], F32, es2)
                rstd, rstd_b = K.sb("lrs" + tag, [128, 1], F32, es2)
                K.dma(sp, gb[:], g_d.ap()[0:1, :].partition_broadcast(128).rearrange("p a h -> p (a h)"), gb_b, w=[gb_b])
                K.dma(sp, bb[:], b_d.ap()[0:1, :].partition_broadcast(128).rearrange("p a h -> p (a h)"), bb_b, w=[bb_b])
                for qt in range(NQ):
                    xt, xt_b = xts[qt % 2]
                    y, y_b = ys[qt % 2]
                    K.dma(act, xt[:], src_a[qt * 128:(qt + 1) * 128, :], xt_b, w=[xt_b])
                    ln_stats(xt, xt_b, mv, mv_b, st, st_b, rstd, rstd_b)
                    K.op(dve, lambda e: e.tensor_scalar(out=y[:], in0=xt[:], scalar1=mv[:, 0:1], scalar2=rstd[:, 0:1], op0=ALU.subtract, op1=ALU.mult), r=[xt_b, mv_b, rstd_b], w=[y_b])
                    K.op(dve, lambda e: e.tensor_tensor(out=y[:], in0=y[:], in1=gb[:], op=ALU.mult), r=[y_b, gb_b], w=[y_b])
                    K.op(pool, lambda e: e.tensor_tensor(out=y[:], in0=y[:], in1=bb[:], op=ALU.add), r=[y_b, bb_b], w=[y_b])
                    K.dma(sp, dst_a[qt * 128:(qt + 1) * 128, :], y[:], y_b, r=[y_b])
                    if sgen is not None:
                        for _ in range(nside):
                            next(sgen, None)
                if sgen is not None:
                    for _ in sgen:
                        pass
                K.barrier()

        ln_pass(xp_s.ap(), x1_s.ap(), ln1g_d, ln1b_d, "1", side=lambda es_: adaln_gen(es_, list(range(40, 48)), ldq=pool), nside=(8 * 128 + S - 1) // S)

        with ExitStack() as es2:
            hT, hT_b = K.sb("hT_E", [128, KC, 512], BF16, es2)
            fT, fT_b = K.sb("fT", [128, FC, 512], BF16, es2)
            cw, cw_b = K.sb("cw", [128, FC, 3], F32, es2)
            cbias, cbias_b = K.sb("cbias", [128, FC], F32, es2)
            carry, carry_b = K.sb("carry", [128, FC, 2], F32, es2)
            K.dma(sp, cw[:], convw_d.ap(), cw_b, w=[cw_b])
            K.dma(sp, cbias[:], convb_d.ap(), cbias_b, w=[cbias_b])
            K.op(dve, lambda e: e.memset(carry[:], 0.0), w=[carry_b])
            htiles = alloc_ht_tiles(es2, xt_alias=fT[:, 0:16, :].rearrange("p a b -> p (a b)").bitcast(F32), xh_alias=fT[:, 16:24, :].rearrange("p a b -> p (a b)"))
            ubs = [K.sb("ub%d" % i, [128, 516], F32, es2) for i in range(2)]
            a1s = [K.sb("a1%d" % i, [128, 512], F32, es2) for i in range(2)]
            g2b, g2b_b = K.sb("g2b", [128, D], F32, es2)
            xbs = [K.sb("fxb%d" % i, [128, 512], F32, es2) for i in range(2)]
            tms = [K.sb("ftm%d" % i, [128, 512], F32, es2) for i in range(2)]
            K.dma(sp, g2b[:], g2b_s.ap(), g2b_b, w=[g2b_b])
            dctr = [0]
            slabs = [(k0, min(16, FC - k0)) for k0 in range(0, FC, 16)]
            for tt in range(NT):
                t0 = tt * 512
                if tt > 0:
                    K.barrier()
                make_hT(es2, x1_s.ap(), t0, 2, hT, hT_b, htiles)
                for f2 in range(FC // 2):
                    wu, wu_b = load_w(wup_a, f2 * 256, 256)
                    wg, wg_b = load_w(wgate_a, f2 * 256, 256)
                    for j in range(2):
                        fc = f2 * 2 + j
                        pu, pu_b = pst(fc % 4)
                        for kc in range(KC):
                            K.op(pe, lambda e, kc=kc: e.matmul(pu[:, :], lhsT=wu[:, kc, j * 128:(j + 1) * 128], rhs=hT[:, kc, :], start=(kc == 0), stop=(kc == KC - 1)),
                                 r=[wu_b, hT_b], w=[pu_b], mark=(kc == KC - 1))
                        ub, ub_b = ubs[fc % 2]
                        a1, a1_b = a1s[fc % 2]
                        K.op(act, lambda e: e.copy(out=ub[:, 2:514], in_=pu[:, :]), r=[pu_b], w=[ub_b])
                        K.op(pool, lambda e, fc=fc: e.tensor_copy(out=ub[:, 0:2], in_=carry[:, fc, :]), r=[carry_b], w=[ub_b])
                        K.op(pool, lambda e, fc=fc: e.tensor_copy(out=carry[:, fc, :], in_=ub[:, 512:514]), r=[ub_b], w=[carry_b])
                        K.op(dve, lambda e, fc=fc: e.tensor_scalar(out=a1[:], in0=ub[:, 2:514], scalar1=cw[:, fc, 2:3], scalar2=cbias[:, fc:fc + 1], op0=ALU.mult, op1=ALU.add),
                             r=[ub_b, cw_b, cbias_b], w=[a1_b])
                        K.op(dve, lambda e, fc=fc: e.scalar_tensor_tensor(out=a1[:], in0=ub[:, 1:513], scalar=cw[:, fc, 1:2], in1=a1[:], op0=ALU.mult, op1=ALU.add),
                             r=[ub_b, cw_b, a1_b], w=[a1_b])
                        K.op(dve, lambda e, fc=fc: e.scalar_tensor_tensor(out=a1[:], in0=ub[:, 0:512], scalar=cw[:, fc, 0:1], in1=a1[:], op0=ALU.mult, op1=ALU.add),
                             r=[ub_b, cw_b, a1_b], w=[a1_b])
                        K.op(act, lambda e: e.activation(out=a1[:], in_=a1[:], func=AF.Gelu_apprx_tanh), r=[a1_b], w=[a1_b])
                    for j in range(2):
                        fc = f2 * 2 + j
                        pg, pg_b = pst(4 + fc % 4)
                        a1, a1_b = a1s[fc % 2]
                        for kc in range(KC):
                            K.op(pe, lambda e, kc=kc: e.matmul(pg[:, :], lhsT=wg[:, kc, j * 128:(j + 1) * 128], rhs=hT[:, kc, :], start=(kc == 0), stop=(kc == KC - 1)),
                                 r=[wg_b, hT_b], w=[pg_b], mark=(kc == KC - 1))
                        K.op(dve, lambda e, fc=fc: e.tensor_tensor(out=fT[:, fc, :], in0=a1[:], in1=pg[:, :], op=ALU.mult), r=[a1_b, pg_b], w=[fT_b])
                for db in range(D // 512):
                    col = db * 512
                    for (k0, nk) in slabs:
                        wt, wt_b = nextw()
                        wtv = wt[:].rearrange("p a b -> p (a b)")[:, 0:16 * 512].rearrange("p (a b) -> p a b", b=512)
                        K.dma(pool, wtv[:, 0:nk, :], wdown_a[:, k0:k0 + nk, col:col + 512], wt_b, w=[wt_b])
                        for ts in range(4):
                            pt, pt_b = pst((db % 2) * 4 + ts)
                            for kl in range(nk):
                                fc = k0 + kl
                                K.op(pe, lambda e, fc=fc, kl=kl: e.matmul(pt[:, :], lhsT=fT[:, fc, ts * 128:(ts + 1) * 128], rhs=wtv[:, kl, :], start=(fc == 0), stop=(fc == FC - 1)),
                                     r=[fT_b, wt_b], w=[pt_b], mark=(kl == nk - 1))
                    for ts in range(4):
                        pt, pt_b = pst((db % 2) * 4 + ts)
                        i = dctr[0] % 2
                        dctr[0] += 1
                        xb, xb_b = xbs[i]
                        tm, tm_b = tms[i]
                        r0 = t0 + ts * 128
                        K.dma(act, xb[:], x1_s.ap()[r0:r0 + 128, col:col + 512], xb_b, w=[xb_b])
                        K.op(dve, lambda e: e.tensor_tensor(out=tm[:], in0=pt[:, :], in1=g2b[:, col:col + 512], op=ALU.mult), r=[pt_b, g2b_b], w=[tm_b])
                        K.op(dve, lambda e: e.scalar_tensor_tensor(out=tm[:], in0=xb[:], scalar=ALPHA, in1=tm[:], op0=ALU.mult, op1=ALU.add), r=[xb_b, tm_b], w=[tm_b])
                        K.dma(sp, xp_s.ap()[r0:r0 + 128, col:col + 512], tm[:], tm_b, r=[tm_b])
            K.barrier()

        ln_pass(xp_s.ap(), out_a, ln2g_d, ln2b_d, "2")
        K.barrier()
    return nc


def make_in_maps(inp, S):
    cmask, oh = host_consts()
    B = inp["x"].shape[0]
    f = lambda a: np.ascontiguousarray(np.asarray(a, dtype=np.float32))
    shared = {
        "t5": f(inp["t5_table"]),
        "w_ada": f(inp["w_ada"][0]), "b_ada": f(inp["b_ada"][0]).reshape(1, -1),
        "w_in": f(inp["w_in"][0]), "w_g2": f(inp["w_g2"][0]), "b_g2": f(inp["b_g2"][0]).reshape(1, -1),
        "gla_norm": f(inp["gla_norm"][0]).reshape(1, -1), "w_out": f(inp["w_out"][0]),
        "ln1_g": f(inp["ln1_g"][0]).reshape(1, -1), "ln1_b": f(inp["ln1_b"][0]).reshape(1, -1),
        "w_up": f(inp["w_up"][0]), "w_gate": f(inp["w_gate"][0]),
        "conv_wT": f(np.asarray(inp["conv_w"][0]).reshape(3, FC, 128).transpose(2, 1, 0)),
        "conv_bT": f(np.asarray(inp["conv_b"][0]).reshape(FC, 128).T),
        "w_down": f(inp["w_down"][0]),
        "ln2_g": f(inp["ln2_g"][0]).reshape(1, -1), "ln2_b": f(inp["ln2_b"][0]).reshape(1, -1),
        "cmask": f(cmask), "onehot": f(oh),
    }
    maps = []
    for b in range(B):
        m = dict(shared)
        m["x"] = f(inp["x"][b])
        m["cT"] = f(np.asarray(inp["c"][b]).reshape(KC, 128).T)
        maps.append(m)
    return maps


def kernel(**inputs):
    S = inputs["x"].shape[1]
    B = inputs["x"].shape[0]
    nc = build(S)
    maps = make_in_maps(inputs, S)
    res = run_bass_kernel_spmd(nc, maps, core_ids=list(range(B)))
    return np.stack([np.asarray(r["out"], dtype=np.float32) for r in res.results], axis=0)
```

```python
import math
from contextlib import ExitStack

import numpy as np
import concourse.bass as bass
import concourse.mybir as mybir
from concourse.ap import AP as RawAP
from concourse.bass_utils import run_bass_kernel_spmd

F32, BF16 = mybir.dt.float32, mybir.dt.bfloat16
AF = mybir.ActivationFunctionType
ALU = mybir.AluOpType

D = 4096
KC = 32
CH = 64
A_HEADS, A_DH = 16, 128
I_HEADS, I_D = 32, 64
B_HEADS, B_DV, B_DK = 4, 512, 256
GR = 16
D_FF = 11008
FC = D_FF // 128
ALPHA = 2.0 ** 0.25
EPS = 1e-6
BIG = 1.0e30
NEG = -30000.0
O_AQ = 0
O_AK = 2048
O_AV = 2176
O_IQ = 2304
O_IK = 4352
O_IW = 4416
O_BQ = 4448
O_BK = 5472
O_BV = 6496
O_BG = 8544
O_BR = 8560
D_IN = 10608
TL = 384


class Tok:
    __slots__ = ("si", "val")

    def __init__(self, si, val):
        self.si, self.val = si, val


class Pending:
    __slots__ = ("eng", "tok")

    def __init__(self, eng):
        self.eng, self.tok = eng, None


class Buf:
    def __init__(self, name):
        self.name = name
        self.w = None
        self.r = {}
        self.dsi = None
        self.dcnt = 0


class Eng:
    def __init__(self, K, name, e):
        self.K, self.name, self.e = K, name, e
        self.si = K.newsem("e_" + name)
        self.cnt = 0
        self.waited = {}
        self.pending = []

    def wait(self, tok):
        if tok is None:
            return
        if isinstance(tok, Pending):
            if tok.tok is None:
                assert tok.eng is self, "unresolved pending token from other engine"
                return
            tok = tok.tok
        if self.waited.get(tok.si, 0) >= tok.val:
            return
        self.e.wait_ge(self.K.sems[tok.si], tok.val)
        self.waited[tok.si] = tok.val


class KB:
    def __init__(self, nc, es):
        self.nc, self.es = nc, es
        self.sems = []
        self.pe = Eng(self, "pe", nc.tensor)
        self.act = Eng(self, "act", nc.scalar)
        self.dve = Eng(self, "dve", nc.vector)
        self.pool = Eng(self, "pool", nc.gpsimd)
        self.sp = Eng(self, "sp", nc.sync)
        self.engs = [self.pe, self.act, self.dve, self.pool, self.sp]
        self.dbufs = []
        self.rr = 0

    def newsem(self, name):
        s = self.es.enter_context(self.nc.semaphore("%s_%d" % (name, len(self.sems))))
        self.sems.append(s)
        return len(self.sems) - 1

    def sb(self, name, shape, dt, es=None):
        t = (es or self.es).enter_context(self.nc.sbuf_tensor(name, list(shape), dt))
        return t, Buf(name)

    def ps(self, name, shape, dt, es=None):
        t = (es or self.es).enter_context(self.nc.psum_tensor(name, list(shape), dt))
        return t, Buf(name)

    def op(self, eng, fn, r=(), w=(), mark=True):
        for b in r:
            eng.wait(b.w)
        for b in w:
            eng.wait(b.w)
            for t in list(b.r.values()):
                eng.wait(t)
        ins = fn(eng.e)
        if mark:
            ins.then_inc(self.sems[eng.si], 1)
            eng.cnt += 1
            tok = Tok(eng.si, eng.cnt)
            for p in eng.pending:
                p.tok = tok
            eng.pending = []
        else:
            tok = Pending(eng)
            eng.pending.append(tok)
        for b in r:
            b.r[eng.si] = tok
        for b in w:
            b.w = tok
            b.r = {}
        return tok

    def dma(self, eng, out, in_, owner, r=(), w=()):
        if owner.dsi is None:
            owner.dsi = self.newsem("d_" + owner.name)
            self.dbufs.append(owner)
        for b in r:
            eng.wait(b.w)
        for b in w:
            if not (isinstance(b.w, Tok) and b.w.si == owner.dsi):
                eng.wait(b.w)
            for t in list(b.r.values()):
                eng.wait(t)
        eng.e.dma_start(out=out, in_=in_).then_inc(self.sems[owner.dsi], 16)
        owner.dcnt += 16
        tok = Tok(owner.dsi, owner.dcnt)
        for b in r:
            b.r[owner.dsi] = tok
        for b in w:
            b.w = tok
            b.r = {}
        return tok

    def barrier(self):
        for e in self.engs:
            assert not e.pending, e.name
        for e in self.engs:
            for e2 in self.engs:
                if e2 is not e and e2.cnt > 0:
                    e.wait(Tok(e2.si, e2.cnt))
            for b in self.dbufs:
                if b.dcnt > 0:
                    e.wait(Tok(b.dsi, b.dcnt))

    def ew(self):
        self.rr += 1
        return self.dve if self.rr % 2 else self.pool


def t5_bucket_np(rel):
    half, max_exact = 16, 8
    ret = np.where(rel > 0, half, 0)
    n = np.abs(rel)
    nf = np.maximum(n, 1).astype(np.float32)
    large = max_exact + (np.log(nf / np.float32(max_exact)) / np.float32(math.log(128 / 8)) * np.float32(8)).astype(np.int32)
    large = np.minimum(large, half - 1)
    return ret + np.where(n < max_exact, n, large)


def host_consts():
    c = {}
    i = np.arange(128)
    same = (i[:, None] // CH) == (i[None, :] // CH)
    c["ident"] = np.eye(128, dtype=np.float32)
    c["utri"] = np.where(same & (i[:, None] <= i[None, :]), -1.0 / 16, 0.0).astype(np.float32)
    c["ustrict"] = np.where(same & (i[:, None] > i[None, :]), -1.0 / 16, 0.0).astype(np.float32)
    c["trimask"] = np.where(same & (i[:, None] <= i[None, :]), 1.0, 0.0).astype(np.float32)
    m = np.arange(TL)
    rel = 127 - m
    bk = t5_bucket_np(rel)
    oh = np.zeros((32, TL), np.float32)
    oh[bk, m] = 1.0
    c["onehot"] = oh
    return np.concatenate([c["ident"], c["utri"], c["ustrict"], c["trimask"]], axis=1), oh


def build(S, dbg=False):
    NT = S // 512
    NQ = S // 128
    TOPK = min(256, S // 4)
    nc = bass.Bass("TRN2", target_bir_lowering=False)

    def din(name, shape, dt=F32):
        return nc.dram_tensor(name, list(shape), dt, kind="ExternalInput")

    x_d = din("x", [S, D])
    cT_d = din("cT", [128, KC])
    t5_d = din("t5", [32, A_HEADS])
    wada_d = din("w_ada", [D, 6 * D])
    bada_d = din("b_ada", [1, 6 * D])
    win_d = din("w_in", [D, D_IN])
    wg2_d = din("w_g2", [GR, 1024])
    bg2_d = din("b_g2", [1, 1024])
    gn_d = din("gla_norm", [1, B_DV])
    wout_d = din("w_out", [D, D])
    ln1g_d = din("ln1_g", [1, D])
    ln1b_d = din("ln1_b", [1, D])
    wup_d = din("w_up", [D, D_FF])
    wgate_d = din("w_gate", [D, D_FF])
    convw_d = din("conv_wT", [128, FC, 3])
    convb_d = din("conv_bT", [128, FC])
    wdown_d = din("w_down", [D_FF, D])
    ln2g_d = din("ln2_g", [1, D])
    ln2b_d = din("ln2_b", [1, D])
    cm_d = din("cmask", [128, 512])
    oh_d = din("onehot", [32, TL])
    out_d = nc.dram_tensor("out", [S, D], F32, kind="ExternalOutput")

    def dscr(name, shape, dt=F32):
        return nc.dram_tensor(name, list(shape), dt, kind=("ExternalOutput" if dbg else "Internal"))

    g1b_s = dscr("g1b_s", [128, D])
    g2b_s = dscr("g2b_s", [128, D])
    gsc_s = dscr("gsc_s", [A_HEADS, TL])
    grep_s = dscr("grep_s", [128, A_HEADS * TL])
    o_s = dscr("o_s", [S, D], BF16)
    hT_s = nc.dram_tensor("hT_s", [NT, 128, KC * 512], BF16, kind="Internal")
    q_s = nc.dram_tensor("q_s", [NT, 128, 16 * 512], BF16, kind="Internal")
    qi_s = nc.dram_tensor("qi_s", [NT, 128, 16 * 512], BF16, kind="Internal")
    xp_s = dscr("xp_s", [S, D])
    x1_s = dscr("x1_s", [S, D])
    dbg_d = {}

    x_a, out_a = x_d.ap(), out_d.ap()
    win_a = win_d.ap().rearrange("(kc p) n -> p kc n", p=128)
    wout_a = wout_d.ap().rearrange("(kc p) n -> p kc n", p=128)
    wup_a = wup_d.ap().rearrange("(kc p) n -> p kc n", p=128)
    wgate_a = wgate_d.ap().rearrange("(kc p) n -> p kc n", p=128)
    wdown_a = wdown_d.ap().rearrange("(fc p) n -> p fc n", p=128)
    wada_a = wada_d.ap().rearrange("(kc p) n -> p kc n", p=128)

    with ExitStack() as es:
        K = KB(nc, es)
        es.enter_context(nc.Block())
        pe, act, dve, pool, sp = K.pe, K.act, K.dve, K.pool, K.sp

        cmf, cmf_b = K.sb("cmf", [128, 512], F32)
        identf = cmf[:, 0:128]
        cmb, cmb_b = K.sb("cmb", [128, 512], BF16)
        identb, utri, ustrict = cmb[:, 0:128], cmb[:, 128:256], cmb[:, 256:384]
        trimf = cmf[:, 384:512]
        modT, modT_b = K.sb("modT", [128, 4, KC], F32)
        onesf, onesf_b = K.sb("onesf", [128, 128], F32)
        onesb, onesb_b = K.sb("onesb", [1, 128], BF16)
        psb = [K.ps("ps%d" % i, [128, 512], F32) for i in range(8)]
        wslots = [K.sb("wbuf%d" % i, [128, KC, 256], BF16) for i in range(3)]
        wctr = [0]

        K.dma(sp, cmf[:], cm_d.ap(), cmf_b, w=[cmf_b])
        K.op(dve, lambda e: e.tensor_copy(out=cmb[:], in_=cmf[:]), r=[cmf_b], w=[cmb_b])
        K.op(dve, lambda e: e.memset(onesf[:], 1.0), w=[onesf_b])
        K.op(dve, lambda e: e.memset(onesb[:], 1.0), w=[onesb_b])
        epsc, epsc_b = K.sb("epsc", [128, 2], F32)
        K.op(dve, lambda e: e.memset(epsc[:, 0:1], EPS), w=[epsc_b])
        K.op(dve, lambda e: e.memset(epsc[:, 1:2], float(B_DV) * EPS), w=[epsc_b])

        def pst(i):
            return psb[i][0], psb[i][1]

        def nextw():
            s = wslots[wctr[0] % 3]
            wctr[0] += 1
            return s

        def ln_stats(xt, xt_b, mv, mv_b, st, st_b, rstd, rstd_b):
            for c8 in range(8):
                K.op(dve, lambda e, c8=c8: e.bn_stats(out=st[:, c8, :], in_=xt[:, c8 * 512:(c8 + 1) * 512]), r=[xt_b], w=[st_b])
            K.op(dve, lambda e: e.bn_aggr(out=mv[:], in_=st[:].rearrange("p a b -> p (a b)")), r=[st_b], w=[mv_b])
            K.op(act, lambda e: e.activation(out=rstd[:], in_=mv[:, 1:2], func=AF.Sqrt, bias=epsc[:, 0:1]), r=[mv_b, epsc_b], w=[rstd_b])
            K.op(dve, lambda e: e.reciprocal(out=rstd[:], in_=rstd[:]), r=[rstd_b], w=[rstd_b])

        def make_hT(es2, src_a, t0, which, hT, hT_b, tiles):
            (xts, xh, xh_b, st, st_b, mv, mv_b, rstd, rstd_b) = tiles
            for ts in range(4):
                xt, xt_b = xts[ts % len(xts)]
                K.dma(act, xt[:], src_a[t0 + ts * 128:t0 + (ts + 1) * 128, :], xt_b, w=[xt_b])
                ln_stats(xt, xt_b, mv, mv_b, st, st_b, rstd, rstd_b)
                K.op(dve, lambda e: e.tensor_scalar(out=xh[:], in0=xt[:], scalar1=mv[:, 0:1], scalar2=rstd[:, 0:1], op0=ALU.subtract, op1=ALU.mult),
                     r=[xt_b, mv_b, rstd_b], w=[xh_b])
                for g in range(8):
                    pt, pt_b = pst(4 + g % 2)
                    ptb = pt[:].bitcast(BF16)
                    for j in range(4):
                        kc = g * 4 + j
                        K.op(pe, lambda e, kc=kc, j=j: e.transpose(out=ptb[:, j * 128:(j + 1) * 128], in_=xh[:, kc * 128:(kc + 1) * 128], identity=identb),
                             r=[xh_b, cmb_b], w=[pt_b], mark=(j == 3))
                    for j in range(4):
                        kc = g * 4 + j
                        if j % 2 == 0:
                            K.op(act, lambda e, kc=kc, j=j: e.activation(out=hT[:, kc, ts * 128:(ts + 1) * 128], in_=ptb[:, j * 128:(j + 1) * 128], func=AF.Identity,
                                                                          scale=modT[:, which + 1, kc:kc + 1], bias=modT[:, which, kc:kc + 1]),
                                 r=[pt_b, modT_b], w=[hT_b])
                        else:
                            K.op(dve, lambda e, kc=kc, j=j: e.tensor_scalar(out=hT[:, kc, ts * 128:(ts + 1) * 128], in0=ptb[:, j * 128:(j + 1) * 128],
                                                                             scalar1=modT[:, which + 1, kc:kc + 1], scalar2=modT[:, which, kc:kc + 1], op0=ALU.mult, op1=ALU.add),
                                 r=[pt_b, modT_b], w=[hT_b])

        hctr = [0]

        def alloc_ht_tiles(es2, xt_alias=None, xh_alias=None):
            if xt_alias is None:
                xts = [K.sb("xt0", [128, D], F32, es2)]
            else:
                xts = [(xt_alias, Buf("xt_alias"))]
            if xh_alias is None:
                xh, xh_b = K.sb("xh", [128, D], BF16, es2)
            else:
                xh, xh_b = xh_alias, Buf("xh_alias")
            hctr[0] += 1
            st, st_b = K.sb("st%d" % hctr[0], [128, 8, 6], F32, es2)
            mv, mv_b = K.sb("mv%d" % hctr[0], [128, 2], F32, es2)
            rstd, rstd_b = K.sb("rstd%d" % hctr[0], [128, 1], F32, es2)
            return (xts, xh, xh_b, st, st_b, mv, mv_b, rstd, rstd_b)

        def load_w(w_a, c0, w, k0=0, nk=KC, dup=False):
            wt, wt_b = nextw()
            K.dma(pool, wt[:, 0:nk, 0:w], w_a[:, k0:k0 + nk, c0:c0 + w], wt_b, w=[wt_b])
            if dup:
                K.dma(pool, wt[:, 0:nk, w:2 * w], w_a[:, k0:k0 + nk, c0:c0 + w], wt_b, w=[wt_b])
            return wt, wt_b

        pctr = [0]

        def proj_S(hT, hT_b, w_a, c0, ncols, evac, T=512, dup=False):
            for cs in range(c0, c0 + ncols, 256):
                w = min(256, c0 + ncols - cs)
                wt, wt_b = load_w(w_a, cs, w, dup=dup)
                weff = 2 * w if dup else w
                for j in range(0, weff, 128):
                    m = min(128, weff - j)
                    pt, pt_b = pst(pctr[0] % 4)
                    pctr[0] += 1
                    for kc in range(KC):
                        K.op(pe, lambda e, kc=kc: e.matmul(pt[0:m, 0:T], lhsT=wt[:, kc, j:j + m], rhs=hT[:, kc, 0:T], start=(kc == 0), stop=(kc == KC - 1)),
                             r=[wt_b, hT_b], w=[pt_b], mark=(kc == KC - 1))
                    evac(pt[0:m, 0:T], pt_b, cs - c0 + j, m)

        pctr2 = [0]

        def proj_M(hT, hT_b, w_a, c0, ncols, evac, T=512):
            if ncols % 512 == 0:
                for cs in range(c0, c0 + ncols, 512):
                    base = (pctr2[0] % 2) * 4
                    pctr2[0] += 1
                    for half in range(2):
                        wt, wt_b = nextw()
                        wtv = wt[:].rearrange("p a b -> p (a b)").rearrange("p (a b) -> p a b", b=512)
                        K.dma(pool, wtv[:, :, :], w_a[:, half * 16:(half + 1) * 16, cs:cs + 512], wt_b, w=[wt_b])
                        for ts in range(T // 128):
                            pt, pt_b = pst(base + ts)
                            for kl in range(16):
                                kc = half * 16 + kl
                                K.op(pe, lambda e, kc=kc, kl=kl: e.matmul(pt[:, :], lhsT=hT[:, kc, ts * 128:(ts + 1) * 128], rhs=wtv[:, kl, :], start=(kc == 0), stop=(kc == KC - 1)),
                                     r=[wt_b, hT_b], w=[pt_b], mark=(kl == 15))
                    for ts in range(T // 128):
                        pt, pt_b = pst(base + ts)
                        evac(pt[:, :], pt_b, ts, cs - c0, 512)
                return
            for cs in range(c0, c0 + ncols, 256):
                w = min(256, c0 + ncols - cs)
                wt, wt_b = load_w(w_a, cs, w)
                for ts in range(T // 128):
                    pt, pt_b = pst(pctr[0] % 4)
                    pctr[0] += 1
                    for kc in range(KC):
                        K.op(pe, lambda e, kc=kc: e.matmul(pt[:, 0:w], lhsT=hT[:, kc, ts * 128:(ts + 1) * 128], rhs=wt[:, kc, 0:w], start=(kc == 0), stop=(kc == KC - 1)),
                             r=[wt_b, hT_b], w=[pt_b], mark=(kc == KC - 1))
                    evac(pt[:, 0:w], pt_b, ts, cs - c0, w)

        ectr = [0]

        def evac_copy(dst_fn, dst_b, scale=None):
            def f(ps_ap, ps_b, *a):
                d = dst_fn(*a)
                ectr[0] += 1
                if ectr[0] % 2:
                    if scale is None:
                        K.op(act, lambda e: e.copy(out=d, in_=ps_ap), r=[ps_b], w=[dst_b])
                    else:
                        K.op(act, lambda e: e.activation(out=d, in_=ps_ap, func=AF.Copy, scale=scale), r=[ps_b], w=[dst_b])
                else:
                    if scale is None:
                        K.op(dve, lambda e: e.tensor_copy(out=d, in_=ps_ap), r=[ps_b], w=[dst_b])
                    else:
                        K.op(dve, lambda e: e.tensor_scalar(out=d, in0=ps_ap, scalar1=scale, scalar2=None, op0=ALU.mult), r=[ps_b], w=[dst_b])
            return f

        actr = [0]

        def adaln_gen(es2, cb_list, ldq=None):
            ldq = ldq or pool
            actr[0] += 1
            tg = "A%d" % actr[0]
            cTt, cT_b = K.sb("cTt" + tg, [128, KC], F32, es2)
            crep, crep_b = K.sb("crep" + tg, [128, KC, 128], BF16, es2)
            was = [K.sb("wa%s%d" % (tg, i), [128, 16, 512], BF16, es2) for i in range(2)]
            bat, bat_b = K.sb("bat" + tg, [1, 512], BF16, es2)
            stg = [K.sb("stg%s%d" % (tg, i), [128, 512], F32, es2) for i in range(2)]
            junk, junk_b = K.sb("junkA" + tg, [128, 128], F32, es2)
            K.dma(sp, cTt[:], cT_d.ap(), cT_b, w=[cT_b])
            K.op(act, lambda e: e.activation(out=cTt[:], in_=cTt[:], func=AF.Silu), r=[cT_b], w=[cT_b])
            for kc in range(KC):
                K.op(dve, lambda e, kc=kc: e.tensor_scalar(out=crep[:, kc, :], in0=onesf[:], scalar1=cTt[:, kc:kc + 1], scalar2=None, op0=ALU.mult),
                     r=[onesf_b, cT_b], w=[crep_b])
            wi = 0
            for cb in cb_list:
                which, cbl = cb // 8, cb % 8
                pt, pt_b = pst(cb % 4)
                K.dma(pool, bat[:], bada_d.ap()[0:1, cb * 512:(cb + 1) * 512], bat_b, w=[bat_b])
                K.op(pe, lambda e: e.matmul(pt[:, :], lhsT=onesb[0:1, :], rhs=bat[0:1, :], start=True, stop=False), r=[onesb_b, bat_b], w=[pt_b], mark=False)
                for kg in range(2):
                    wa, wa_b = was[wi % 2]
                    wi += 1
                    K.dma(pool, wa[:], wada_a[:, kg * 16:(kg + 1) * 16, cb * 512:(cb + 1) * 512], wa_b, w=[wa_b])
                    for kl in range(16):
                        kc = kg * 16 + kl
                        K.op(pe, lambda e, kc=kc, kl=kl: e.matmul(pt[:, :], lhsT=crep[:, kc, :], rhs=wa[:, kl, :], start=False, stop=(kc == KC - 1)),
                             r=[crep_b, wa_b], w=[pt_b], mark=(kl == 15))
                if which in (2, 5):
                    sg, sg_b = stg[cb % 2]
                    K.op(act, lambda e: e.copy(out=sg[:], in_=pt[:, :]), r=[pt_b], w=[sg_b])
                    dst = (g1b_s if which == 2 else g2b_s).ap()[:, cbl * 512:(cbl + 1) * 512]
                    K.dma(sp, dst, sg[:], sg_b, r=[sg_b])
                else:
                    slot = {0: 0, 1: 1, 3: 2, 4: 3}[which]
                    for c in range(4):
                        col = cbl * 4 + c
                        K.op(dve, lambda e, c=c, col=col: e.scalar_tensor_tensor(out=junk[:], in0=pt[:, c * 128:(c + 1) * 128], scalar=1.0, in1=identf, op0=ALU.mult, op1=ALU.mult,
                                                                                   accum_out=modT[:, slot, col:col + 1]),
                             r=[pt_b, cmf_b], w=[junk_b, modT_b])
                    if cbl == 7 and which in (1, 4):
                        K.op(dve, lambda e, slot=slot: e.tensor_scalar(out=modT[:, slot, :], in0=modT[:, slot, :], scalar1=1.0, scalar2=None, op0=ALU.add), r=[modT_b], w=[modT_b])
                yield

        K.op(dve, lambda e: e.memset(modT[:], 0.0), w=[modT_b])
        with ExitStack() as es2:
            for _ in adaln_gen(es2, list(range(16))):
                pass
            K.barrier()

        with ExitStack() as esB:
            SW = max(S, 2048)
            kT, kT_b = K.sb("kT", [128, S], BF16, esB)
            kiT, kiT_b = K.sb("kiT", [128, S], BF16, esB)
            vaug, vaug_b = K.sb("vaug", [128, NQ, 132], BF16, esB)
            wabs, wabs_b = K.sb("wabs", [128, NQ, 32], F32, esB)
            wsgn, wsgn_b = K.sb("wsgn", [128, NQ, 32], F32, esB)
            K.op(dve, lambda e: e.memset(vaug[:], 1.0), w=[vaug_b])

            with ExitStack() as es2:
                hTs = [K.sb("hT_B%d" % i, [128, KC, 512], BF16, es2) for i in range(2)]
                qT, qT_b = K.sb("qT", [128, 4, A_HEADS, 128], BF16, es2)
                qiT, qiT_b = K.sb("qiT", [128, 4, 16, 128], BF16, es2)
                wraw, wraw_b = K.sb("wraw", [128, 4, 32], F32, es2)
                xtB = [K.sb("xtB%d" % i, [128, D], F32, es2) for i in range(1)]
                xh4 = [K.sb("xh4_%d" % i, [128, D], BF16, es2) for i in range(2)]
                stB, stB_b = K.sb("stB", [128, 8, 6], F32, es2)
                mvB, mvB_b = K.sb("mvB", [128, 2], F32, es2)
                rsB, rsB_b = K.sb("rsB", [128, 1], F32, es2)

                def ln_part(t0, tss=(0, 1, 2, 3)):
                    for ts in tss:
                        xt, xt_b = xtB[ts % len(xtB)]
                        xh, xh_b = xh4[ts % 2]
                        K.dma(act, xt[:], x_a[t0 + ts * 128:t0 + (ts + 1) * 128, :], xt_b, w=[xt_b])
                        ln_stats(xt, xt_b, mvB, mvB_b, stB, stB_b, rsB, rsB_b)
                        K.op(dve, lambda e: e.tensor_scalar(out=xh[:], in0=xt[:], scalar1=mvB[:, 0:1], scalar2=rsB[:, 0:1], op0=ALU.subtract, op1=ALU.mult),
                             r=[xt_b, mvB_b, rsB_b], w=[xh_b])

                def tr_part(hT, hT_b, tss=(0, 1, 2, 3)):
                    for ts in tss:
                        xh, xh_b = xh4[ts % 2]
                        for g in range(8):
                            pt, pt_b = pst(4 + g % 2)
                            ptb = pt[:].bitcast(BF16)
                            for j in range(4):
                                kc = g * 4 + j
                                K.op(pe, lambda e, kc=kc, j=j: e.transpose(out=ptb[:, j * 128:(j + 1) * 128], in_=xh[:, kc * 128:(kc + 1) * 128], identity=identb),
                                     r=[xh_b, cmb_b], w=[pt_b], mark=(j == 3))
                            for j in range(4):
                                kc = g * 4 + j
                                if j % 2 == 0:
                                    K.op(act, lambda e, kc=kc, j=j: e.activation(out=hT[:, kc, ts * 128:(ts + 1) * 128], in_=ptb[:, j * 128:(j + 1) * 128], func=AF.Identity,
                                                                                  scale=modT[:, 1, kc:kc + 1], bias=modT[:, 0, kc:kc + 1]), r=[pt_b, modT_b], w=[hT_b])
                                else:
                                    K.op(dve, lambda e, kc=kc, j=j: e.tensor_scalar(out=hT[:, kc, ts * 128:(ts + 1) * 128], in0=ptb[:, j * 128:(j + 1) * 128],
                                                                                     scalar1=modT[:, 1, kc:kc + 1], scalar2=modT[:, 0, kc:kc + 1], op0=ALU.mult, op1=ALU.add),
                                         r=[pt_b, modT_b], w=[hT_b])

                def evq(dst, dst_b, scale):
                    def f(ps_ap, ps_b, col, m):
                        d = dst[:, :, col // 128, :]
                        src = ps_ap.rearrange("p (a b) -> p a b", b=128)
                        ectr[0] += 1
                        if ectr[0] % 2:
                            K.op(act, lambda e: e.activation(out=d, in_=src, func=AF.Copy, scale=scale), r=[ps_b], w=[dst_b])
                        else:
                            K.op(dve, lambda e: e.tensor_scalar(out=d, in0=src, scalar1=scale, scalar2=None, op0=ALU.mult), r=[ps_b], w=[dst_b])
                    return f

                for ts_ in range(4):
                    ln_part(0, (ts_,))
                    tr_part(*hTs[0], tss=(ts_,))
                for tt in range(NT):
                    t0 = tt * 512
                    hT, hT_b = hTs[tt % 2]
                    K.dma(sp, hT_s.ap()[tt], hT[:].rearrange("p a b -> p (a b)"), hT_b, r=[hT_b])
                    proj_S(hT, hT_b, win_a, O_AK, 128, evac_copy(lambda col, m: kT[:, t0:t0 + 512], kT_b))
                    proj_S(hT, hT_b, win_a, O_IK, 64, evac_copy(lambda col, m: kiT[:, t0:t0 + 512], kiT_b), dup=True)
                    proj_M(hT, hT_b, win_a, O_AV, 128, evac_copy(lambda ts, col, w: vaug[:, tt * 4 + ts, 0:128], vaug_b))
                    proj_M(hT, hT_b, win_a, O_IW, 32, evac_copy(lambda ts, col, w: wraw[:, ts, :], wraw_b))
                    K.op(act, lambda e: e.activation(out=wabs[:, tt * 4:tt * 4 + 4, :], in_=wraw[:], func=AF.Abs, scale=I_HEADS ** -0.5), r=[wraw_b], w=[wabs_b])
                    K.op(act, lambda e: e.activation(out=wsgn[:, tt * 4:tt * 4 + 4, :], in_=wraw[:], func=AF.Sign), r=[wraw_b], w=[wsgn_b])
                    proj_S(hT, hT_b, win_a, O_IQ, 2048, evq(qiT, qiT_b, I_D ** -0.5))
                    K.dma(sp, qi_s.ap()[tt], qiT[:].rearrange("p a b c -> p (a b c)"), qiT_b, r=[qiT_b])
                    if tt + 1 < NT:
                        ln_part(t0 + 512, (0, 1))
                    proj_S(hT, hT_b, win_a, O_AQ, 2048, evq(qT, qT_b, A_DH ** -0.5))
                    K.dma(sp, q_s.ap()[tt], qT[:].rearrange("p a b c -> p (a b c)"), qT_b, r=[qT_b])
                    if tt + 1 < NT:
                        nh = hTs[(tt + 1) % 2]
                        tr_part(*nh, tss=(0,))
                        ln_part(t0 + 512, (2,))
                        tr_part(*nh, tss=(1,))
                        ln_part(t0 + 512, (3,))
                        tr_part(*nh, tss=(2, 3))
                K.barrier()

            biasM, biasM_b = K.sb("biasM", [128, A_HEADS, 2, 128], BF16, esB)
            b15, b15_b = K.sb("b15", [128, A_HEADS], F32, esB)
            es3 = ExitStack()
            biasF, biasF_b = K.sb("biasF", [128, A_HEADS, 2, 128], F32, es3)
            t5t, t5_b = K.sb("t5t", [32, A_HEADS], F32, es3)
            oht, oh_b = K.sb("oht", [32, TL], F32, es3)
            gsb, gsb_b = K.sb("gsb", [A_HEADS, TL], F32, es3)
            gsc_b, grep_b = Buf("gsc"), Buf("grep")

            K.dma(sp, t5t[:], t5_d.ap(), t5_b, w=[t5_b])
            K.dma(sp, oht[:], oh_d.ap(), oh_b, w=[oh_b])
            K.dma(sp, b15[:], t5_d.ap()[15:16, :].partition_broadcast(128).rearrange("p a h -> p (a h)"), b15_b, w=[b15_b])
            pt, pt_b = pst(7)
            K.op(pe, lambda e: e.matmul(pt[0:A_HEADS, 0:TL], lhsT=t5t[:, :], rhs=oht[:, :], start=True, stop=True), r=[t5_b, oh_b], w=[pt_b])
            K.op(act, lambda e: e.copy(out=gsb[:], in_=pt[0:A_HEADS, 0:TL]), r=[pt_b], w=[gsb_b])
            K.dma(sp, gsc_s.ap(), gsb[:], gsb_b, r=[gsb_b], w=[gsc_b])
            K.dma(sp, grep_s.ap(), gsc_s.ap().rearrange("h l -> (h l)").partition_broadcast(128), gsb_b, r=[gsc_b], w=[grep_b])
            for hh in range(A_HEADS):
                for a_ in range(2):
                    src = RawAP(grep_s, hh * TL + 127 + 128 * (1 - a_), [[A_HEADS * TL - 1, 128], [1, 128]])
                    K.dma(sp, biasF[:, hh, a_, :], src, biasF_b, r=[grep_b], w=[biasF_b])
            K.op(dve, lambda e: e.tensor_copy(out=biasM[:], in_=biasF[:]), r=[biasF_b], w=[biasM_b])
            K.barrier()
            es3.close()

            es2 = esB
            acc, acc_b = K.sb("acc", [128, SW], F32, es2)
            work, work_b = K.sb("work", [128, SW], F32, es2)
            rts = [K.sb("rt%d" % i, [128, 512], F32, es2) for i in range(4)]
            m8, m8_b = K.sb("m8", [128, 8], F32, es2)
            thr, thr_b = K.sb("thr", [128, 1], F32, es2)
            madd, madd_b = K.sb("madd", [128, S], BF16, es2)
            maskTs = [K.sb("maskT%d" % i, [128, NQ, 128], BF16, es2) for i in range(2)]
            pts = [K.sb("pt%d" % i, [128, 512], BF16, es2) for i in range(3)]
            oa, oa_b = K.sb("oa", [128, 2048], BF16, es2)
            oraw, oraw_b = K.sb("oraw", [128, A_HEADS, 132], F32, es2)
            rinv16, rinv16_b = K.sb("rinv16", [128, A_HEADS], F32, es2)
            qTqs = [K.sb("qTq%d" % i, [128, A_HEADS, 128], BF16, es2) for i in range(2)]
            qiTqs = [K.sb("qiTq%d" % i, [128, 16, 128], BF16, es2) for i in range(2)]

            def load_q(qt):
                tt, ts = qt // 4, qt % 4
                qq, qq_b = qTqs[qt % 2]
                qi, qi_b = qiTqs[qt % 2]
                K.dma(act, qi[:].rearrange("p a b -> p (a b)"), qi_s.ap()[tt][:, ts * 2048:(ts + 1) * 2048], qi_b, w=[qi_b])
                K.dma(act, qq[:].rearrange("p a b -> p (a b)"), q_s.ap()[tt][:, ts * 2048:(ts + 1) * 2048], qq_b, w=[qq_b])

            def S1(qt):
                qiT, qiT_b = qiTqs[qt % 2]
                ts = qt
                sadm = 128 * (qt + 1)
                nkb = (sadm + 511) // 512
                tq = slice(0, 128)
                ri = 0
                for h in range(I_HEADS):
                    rows = slice((h % 2) * 64, (h % 2) * 64 + 64)
                    for kb in range(nkb):
                        kw = min(512, sadm - kb * 512)
                        ks_ = slice(kb * 512, kb * 512 + kw)
                        pt, pt_b = pst(ri % 4)
                        rt, rt_b = rts[ri % 4]
                        ri += 1
                        K.op(pe, lambda e: e.matmul(pt[:, 0:kw], lhsT=qiT[rows, h // 2, tq], rhs=kiT[rows, ks_], start=True, stop=True), r=[qiT_b, kiT_b], w=[pt_b])
                        K.op(act, lambda e: e.activation(out=rt[:, 0:kw], in_=pt[:, 0:kw], func=AF.Relu, scale=wabs[:, ts, h:h + 1]), r=[pt_b, wabs_b], w=[rt_b])
                        if h == 0:
                            K.op(dve, lambda e: e.tensor_scalar(out=acc[:, ks_], in0=rt[:, 0:kw], scalar1=wsgn[:, ts, h:h + 1], scalar2=None, op0=ALU.mult), r=[rt_b, wsgn_b], w=[acc_b])
                        else:
                            K.op(dve, lambda e: e.scalar_tensor_tensor(out=acc[:, ks_], in0=rt[:, 0:kw], scalar=wsgn[:, ts, h:h + 1], in1=acc[:, ks_], op0=ALU.mult, op1=ALU.add),
                                 r=[rt_b, wsgn_b, acc_b], w=[acc_b])
                K.op(dve, lambda e: e.memset(acc[0:64, sadm - 64:sadm], -BIG), w=[acc_b])
                if sadm > TOPK:
                    K.op(act, lambda e: e.copy(out=work[:, 0:sadm], in_=acc[:, 0:sadm]), r=[acc_b], w=[work_b])
                    nr = TOPK // 8
                    for r_ in range(nr):
                        K.op(dve, lambda e: e.max(out=m8[:], in_=work[:, 0:sadm]), r=[work_b], w=[m8_b])
                        if r_ < nr - 1:
                            K.op(dve, lambda e: e.match_replace(out=work[:, 0:sadm], in_to_replace=m8[:], in_values=work[:, 0:sadm], imm_value=-BIG), r=[m8_b, work_b], w=[work_b])
                    K.op(dve, lambda e: e.tensor_copy(out=thr[:], in_=m8[:, 7:8]), r=[m8_b], w=[thr_b])
                else:
                    K.op(dve, lambda e: e.memset(thr[:], -0.5 * BIG), w=[thr_b])
                K.op(dve, lambda e: e.tensor_scalar(out=madd[:, 0:sadm], in0=acc[:, 0:sadm], scalar1=thr[:, 0:1], scalar2=NEG, op0=ALU.is_lt, op1=ALU.mult), r=[acc_b, thr_b], w=[madd_b])

            def S2(qt):
                mT, mT_b = maskTs[qt % 2]
                for g in range((qt + 4) // 4):
                    pt, pt_b = pst(4 + g % 2)
                    ptb = pt[:].bitcast(BF16)
                    n = min(4, qt + 1 - g * 4)
                    for j in range(n):
                        ks = g * 4 + j
                        K.op(pe, lambda e, ks=ks, j=j: e.transpose(out=ptb[:, j * 128:(j + 1) * 128], in_=madd[:, ks * 128:(ks + 1) * 128], identity=identb),
                             r=[madd_b, cmb_b], w=[pt_b], mark=(j == n - 1))
                    K.op(act, lambda e, g=g, n=n: e.copy(out=mT[:, g * 4:g * 4 + n, :], in_=ptb[:, 0:n * 128].rearrange("p (a b) -> p a b", b=128)), r=[pt_b], w=[mT_b])

            def S3(qt):
                qT, qT_b = qTqs[qt % 2]
                tq = slice(0, 128)
                mT, mT_b = maskTs[qt % 2]
                nfar = max(0, qt - 1)
                groups = [(list(range(g, min(g + 4, nfar))), False) for g in range(0, nfar, 4)]
                groups.append(([ks for ks in (qt - 1, qt) if ks >= 0], True))
                pi = 0
                for h in range(A_HEADS):
                    po, po_b = pst(6 + h % 2)
                    for gi, (grp, near) in enumerate(groups):
                        n = len(grp)
                        ks0 = grp[0]
                        ps_, ps_b = pst(4 + pi % 2)
                        pT_, pT_b = pts[pi % 3]
                        pi += 1
                        K.op(pe, lambda e: e.matmul(ps_[:, 0:n * 128], lhsT=identb, rhs=mT[:, ks0:ks0 + n, :].rearrange("p a b -> p (a b)"), start=True, stop=False),
                             r=[cmb_b, mT_b], w=[ps_b], mark=False)
                        if near:
                            brhs = biasM[:, h, 2 - n:2, :].rearrange("p a b -> p (a b)")
                            K.op(pe, lambda e: e.matmul(ps_[:, 0:n * 128], lhsT=identb, rhs=brhs, start=False, stop=False), r=[cmb_b, biasM_b], w=[ps_b], mark=False)
                        for j, ks in enumerate(grp):
                            K.op(pe, lambda e, j=j, ks=ks: e.matmul(ps_[:, j * 128:(j + 1) * 128], lhsT=kT[:, ks * 128:(ks + 1) * 128], rhs=qT[:, h, tq], start=False, stop=(j == n - 1)),
                                 r=[kT_b, qT_b], w=[ps_b], mark=(j == n - 1))
                        if near:
                            K.op(act, lambda e: e.activation(out=pT_[:, 0:n * 128], in_=ps_[:, 0:n * 128], func=AF.Exp), r=[ps_b], w=[pT_b])
                        else:
                            K.op(act, lambda e: e.activation(out=pT_[:, 0:n * 128], in_=ps_[:, 0:n * 128], func=AF.Exp, bias=b15[:, h:h + 1]), r=[ps_b, b15_b], w=[pT_b])
                        for j, ks in enumerate(grp):
                            last = (gi == len(groups) - 1) and (j == n - 1)
                            K.op(pe, lambda e, j=j, ks=ks: e.matmul(po[:, 0:129], lhsT=pT_[:, j * 128:(j + 1) * 128], rhs=vaug[:, ks, 0:129], start=(gi == 0 and j == 0), stop=last),
                                 r=[pT_b, vaug_b], w=[po_b], mark=(j == n - 1))
                    K.op(act, lambda e: e.copy(out=oraw[:, h, 0:129], in_=po[:, 0:129]), r=[po_b], w=[oraw_b])
                K.op(dve, lambda e: e.reciprocal(out=rinv16[:], in_=oraw[:, :, 128]), r=[oraw_b], w=[rinv16_b])
                for h in range(A_HEADS):
                    K.op(dve, lambda e: e.tensor_scalar(out=oa[:, h * 128:(h + 1) * 128], in0=oraw[:, h, 0:128], scalar1=rinv16[:, h:h + 1], scalar2=None, op0=ALU.mult),
                         r=[oraw_b, rinv16_b], w=[oa_b])
                K.dma(sp, o_s.ap()[qt * 128:(qt + 1) * 128, 0:2048], oa[:], oa_b, r=[oa_b])

            with ExitStack() as esS:
                side = adaln_gen(esS, list(range(16, 40)), ldq=pool)
                load_q(0)
                if NQ > 1:
                    load_q(1)
                S1(0)
                S2(0)
                for qt in range(NQ):
                    if qt + 1 < NQ:
                        S1(qt + 1)
                    S3(qt)
                    if qt + 2 < NQ:
                        load_q(qt + 2)
                    if qt + 1 < NQ:
                        S2(qt + 1)
                    for _ in range(2 if qt % 2 == 0 else 1):
                        next(side, None)
                for _ in side:
                    pass
                K.barrier()

        with ExitStack() as es2:
            hT, hT_b = K.sb("hT_C", [128, KC, 512], BF16, es2)
            qTg, qTg_b = K.sb("qTg", [128, 8, 512], BF16, es2)
            kTg, kTg_b = K.sb("kTg", [128, 8, 512], BF16, es2)
            ktok, ktok_b = K.sb("ktok", [128, 4, 1024], BF16, es2)
            vtok, vtok_b = K.sb("vtok", [128, 4, 2048], BF16, es2)
            Gt, Gt_b = K.sb("Gt", [128, 4, 2048], BF16, es2)
            lrT, lrT_b = K.sb("lrT", [GR, 512], BF16, es2)
            wg2, wg2_b = K.sb("wg2", [GR, 1024], BF16, es2)
            bg2, bg2_b = K.sb("bg2", [1, 1024], BF16, es2)
            gnb, gnb_b = K.sb("gnb", [128, B_DV], F32, es2)
            Sf, Sf_b = K.sb("Sf", [128, 8, 512], F32, es2)
            Sb, Sb_b = K.sb("Sb", [128, 8, 512], BF16, es2)
            e1, e1_b = K.sb("e1", [128, 1024], F32, es2)
            spl, spl_b = K.sb("spl", [128, 1024], BF16, es2)
            EbT, EbT_b = K.sb("EbT", [128, 8, 128], F32, es2)
            EnbT, EnbT_b = K.sb("EnbT", [128, 8, 128], BF16, es2)
            qe, qe_b = K.sb("qe", [128, 8, 128], BF16, es2)
            qeP, qeP_b = K.sb("qeP", [128, 8, 2, 128], BF16, es2)
            ke, ke_b = K.sb("ke", [128, 8, 128], BF16, es2)
            Ec, Ec_b = e1, e1_b
            sq, sq_b = e1[:, 0:256].bitcast(BF16), e1_b
            kd, kd_b = K.sb("kd", [128, 1024], BF16, es2)
            Am4, _ = K.sb("Am4", [128, 4, 128], BF16, es2)
            Am_bs = [Buf("Am%d" % i) for i in range(4)]
            pa_bs = [Buf("pa%d" % i) for i in range(4)]
            Sf_bs = [Buf("Sf%d" % i) for i in range(8)]
            Sb_bs = [Buf("Sb%d" % i) for i in range(8)]
            ss_bs = [Buf("ss%d" % i) for i in range(4)]
            rs_bs = [Buf("rs%d" % i) for i in range(4)]
            ssq4, _ = K.sb("ssq4", [128, 4], F32, es2)
            rs4, _ = K.sb("rs4", [128, 4], F32, es2)
            uctr = [0]

            ssq, ssq_b = K.sb("ssq", [128, 1], F32, es2)
            rs2, rs2_b = K.sb("rs2", [128, 1], F32, es2)
            ob, ob_b = K.sb("ob", [128, 2048], BF16, es2)
            silt = [K.sb("silt%d" % i, [128, 256], F32, es2) for i in range(2)]

            es3 = ExitStack()
            K.dma(pool, wg2[:], wg2_d.ap(), wg2_b, w=[wg2_b])
            K.dma(pool, bg2[:], bg2_d.ap(), bg2_b, w=[bg2_b])
            K.dma(sp, gnb[:], gn_d.ap()[0:1, :].partition_broadcast(128).rearrange("p a h -> p (a h)"), gnb_b, w=[gnb_b])
            K.op(dve, lambda e: e.tensor_scalar(out=gnb[:], in0=gnb[:], scalar1=float(B_DV) ** 0.5, scalar2=None, op0=ALU.mult), r=[gnb_b], w=[gnb_b])
            K.op(dve, lambda e: e.memset(Sf[:], 0.0), w=[Sf_b])
            K.op(pool, lambda e: e.memset(Sb[:], 0.0), w=[Sb_b])
            for b_ in Sf_bs:
                b_.w = Sf_b.w
            for b_ in Sb_bs:
                b_.w = Sb_b.w
            K.op(pool, lambda e: e.memset(qeP[:], 0.0), w=[qeP_b])
            K.barrier()
            es3.close()

            sctr = [0]

            def evac_silu(ps_ap, ps_b, ts, col, w):
                for o_ in range(0, w, 256):
                    sl, sl_b = silt[sctr[0] % 2]
                    sctr[0] += 1
                    K.op(act, lambda e: e.activation(out=sl[:, 0:256], in_=ps_ap[:, o_:o_ + 256], func=AF.Silu), r=[ps_b], w=[sl_b])
                    hcol = (col + o_) % B_DV
                    K.op(pool, lambda e: e.tensor_tensor(out=Gt[:, ts, col + o_:col + o_ + 256], in0=sl[:, 0:256], in1=gnb[:, hcol:hcol + 256], op=ALU.mult), r=[sl_b, gnb_b], w=[Gt_b])

            for tt in range(NT):
                t0 = tt * 512
                K.dma(act, hT[:].rearrange("p a b -> p (a b)"), hT_s.ap()[tt], hT_b, w=[hT_b])
                proj_S(hT, hT_b, win_a, O_BQ, 1024, evac_copy(lambda col, m: qTg[:, col // 128, :], qTg_b, scale=B_DK ** -0.5))
                proj_S(hT, hT_b, win_a, O_BK, 1024, evac_copy(lambda col, m: kTg[:, col // 128, :], kTg_b))
                proj_M(hT, hT_b, win_a, O_BK, 1024, evac_copy(lambda ts, col, w: ktok[:, ts, col:col + w], ktok_b))
                proj_M(hT, hT_b, win_a, O_BV, 2048, evac_copy(lambda ts, col, w: vtok[:, ts, col:col + w], vtok_b))
                proj_S(hT, hT_b, win_a, O_BG, GR, evac_copy(lambda col, m: lrT[:, :], lrT_b))
                proj_M(hT, hT_b, win_a, O_BR, 2048, evac_silu)
                for ts in range(4):
                    tq = slice(ts * 128, (ts + 1) * 128)
                    for hf in range(2):
                        pt, pt_b = pst(4 + hf)
                        cs_ = slice(hf * 512, (hf + 1) * 512)
                        K.op(pe, lambda e: e.matmul(pt[:, :], lhsT=onesb[0:1, :], rhs=bg2[0:1, cs_], start=True, stop=False), r=[onesb_b, bg2_b], w=[pt_b], mark=False)
                        K.op(pe, lambda e: e.matmul(pt[:, :], lhsT=lrT[:, tq], rhs=wg2[:, cs_], start=False, stop=True), r=[lrT_b, wg2_b], w=[pt_b])
                        K.op(act, lambda e: e.activation(out=e1[:, cs_], in_=pt[:, :], func=AF.Exp, scale=-1.0), r=[pt_b], w=[e1_b])
                    K.op(act, lambda e: e.activation(out=spl[:], in_=e1[:], func=AF.Ln, bias=1.0), r=[e1_b], w=[spl_b])
                    for hf in range(2):
                        pt, pt_b = pst(4 + hf)
                        for j in range(4):
                            c8 = hf * 4 + j
                            K.op(pe, lambda e, c8=c8, j=j: e.matmul(pt[:, j * 128:(j + 1) * 128], lhsT=spl[:, c8 * 128:(c8 + 1) * 128], rhs=utri, start=True, stop=True),
                                 r=[spl_b, cmb_b], w=[pt_b], mark=(j == 3))
                        bsl = slice(hf * 4, hf * 4 + 4)
                        K.op(act, lambda e: e.activation(out=EbT[:, bsl, :], in_=pt[:, :].rearrange("p (a b) -> p a b", b=128), func=AF.Exp), r=[pt_b], w=[EbT_b])
                        K.op(act, lambda e: e.activation(out=EnbT[:, bsl, :], in_=pt[:, :].rearrange("p (a b) -> p a b", b=128), func=AF.Exp, scale=-1.0), r=[pt_b], w=[EnbT_b])
                    K.op(dve, lambda e: e.tensor_tensor(out=qe[:], in0=qTg[:, :, tq], in1=EbT[:], op=ALU.mult), r=[qTg_b, EbT_b], w=[qe_b])
                    K.op(pool, lambda e: e.tensor_copy(out=qeP[:, :, 0, 0:64], in_=qe[:, :, 0:64]), r=[qe_b], w=[qeP_b])
                    K.op(pool, lambda e: e.tensor_copy(out=qeP[:, :, 1, 64:128], in_=qe[:, :, 64:128]), r=[qe_b], w=[qeP_b])
                    K.op(dve, lambda e: e.tensor_tensor(out=ke[:], in0=kTg[:, :, tq], in1=EnbT[:], op=ALU.mult), r=[kTg_b, EnbT_b], w=[ke_b])
                    for hf in range(2):
                        pt, pt_b = pst(4 + hf)
                        cs_ = slice(hf * 512, (hf + 1) * 512)
                        K.op(pe, lambda e: e.matmul(pt[:, :], lhsT=ustrict, rhs=spl[:, cs_], start=True, stop=True), r=[spl_b, cmb_b], w=[pt_b])
                        K.op(act, lambda e: e.activation(out=Ec[:, cs_], in_=pt[:, :], func=AF.Exp), r=[pt_b], w=[Ec_b])
                    K.op(dve, lambda e: e.tensor_tensor(out=kd[:], in0=ktok[:, ts, :], in1=Ec[:], op=ALU.mult), r=[ktok_b, Ec_b], w=[kd_b])
                    pa, _ = pst(6)
                    for hh in range(B_HEADS):
                        for c2 in range(2):
                            c8 = hh * 2 + c2
                            K.op(pe, lambda e, c8=c8, c2=c2: e.matmul(pa[:, hh * 128:(hh + 1) * 128], lhsT=ke[:, c8, :], rhs=qe[:, c8, :], start=(c2 == 0), stop=(c2 == 1)),
                                 r=[ke_b, qe_b], w=[pa_bs[hh]], mark=(c2 == 1))
                        K.op(dve, lambda e: e.tensor_tensor(out=Am4[:, hh, :], in0=pa[:, hh * 128:(hh + 1) * 128], in1=trimf, op=ALU.mult), r=[pa_bs[hh], cmf_b], w=[Am_bs[hh]])
                    for hh in range(B_HEADS):
                        vs = slice(hh * B_DV, (hh + 1) * B_DV)
                        po, po_b = pst(hh)
                        K.op(pe, lambda e: e.matmul(po[:, :], lhsT=Am4[:, hh, :], rhs=vtok[:, ts, vs], start=True, stop=False), r=[Am_bs[hh], vtok_b], w=[po_b], mark=False)
                        for c2 in range(2):
                            c8 = hh * 2 + c2
                            K.op(pe, lambda e, c8=c8: e.matmul(po[:, :], lhsT=qeP[:, c8, 0, :], rhs=Sb[:, c8, :], start=False, stop=False),
                                 r=[qeP_b, Sb_bs[c8]], w=[po_b], mark=(c2 == 1))
                    for ck in range(2):
                        rows = slice(ck * 64, ck * 64 + 64)
                        if ck == 1:
                            for hh in range(B_HEADS):
                                po, po_b = pst(hh)
                                for c2 in range(2):
                                    c8 = hh * 2 + c2
                                    K.op(pe, lambda e, c8=c8: e.matmul(po[:, :], lhsT=qeP[:, c8, 1, :], rhs=Sb[:, c8, :], start=False, stop=(c2 == 1)),
                                         r=[qeP_b, Sb_bs[c8]], w=[po_b], mark=(c2 == 1))
                        for hh in range(B_HEADS):
                            vs = slice(hh * B_DV, (hh + 1) * B_DV)
                            for c2 in range(2):
                                c8 = hh * 2 + c2
                                pu, pu_b = pst((4, 5, 7)[uctr[0] % 3])
                                uctr[0] += 1
                                K.op(pe, lambda e, c8=c8: e.matmul(pu[:, :], lhsT=kd[rows, c8 * 128:(c8 + 1) * 128], rhs=vtok[rows, ts, vs], start=True, stop=True),
                                     r=[kd_b, vtok_b], w=[pu_b])
                                K.op(dve, lambda e, c8=c8, ck=ck: e.scalar_tensor_tensor(out=Sf[:, c8, :], in0=Sf[:, c8, :], scalar=EbT[:, c8, ck * 64 + 63:ck * 64 + 64], in1=pu[:, :],
                                                                                          op0=ALU.mult, op1=ALU.add), r=[Sf_bs[c8], EbT_b, pu_b], w=[Sf_bs[c8]])
                                K.op(act, lambda e, c8=c8: e.copy(out=Sb[:, c8, :], in_=Sf[:, c8, :]), r=[Sf_bs[c8]], w=[Sb_bs[c8]])
                    for hh in range(B_HEADS):
                        vs = slice(hh * B_DV, (hh + 1) * B_DV)
                        po, po_b = pst(hh)
                        K.op(act, lambda e: e.activation(out=sq[:], in_=po[:, :], func=AF.Square, accum_out=ssq4[:, hh:hh + 1]), r=[po_b], w=[sq_b, ss_bs[hh]])
                        K.op(act, lambda e: e.activation(out=rs4[:, hh:hh + 1], in_=ssq4[:, hh:hh + 1], func=AF.Sqrt, bias=epsc[:, 1:2]), r=[ss_bs[hh], epsc_b], w=[rs_bs[hh]])
                        K.op(dve, lambda e: e.reciprocal(out=rs4[:, hh:hh + 1], in_=rs4[:, hh:hh + 1]), r=[rs_bs[hh]], w=[rs_bs[hh]])
                        K.op(dve, lambda e: e.scalar_tensor_tensor(out=ob[:, vs], in0=po[:, :], scalar=rs4[:, hh:hh + 1], in1=Gt[:, ts, vs], op0=ALU.mult, op1=ALU.mult),
                             r=[po_b, rs_bs[hh], Gt_b], w=[ob_b])
                    r0 = t0 + ts * 128
                    K.dma(sp, o_s.ap()[r0:r0 + 128, 2048:4096], ob[:], ob_b, r=[ob_b])
            K.barrier()

        with ExitStack() as es2:
            oTs = [K.sb("oT%d" % i, [128, KC, 512], BF16, es2) for i in range(2)]
            ots = [K.sb("ot%d" % i, [128, D], BF16, es2) for i in range(2)]
            g1b, g1b_b = K.sb("g1b", [128, D], F32, es2)
            xbs = [K.sb("xb%d" % i, [128, 512], F32, es2) for i in range(3)]
            tms = [K.sb("tm%d" % i, [128, 512], F32, es2) for i in range(3)]
            K.dma(sp, g1b[:], g1b_s.ap(), g1b_b, w=[g1b_b])
            dctr = [0]

            def prepD(tt):
                t0 = tt * 512
                oT, oT_b = oTs[tt % 2]
                for ts in range(4):
                    ot, ot_b = ots[ts % 2]
                    K.dma(act, ot[:], o_s.ap()[t0 + ts * 128:t0 + (ts + 1) * 128, :], ot_b, w=[ot_b])
                    for g in range(8):
                        pt, pt_b = pst(4 + g % 2)
                        ptb = pt[:].bitcast(BF16)
                        for j in range(4):
                            kc = g * 4 + j
                            K.op(pe, lambda e, kc=kc, j=j: e.transpose(out=ptb[:, j * 128:(j + 1) * 128], in_=ot[:, kc * 128:(kc + 1) * 128], identity=identb),
                                 r=[ot_b, cmb_b], w=[pt_b], mark=(j == 3))
                        if g % 2:
                            K.op(act, lambda e, g=g: e.copy(out=oT[:, g * 4:g * 4 + 4, ts * 128:(ts + 1) * 128], in_=ptb[:, 0:512].rearrange("p (a b) -> p a b", b=128)), r=[pt_b], w=[oT_b])
                        else:
                            K.op(dve, lambda e, g=g: e.tensor_copy(out=oT[:, g * 4:g * 4 + 4, ts * 128:(ts + 1) * 128], in_=ptb[:, 0:512].rearrange("p (a b) -> p a b", b=128)), r=[pt_b], w=[oT_b])

            prepD(0)
            for tt in range(NT):
                t0 = tt * 512
                oT, oT_b = oTs[tt % 2]
                if tt + 1 < NT:
                    prepD(tt + 1)

                def evac_res(ps_ap, ps_b, ts, col, w):
                    i = dctr[0] % 3
                    dctr[0] += 1
                    xb, xb_b = xbs[i]
                    tm, tm_b = tms[i]
                    r0 = t0 + ts * 128
                    K.dma(act, xb[:, 0:w], x_a[r0:r0 + 128, col:col + w], xb_b, w=[xb_b])
                    K.op(dve, lambda e: e.tensor_tensor(out=tm[:, 0:w], in0=ps_ap, in1=g1b[:, col:col + w], op=ALU.mult), r=[ps_b, g1b_b], w=[tm_b])
                    K.op(dve, lambda e: e.scalar_tensor_tensor(out=tm[:, 0:w], in0=xb[:, 0:w], scalar=ALPHA, in1=tm[:, 0:w], op0=ALU.mult, op1=ALU.add), r=[xb_b, tm_b], w=[tm_b])
                    K.dma(sp, xp_s.ap()[r0:r0 + 128, col:col + w], tm[:, 0:w], tm_b, r=[tm_b])

                proj_M(oT, oT_b, wout_a, 0, D, evac_res)
            K.barrier()

        def ln_pass(src_a, dst_a, g_d, b_d, tag, side=None, nside=0):
            with ExitStack() as es2:
                sgen = side(es2) if side is not None else None
                gb, gb_b = K.sb("lng" + tag, [128, D], F32, es2)
                bb, bb_b = K.sb("lnb" + tag, [128, D], F32, es2)
                xts = [K.sb("lx%s%d" % (tag, i), [128, D], F32, es2) for i in range(2)]
                ys = [K.sb("ly%s%d" % (tag, i), [128, D], F32, es2) for i in range(2)]
                sts = [K.sb("lst%s%d" % (tag, i), [128, 8, 6], F32, es2) for i in range(2)]
                mvs = [K.sb("lmv%s%d" % (tag, i), [128, 2], F32, es2) for i in range(2)]
                rss = [K.sb("lrs%s%d" % (tag, i), [128, 1], F32, es2) for i in range(2)]
                K.dma(sp, gb[:], g_d.ap()[0:1, :].partition_broadcast(128).rearrange("p a h -> p (a h)"), gb_b, w=[gb_b])
                K.dma(sp, bb[:], b_d.ap()[0:1, :].partition_broadcast(128).rearrange("p a h -> p (a h)"), bb_b, w=[bb_b])

                def stA(qt):
                    xt, xt_b = xts[qt % 2]
                    K.dma(act, xt[:], src_a[qt * 128:(qt + 1) * 128, :], xt_b, w=[xt_b])
                    ln_stats(xt, xt_b, *mvs[qt % 2], *sts[qt % 2], *rss[qt % 2])

                def stB(qt):
                    xt, xt_b = xts[qt % 2]
                    y, y_b = ys[qt % 2]
                    mv, mv_b = mvs[qt % 2]
                    rstd, rstd_b = rss[qt % 2]
                    K.op(dve, lambda e: e.tensor_scalar(out=y[:], in0=xt[:], scalar1=mv[:, 0:1], scalar2=rstd[:, 0:1], op0=ALU.subtract, op1=ALU.mult), r=[xt_b, mv_b, rstd_b], w=[y_b])
                    K.op(dve, lambda e: e.tensor_tensor(out=y[:], in0=y[:], in1=gb[:], op=ALU.mult), r=[y_b, gb_b], w=[y_b])
                    K.op(pool, lambda e: e.tensor_tensor(out=y[:], in0=y[:], in1=bb[:], op=ALU.add), r=[y_b, bb_b], w=[y_b])
                    K.dma(sp, dst_a[qt * 128:(qt + 1) * 128, :], y[:], y_b, r=[y_b])

                stA(0)
                for qt in range(NQ):
                    if qt + 1 < NQ:
                        stA(qt + 1)
                    stB(qt)
                    if sgen is not None:
                        for _ in range(nside):
                            next(sgen, None)
                if sgen is not None:
                    for _ in sgen:
                        pass
                K.barrier()

        ln_pass(xp_s.ap(), x1_s.ap(), ln1g_d, ln1b_d, "1", side=lambda es_: adaln_gen(es_, list(range(40, 48)), ldq=pool), nside=(8 * 128 + S - 1) // S)

        with ExitStack() as es2:
            hT, hT_b = K.sb("hT_E", [128, KC, 512], BF16, es2)
            fT, fT_b = K.sb("fT", [128, FC, 512], BF16, es2)
            cw, cw_b = K.sb("cw", [128, FC, 3], F32, es2)
            cbias, cbias_b = K.sb("cbias", [128, FC], F32, es2)
            carry, carry_b = K.sb("carry", [128, FC, 2], F32, es2)
            K.dma(sp, cw[:], convw_d.ap(), cw_b, w=[cw_b])
            K.dma(sp, cbias[:], convb_d.ap(), cbias_b, w=[cbias_b])
            K.op(dve, lambda e: e.memset(carry[:], 0.0), w=[carry_b])
            htiles = alloc_ht_tiles(es2, xt_alias=fT[:, 0:16, :].rearrange("p a b -> p (a b)").bitcast(F32), xh_alias=fT[:, 16:24, :].rearrange("p a b -> p (a b)"))
            ubs = [K.sb("ub%d" % i, [128, 516], F32, es2) for i in range(2)]
            a1s = [K.sb("a1%d" % i, [128, 512], F32, es2) for i in range(2)]
            g2b, g2b_b = K.sb("g2b", [128, D], F32, es2)
            xbs = [K.sb("fxb%d" % i, [128, 512], F32, es2) for i in range(2)]
            tms = [K.sb("ftm%d" % i, [128, 512], F32, es2) for i in range(2)]
            K.dma(sp, g2b[:], g2b_s.ap(), g2b_b, w=[g2b_b])
            dctr = [0]
            slabs = [(k0, min(16, FC - k0)) for k0 in range(0, FC, 16)]
            for tt in range(NT):
                t0 = tt * 512
                if tt > 0:
                    K.barrier()
                make_hT(es2, x1_s.ap(), t0, 2, hT, hT_b, htiles)
                for f2 in range(FC // 2):
                    wu, wu_b = load_w(wup_a, f2 * 256, 256)
                    wg, wg_b = load_w(wgate_a, f2 * 256, 256)
                    for j in range(2):
                        fc = f2 * 2 + j
                        pu, pu_b = pst(fc % 4)
                        for kc in range(KC):
                            K.op(pe, lambda e, kc=kc: e.matmul(pu[:, :], lhsT=wu[:, kc, j * 128:(j + 1) * 128], rhs=hT[:, kc, :], start=(kc == 0), stop=(kc == KC - 1)),
                                 r=[wu_b, hT_b], w=[pu_b], mark=(kc == KC - 1))
                        ub, ub_b = ubs[fc % 2]
                        a1, a1_b = a1s[fc % 2]
                        K.op(act, lambda e: e.copy(out=ub[:, 2:514], in_=pu[:, :]), r=[pu_b], w=[ub_b])
                        K.op(pool, lambda e, fc=fc: e.tensor_copy(out=ub[:, 0:2], in_=carry[:, fc, :]), r=[carry_b], w=[ub_b])
                        K.op(pool, lambda e, fc=fc: e.tensor_copy(out=carry[:, fc, :], in_=ub[:, 512:514]), r=[ub_b], w=[carry_b])
                        K.op(dve, lambda e, fc=fc: e.tensor_scalar(out=a1[:], in0=ub[:, 2:514], scalar1=cw[:, fc, 2:3], scalar2=cbias[:, fc:fc + 1], op0=ALU.mult, op1=ALU.add),
                             r=[ub_b, cw_b, cbias_b], w=[a1_b])
                        K.op(dve, lambda e, fc=fc: e.scalar_tensor_tensor(out=a1[:], in0=ub[:, 1:513], scalar=cw[:, fc, 1:2], in1=a1[:], op0=ALU.mult, op1=ALU.add),
                             r=[ub_b, cw_b, a1_b], w=[a1_b])
                        K.op(dve, lambda e, fc=fc: e.scalar_tensor_tensor(out=a1[:], in0=ub[:, 0:512], scalar=cw[:, fc, 0:1], in1=a1[:], op0=ALU.mult, op1=ALU.add),
                             r=[ub_b, cw_b, a1_b], w=[a1_b])
                        K.op(act, lambda e: e.activation(out=a1[:], in_=a1[:], func=AF.Gelu_apprx_tanh), r=[a1_b], w=[a1_b])
                    for j in range(2):
                        fc = f2 * 2 + j
                        pg, pg_b = pst(4 + fc % 4)
                        a1, a1_b = a1s[fc % 2]
                        for kc in range(KC):
                            K.op(pe, lambda e, kc=kc: e.matmul(pg[:, :], lhsT=wg[:, kc, j * 128:(j + 1) * 128], rhs=hT[:, kc, :], start=(kc == 0), stop=(kc == KC - 1)),
                                 r=[wg_b, hT_b], w=[pg_b], mark=(kc == KC - 1))
                        K.op(dve, lambda e, fc=fc: e.tensor_tensor(out=fT[:, fc, :], in0=a1[:], in1=pg[:, :], op=ALU.mult), r=[a1_b, pg_b], w=[fT_b])
                for db in range(D // 512):
                    col = db * 512
                    for (k0, nk) in slabs:
                        wt, wt_b = nextw()
                        wtv = wt[:].rearrange("p a b -> p (a b)")[:, 0:16 * 512].rearrange("p (a b) -> p a b", b=512)
                        K.dma(pool, wtv[:, 0:nk, :], wdown_a[:, k0:k0 + nk, col:col + 512], wt_b, w=[wt_b])
                        for ts in range(4):
                            pt, pt_b = pst((db % 2) * 4 + ts)
                            for kl in range(nk):
                                fc = k0 + kl
                                K.op(pe, lambda e, fc=fc, kl=kl: e.matmul(pt[:, :], lhsT=fT[:, fc, ts * 128:(ts + 1) * 128], rhs=wtv[:, kl, :], start=(fc == 0), stop=(fc == FC - 1)),
                                     r=[fT_b, wt_b], w=[pt_b], mark=(kl == nk - 1))
                    for ts in range(4):
                        pt, pt_b = pst((db % 2) * 4 + ts)
                        i = dctr[0] % 2
                        dctr[0] += 1
                        xb, xb_b = xbs[i]
                        tm, tm_b = tms[i]
                        r0 = t0 + ts * 128
                        K.dma(act, xb[:], x1_s.ap()[r0:r0 + 128, col:col + 512], xb_b, w=[xb_b])
                        K.op(dve, lambda e: e.tensor_tensor(out=tm[:], in0=pt[:, :], in1=g2b[:, col:col + 512], op=ALU.mult), r=[pt_b, g2b_b], w=[tm_b])
                        K.op(dve, lambda e: e.scalar_tensor_tensor(out=tm[:], in0=xb[:], scalar=ALPHA, in1=tm[:], op0=ALU.mult, op1=ALU.add), r=[xb_b, tm_b], w=[tm_b])
                        K.dma(sp, xp_s.ap()[r0:r0 + 128, col:col + 512], tm[:], tm_b, r=[tm_b])
            K.barrier()

        ln_pass(xp_s.ap(), out_a, ln2g_d, ln2b_d, "2")
        K.barrier()
    return nc


def make_in_maps(inp, S):
    cmask, oh = host_consts()
    B = inp["x"].shape[0]
    f = lambda a: np.ascontiguousarray(np.asarray(a, dtype=np.float32))
    shared = {
        "t5": f(inp["t5_table"]),
        "w_ada": f(inp["w_ada"][0]), "b_ada": f(inp["b_ada"][0]).reshape(1, -1),
        "w_in": f(inp["w_in"][0]), "w_g2": f(inp["w_g2"][0]), "b_g2": f(inp["b_g2"][0]).reshape(1, -1),
        "gla_norm": f(inp["gla_norm"][0]).reshape(1, -1), "w_out": f(inp["w_out"][0]),
        "ln1_g": f(inp["ln1_g"][0]).reshape(1, -1), "ln1_b": f(inp["ln1_b"][0]).reshape(1, -1),
        "w_up": f(inp["w_up"][0]), "w_gate": f(inp["w_gate"][0]),
        "conv_wT": f(np.asarray(inp["conv_w"][0]).reshape(3, FC, 128).transpose(2, 1, 0)),
        "conv_bT": f(np.asarray(inp["conv_b"][0]).reshape(FC, 128).T),
        "w_down": f(inp["w_down"][0]),
        "ln2_g": f(inp["ln2_g"][0]).reshape(1, -1), "ln2_b": f(inp["ln2_b"][0]).reshape(1, -1),
        "cmask": f(cmask), "onehot": f(oh),
    }
    maps = []
    for b in range(B):
        m = dict(shared)
        m["x"] = f(inp["x"][b])
        m["cT"] = f(np.asarray(inp["c"][b]).reshape(KC, 128).T)
        maps.append(m)
    return maps


def kernel(**inputs):
    S = inputs["x"].shape[1]
    B = inputs["x"].shape[0]
    nc = build(S)
    maps = make_in_maps(inputs, S)
    res = run_bass_kernel_spmd(nc, maps, core_ids=list(range(B)))
    return np.stack([np.asarray(r["out"], dtype=np.float32) for r in res.results], axis=0)
```

```python
import math
from contextlib import ExitStack

import numpy as np
import concourse.bass as bass
import concourse.mybir as mybir
from concourse.ap import AP as RawAP
from concourse.bass_utils import run_bass_kernel_spmd

F32, BF16 = mybir.dt.float32, mybir.dt.bfloat16
AF = mybir.ActivationFunctionType
ALU = mybir.AluOpType

D = 4096
KC = 32
CH = 64
A_HEADS, A_DH = 16, 128
I_HEADS, I_D = 32, 64
B_HEADS, B_DV, B_DK = 4, 512, 256
GR = 16
D_FF = 11008
FC = D_FF // 128
ALPHA = 2.0 ** 0.25
EPS = 1e-6
BIG = 1.0e30
NEG = -30000.0
O_AQ = 0
O_AK = 2048
O_AV = 2176
O_IQ = 2304
O_IK = 4352
O_IW = 4416
O_BQ = 4448
O_BK = 5472
O_BV = 6496
O_BG = 8544
O_BR = 8560
D_IN = 10608
TL = 384


class Tok:
    __slots__ = ("si", "val")

    def __init__(self, si, val):
        self.si, self.val = si, val


class Pending:
    __slots__ = ("eng", "tok")

    def __init__(self, eng):
        self.eng, self.tok = eng, None


class Buf:
    def __init__(self, name):
        self.name = name
        self.w = None
        self.r = {}
        self.dsi = None
        self.dcnt = 0


class Eng:
    def __init__(self, K, name, e):
        self.K, self.name, self.e = K, name, e
        self.si = K.newsem("e_" + name)
        self.cnt = 0
        self.waited = {}
        self.pending = []

    def wait(self, tok):
        if tok is None:
            return
        if isinstance(tok, Pending):
            if tok.tok is None:
                assert tok.eng is self, "unresolved pending token from other engine"
                return
            tok = tok.tok
        if self.waited.get(tok.si, 0) >= tok.val:
            return
        self.e.wait_ge(self.K.sems[tok.si], tok.val)
        self.waited[tok.si] = tok.val


class KB:
    def __init__(self, nc, es):
        self.nc, self.es = nc, es
        self.sems = []
        self.pe = Eng(self, "pe", nc.tensor)
        self.act = Eng(self, "act", nc.scalar)
        self.dve = Eng(self, "dve", nc.vector)
        self.pool = Eng(self, "pool", nc.gpsimd)
        self.sp = Eng(self, "sp", nc.sync)
        self.engs = [self.pe, self.act, self.dve, self.pool, self.sp]
        self.dbufs = []
        self.rr = 0

    def newsem(self, name):
        s = self.es.enter_context(self.nc.semaphore("%s_%d" % (name, len(self.sems))))
        self.sems.append(s)
        return len(self.sems) - 1

    def sb(self, name, shape, dt, es=None):
        t = (es or self.es).enter_context(self.nc.sbuf_tensor(name, list(shape), dt))
        return t, Buf(name)

    def ps(self, name, shape, dt, es=None):
        t = (es or self.es).enter_context(self.nc.psum_tensor(name, list(shape), dt))
        return t, Buf(name)

    def op(self, eng, fn, r=(), w=(), mark=True):
        for b in r:
            eng.wait(b.w)
        for b in w:
            eng.wait(b.w)
            for t in list(b.r.values()):
                eng.wait(t)
        ins = fn(eng.e)
        if mark:
            ins.then_inc(self.sems[eng.si], 1)
            eng.cnt += 1
            tok = Tok(eng.si, eng.cnt)
            for p in eng.pending:
                p.tok = tok
            eng.pending = []
        else:
            tok = Pending(eng)
            eng.pending.append(tok)
        for b in r:
            b.r[eng.si] = tok
        for b in w:
            b.w = tok
            b.r = {}
        return tok

    def dma(self, eng, out, in_, owner, r=(), w=()):
        if owner.dsi is None:
            owner.dsi = self.newsem("d_" + owner.name)
            self.dbufs.append(owner)
        for b in r:
            eng.wait(b.w)
        for b in w:
            if not (isinstance(b.w, Tok) and b.w.si == owner.dsi):
                eng.wait(b.w)
            for t in list(b.r.values()):
                eng.wait(t)
        eng.e.dma_start(out=out, in_=in_).then_inc(self.sems[owner.dsi], 16)
        owner.dcnt += 16
        tok = Tok(owner.dsi, owner.dcnt)
        for b in r:
            b.r[owner.dsi] = tok
        for b in w:
            b.w = tok
            b.r = {}
        return tok

    def barrier(self):
        for e in self.engs:
            assert not e.pending, e.name
        for e in self.engs:
            for e2 in self.engs:
                if e2 is not e and e2.cnt > 0:
                    e.wait(Tok(e2.si, e2.cnt))
            for b in self.dbufs:
                if b.dcnt > 0:
                    e.wait(Tok(b.dsi, b.dcnt))

    def ew(self):
        self.rr += 1
        return self.dve if self.rr % 2 else self.pool


def t5_bucket_np(rel):
    half, max_exact = 16, 8
    ret = np.where(rel > 0, half, 0)
    n = np.abs(rel)
    nf = np.maximum(n, 1).astype(np.float32)
    large = max_exact + (np.log(nf / np.float32(max_exact)) / np.float32(math.log(128 / 8)) * np.float32(8)).astype(np.int32)
    large = np.minimum(large, half - 1)
    return ret + np.where(n < max_exact, n, large)


def host_consts():
    c = {}
    i = np.arange(128)
    same = (i[:, None] // CH) == (i[None, :] // CH)
    c["ident"] = np.eye(128, dtype=np.float32)
    c["utri"] = np.where(same & (i[:, None] <= i[None, :]), -1.0 / 16, 0.0).astype(np.float32)
    c["ustrict"] = np.where(same & (i[:, None] > i[None, :]), -1.0 / 16, 0.0).astype(np.float32)
    c["trimask"] = np.where(same & (i[:, None] <= i[None, :]), 1.0, 0.0).astype(np.float32)
    m = np.arange(TL)
    rel = 127 - m
    bk = t5_bucket_np(rel)
    oh = np.zeros((32, TL), np.float32)
    oh[bk, m] = 1.0
    c["onehot"] = oh
    return np.concatenate([c["ident"], c["utri"], c["ustrict"], c["trimask"]], axis=1), oh


def build(S, dbg=False):
    NT = S // 512
    NQ = S // 128
    TOPK = min(256, S // 4)
    nc = bass.Bass("TRN2", target_bir_lowering=False)

    def din(name, shape, dt=F32):
        return nc.dram_tensor(name, list(shape), dt, kind="ExternalInput")

    x_d = din("x", [S, D])
    cT_d = din("cT", [128, KC])
    t5_d = din("t5", [32, A_HEADS])
    wada_d = din("w_ada", [D, 6 * D])
    bada_d = din("b_ada", [1, 6 * D])
    win_d = din("w_in", [D, D_IN])
    wg2_d = din("w_g2", [GR, 1024])
    bg2_d = din("b_g2", [1, 1024])
    gn_d = din("gla_norm", [1, B_DV])
    wout_d = din("w_out", [D, D])
    ln1g_d = din("ln1_g", [1, D])
    ln1b_d = din("ln1_b", [1, D])
    wup_d = din("w_up", [D, D_FF])
    wgate_d = din("w_gate", [D, D_FF])
    convw_d = din("conv_wT", [128, FC, 3])
    convb_d = din("conv_bT", [128, FC])
    wdown_d = din("w_down", [D_FF, D])
    ln2g_d = din("ln2_g", [1, D])
    ln2b_d = din("ln2_b", [1, D])
    cm_d = din("cmask", [128, 512])
    oh_d = din("onehot", [32, TL])
    out_d = nc.dram_tensor("out", [S, D], F32, kind="ExternalOutput")

    def dscr(name, shape, dt=F32):
        return nc.dram_tensor(name, list(shape), dt, kind=("ExternalOutput" if dbg else "Internal"))

    g1b_s = dscr("g1b_s", [128, D])
    g2b_s = dscr("g2b_s", [128, D])
    gsc_s = dscr("gsc_s", [A_HEADS, TL])
    grep_s = dscr("grep_s", [128, A_HEADS * TL])
    o_s = dscr("o_s", [S, D], BF16)
    hT_s = nc.dram_tensor("hT_s", [NT, 128, KC * 512], BF16, kind="Internal")
    q_s = nc.dram_tensor("q_s", [NT, 128, 16 * 512], BF16, kind="Internal")
    qi_s = nc.dram_tensor("qi_s", [NT, 128, 16 * 512], BF16, kind="Internal")
    xp_s = dscr("xp_s", [S, D])
    x1_s = dscr("x1_s", [S, D])
    dbg_d = {}

    x_a, out_a = x_d.ap(), out_d.ap()
    win_a = win_d.ap().rearrange("(kc p) n -> p kc n", p=128)
    wout_a = wout_d.ap().rearrange("(kc p) n -> p kc n", p=128)
    wup_a = wup_d.ap().rearrange("(kc p) n -> p kc n", p=128)
    wgate_a = wgate_d.ap().rearrange("(kc p) n -> p kc n", p=128)
    wdown_a = wdown_d.ap().rearrange("(fc p) n -> p fc n", p=128)
    wada_a = wada_d.ap().rearrange("(kc p) n -> p kc n", p=128)

    with ExitStack() as es:
        K = KB(nc, es)
        es.enter_context(nc.Block())
        pe, act, dve, pool, sp = K.pe, K.act, K.dve, K.pool, K.sp

        cmf, cmf_b = K.sb("cmf", [128, 512], F32)
        identf = cmf[:, 0:128]
        cmb, cmb_b = K.sb("cmb", [128, 512], BF16)
        identb, utri, ustrict = cmb[:, 0:128], cmb[:, 128:256], cmb[:, 256:384]
        trimf = cmf[:, 384:512]
        modT, modT_b = K.sb("modT", [128, 4, KC], F32)
        onesf, onesf_b = K.sb("onesf", [128, 128], F32)
        onesb, onesb_b = K.sb("onesb", [1, 128], BF16)
        psb = [K.ps("ps%d" % i, [128, 512], F32) for i in range(8)]
        wslots = [K.sb("wbuf%d" % i, [128, KC, 256], BF16) for i in range(3)]
        wctr = [0]

        K.dma(sp, cmf[:], cm_d.ap(), cmf_b, w=[cmf_b])
        K.op(dve, lambda e: e.tensor_copy(out=cmb[:], in_=cmf[:]), r=[cmf_b], w=[cmb_b])
        K.op(dve, lambda e: e.memset(onesf[:], 1.0), w=[onesf_b])
        K.op(dve, lambda e: e.memset(onesb[:], 1.0), w=[onesb_b])
        epsc, epsc_b = K.sb("epsc", [128, 2], F32)
        K.op(dve, lambda e: e.memset(epsc[:, 0:1], EPS), w=[epsc_b])
        K.op(dve, lambda e: e.memset(epsc[:, 1:2], float(B_DV) * EPS), w=[epsc_b])

        def pst(i):
            return psb[i][0], psb[i][1]

        def nextw():
            s = wslots[wctr[0] % 3]
            wctr[0] += 1
            return s

        def ln_stats(xt, xt_b, mv, mv_b, st, st_b, rstd, rstd_b):
            for c8 in range(8):
                K.op(dve, lambda e, c8=c8: e.bn_stats(out=st[:, c8, :], in_=xt[:, c8 * 512:(c8 + 1) * 512]), r=[xt_b], w=[st_b])
            K.op(dve, lambda e: e.bn_aggr(out=mv[:], in_=st[:].rearrange("p a b -> p (a b)")), r=[st_b], w=[mv_b])
            K.op(act, lambda e: e.activation(out=rstd[:], in_=mv[:, 1:2], func=AF.Sqrt, bias=epsc[:, 0:1]), r=[mv_b, epsc_b], w=[rstd_b])
            K.op(dve, lambda e: e.reciprocal(out=rstd[:], in_=rstd[:]), r=[rstd_b], w=[rstd_b])

        def make_hT(es2, src_a, t0, which, hT, hT_b, tiles):
            (xts, xh, xh_b, st, st_b, mv, mv_b, rstd, rstd_b) = tiles
            for ts in range(4):
                xt, xt_b = xts[ts % len(xts)]
                K.dma(act, xt[:], src_a[t0 + ts * 128:t0 + (ts + 1) * 128, :], xt_b, w=[xt_b])
                ln_stats(xt, xt_b, mv, mv_b, st, st_b, rstd, rstd_b)
                K.op(dve, lambda e: e.tensor_scalar(out=xh[:], in0=xt[:], scalar1=mv[:, 0:1], scalar2=rstd[:, 0:1], op0=ALU.subtract, op1=ALU.mult),
                     r=[xt_b, mv_b, rstd_b], w=[xh_b])
                for g in range(8):
                    pt, pt_b = pst(4 + g % 2)
                    ptb = pt[:].bitcast(BF16)
                    for j in range(4):
                        kc = g * 4 + j
                        K.op(pe, lambda e, kc=kc, j=j: e.transpose(out=ptb[:, j * 128:(j + 1) * 128], in_=xh[:, kc * 128:(kc + 1) * 128], identity=identb),
                             r=[xh_b, cmb_b], w=[pt_b], mark=(j == 3))
                    for j in range(4):
                        kc = g * 4 + j
                        if j % 2 == 0:
                            K.op(act, lambda e, kc=kc, j=j: e.activation(out=hT[:, kc, ts * 128:(ts + 1) * 128], in_=ptb[:, j * 128:(j + 1) * 128], func=AF.Identity,
                                                                          scale=modT[:, which + 1, kc:kc + 1], bias=modT[:, which, kc:kc + 1]),
                                 r=[pt_b, modT_b], w=[hT_b])
                        else:
                            K.op(dve, lambda e, kc=kc, j=j: e.tensor_scalar(out=hT[:, kc, ts * 128:(ts + 1) * 128], in0=ptb[:, j * 128:(j + 1) * 128],
                                                                             scalar1=modT[:, which + 1, kc:kc + 1], scalar2=modT[:, which, kc:kc + 1], op0=ALU.mult, op1=ALU.add),
                                 r=[pt_b, modT_b], w=[hT_b])

        hctr = [0]

        def alloc_ht_tiles(es2, xt_alias=None, xh_alias=None):
            if xt_alias is None:
                xts = [K.sb("xt0", [128, D], F32, es2)]
            else:
                xts = [(xt_alias, Buf("xt_alias"))]
            if xh_alias is None:
                xh, xh_b = K.sb("xh", [128, D], BF16, es2)
            else:
                xh, xh_b = xh_alias, Buf("xh_alias")
            hctr[0] += 1
            st, st_b = K.sb("st%d" % hctr[0], [128, 8, 6], F32, es2)
            mv, mv_b = K.sb("mv%d" % hctr[0], [128, 2], F32, es2)
            rstd, rstd_b = K.sb("rstd%d" % hctr[0], [128, 1], F32, es2)
            return (xts, xh, xh_b, st, st_b, mv, mv_b, rstd, rstd_b)

        def load_w(w_a, c0, w, k0=0, nk=KC, dup=False):
            wt, wt_b = nextw()
            K.dma(pool, wt[:, 0:nk, 0:w], w_a[:, k0:k0 + nk, c0:c0 + w], wt_b, w=[wt_b])
            if dup:
                K.dma(pool, wt[:, 0:nk, w:2 * w], w_a[:, k0:k0 + nk, c0:c0 + w], wt_b, w=[wt_b])
            return wt, wt_b

        pctr = [0]

        def proj_S(hT, hT_b, w_a, c0, ncols, evac, T=512, dup=False):
            for cs in range(c0, c0 + ncols, 256):
                w = min(256, c0 + ncols - cs)
                wt, wt_b = load_w(w_a, cs, w, dup=dup)
                weff = 2 * w if dup else w
                for j in range(0, weff, 128):
                    m = min(128, weff - j)
                    pt, pt_b = pst(pctr[0] % 4)
                    pctr[0] += 1
                    for kc in range(KC):
                        K.op(pe, lambda e, kc=kc: e.matmul(pt[0:m, 0:T], lhsT=wt[:, kc, j:j + m], rhs=hT[:, kc, 0:T], start=(kc == 0), stop=(kc == KC - 1)),
                             r=[wt_b, hT_b], w=[pt_b], mark=(kc == KC - 1))
                    evac(pt[0:m, 0:T], pt_b, cs - c0 + j, m)

        pctr2 = [0]

        def proj_M(hT, hT_b, w_a, c0, ncols, evac, T=512):
            if ncols % 512 == 0:
                for cs in range(c0, c0 + ncols, 512):
                    base = (pctr2[0] % 2) * 4
                    pctr2[0] += 1
                    for half in range(2):
                        wt, wt_b = nextw()
                        wtv = wt[:].rearrange("p a b -> p (a b)").rearrange("p (a b) -> p a b", b=512)
                        K.dma(pool, wtv[:, :, :], w_a[:, half * 16:(half + 1) * 16, cs:cs + 512], wt_b, w=[wt_b])
                        for ts in range(T // 128):
                            pt, pt_b = pst(base + ts)
                            for kl in range(16):
                                kc = half * 16 + kl
                                K.op(pe, lambda e, kc=kc, kl=kl: e.matmul(pt[:, :], lhsT=hT[:, kc, ts * 128:(ts + 1) * 128], rhs=wtv[:, kl, :], start=(kc == 0), stop=(kc == KC - 1)),
                                     r=[wt_b, hT_b], w=[pt_b], mark=(kl == 15))
                    for ts in range(T // 128):
                        pt, pt_b = pst(base + ts)
                        evac(pt[:, :], pt_b, ts, cs - c0, 512)
                return
            for cs in range(c0, c0 + ncols, 256):
                w = min(256, c0 + ncols - cs)
                wt, wt_b = load_w(w_a, cs, w)
                for ts in range(T // 128):
                    pt, pt_b = pst(pctr[0] % 4)
                    pctr[0] += 1
                    for kc in range(KC):
                        K.op(pe, lambda e, kc=kc: e.matmul(pt[:, 0:w], lhsT=hT[:, kc, ts * 128:(ts + 1) * 128], rhs=wt[:, kc, 0:w], start=(kc == 0), stop=(kc == KC - 1)),
                             r=[wt_b, hT_b], w=[pt_b], mark=(kc == KC - 1))
                    evac(pt[:, 0:w], pt_b, ts, cs - c0, w)

        ectr = [0]

        def evac_copy(dst_fn, dst_b, scale=None):
            def f(ps_ap, ps_b, *a):
                d = dst_fn(*a)
                ectr[0] += 1
                if ectr[0] % 2:
                    if scale is None:
                        K.op(act, lambda e: e.copy(out=d, in_=ps_ap), r=[ps_b], w=[dst_b])
                    else:
                        K.op(act, lambda e: e.activation(out=d, in_=ps_ap, func=AF.Copy, scale=scale), r=[ps_b], w=[dst_b])
                else:
                    if scale is None:
                        K.op(dve, lambda e: e.tensor_copy(out=d, in_=ps_ap), r=[ps_b], w=[dst_b])
                    else:
                        K.op(dve, lambda e: e.tensor_scalar(out=d, in0=ps_ap, scalar1=scale, scalar2=None, op0=ALU.mult), r=[ps_b], w=[dst_b])
            return f

        actr = [0]

        def adaln_gen(es2, cb_list, ldq=None):
            ldq = ldq or pool
            actr[0] += 1
            tg = "A%d" % actr[0]
            cTt, cT_b = K.sb("cTt" + tg, [128, KC], F32, es2)
            crep, crep_b = K.sb("crep" + tg, [128, KC, 128], BF16, es2)
            was = [K.sb("wa%s%d" % (tg, i), [128, 16, 512], BF16, es2) for i in range(2)]
            bat, bat_b = K.sb("bat" + tg, [1, 512], BF16, es2)
            stg = [K.sb("stg%s%d" % (tg, i), [128, 512], F32, es2) for i in range(2)]
            junk, junk_b = K.sb("junkA" + tg, [128, 128], F32, es2)
            K.dma(sp, cTt[:], cT_d.ap(), cT_b, w=[cT_b])
            K.op(act, lambda e: e.activation(out=cTt[:], in_=cTt[:], func=AF.Silu), r=[cT_b], w=[cT_b])
            for kc in range(KC):
                K.op(dve, lambda e, kc=kc: e.tensor_scalar(out=crep[:, kc, :], in0=onesf[:], scalar1=cTt[:, kc:kc + 1], scalar2=None, op0=ALU.mult),
                     r=[onesf_b, cT_b], w=[crep_b])
            wi = 0
            for cb in cb_list:
                which, cbl = cb // 8, cb % 8
                pt, pt_b = pst(cb % 4)
                K.dma(pool, bat[:], bada_d.ap()[0:1, cb * 512:(cb + 1) * 512], bat_b, w=[bat_b])
                K.op(pe, lambda e: e.matmul(pt[:, :], lhsT=onesb[0:1, :], rhs=bat[0:1, :], start=True, stop=False), r=[onesb_b, bat_b], w=[pt_b], mark=False)
                for kg in range(2):
                    wa, wa_b = was[wi % 2]
                    wi += 1
                    K.dma(pool, wa[:], wada_a[:, kg * 16:(kg + 1) * 16, cb * 512:(cb + 1) * 512], wa_b, w=[wa_b])
                    for kl in range(16):
                        kc = kg * 16 + kl
                        K.op(pe, lambda e, kc=kc, kl=kl: e.matmul(pt[:, :], lhsT=crep[:, kc, :], rhs=wa[:, kl, :], start=False, stop=(kc == KC - 1)),
                             r=[crep_b, wa_b], w=[pt_b], mark=(kl == 15))
                if which in (2, 5):
                    sg, sg_b = stg[cb % 2]
                    K.op(act, lambda e: e.copy(out=sg[:], in_=pt[:, :]), r=[pt_b], w=[sg_b])
                    dst = (g1b_s if which == 2 else g2b_s).ap()[:, cbl * 512:(cbl + 1) * 512]
                    K.dma(sp, dst, sg[:], sg_b, r=[sg_b])
                else:
                    slot = {0: 0, 1: 1, 3: 2, 4: 3}[which]
                    for c in range(4):
                        col = cbl * 4 + c
                        K.op(dve, lambda e, c=c, col=col: e.scalar_tensor_tensor(out=junk[:], in0=pt[:, c * 128:(c + 1) * 128], scalar=1.0, in1=identf, op0=ALU.mult, op1=ALU.mult,
                                                                                   accum_out=modT[:, slot, col:col + 1]),
                             r=[pt_b, cmf_b], w=[junk_b, modT_b])
                    if cbl == 7 and which in (1, 4):
                        K.op(dve, lambda e, slot=slot: e.tensor_scalar(out=modT[:, slot, :], in0=modT[:, slot, :], scalar1=1.0, scalar2=None, op0=ALU.add), r=[modT_b], w=[modT_b])
                yield

        K.op(dve, lambda e: e.memset(modT[:], 0.0), w=[modT_b])
        with ExitStack() as es2:
            for _ in adaln_gen(es2, list(range(16))):
                pass
            K.barrier()

        with ExitStack() as esB:
            SW = max(S, 2048)
            kT, kT_b = K.sb("kT", [128, S], BF16, esB)
            kiT, kiT_b = K.sb("kiT", [128, S], BF16, esB)
            vaug, vaug_b = K.sb("vaug", [128, NQ, 132], BF16, esB)
            wabs, wabs_b = K.sb("wabs", [128, NQ, 32], F32, esB)
            wsgn, wsgn_b = K.sb("wsgn", [128, NQ, 32], F32, esB)
            K.op(dve, lambda e: e.memset(vaug[:], 1.0), w=[vaug_b])

            with ExitStack() as es2:
                hTs = [K.sb("hT_B%d" % i, [128, KC, 512], BF16, es2) for i in range(2)]
                qT, qT_b = K.sb("qT", [128, 4, A_HEADS, 128], BF16, es2)
                qiT, qiT_b = K.sb("qiT", [128, 4, 16, 128], BF16, es2)
                wraw, wraw_b = K.sb("wraw", [128, 4, 32], F32, es2)
                xtB = [K.sb("xtB%d" % i, [128, D], F32, es2) for i in range(1)]
                xh4 = [K.sb("xh4_%d" % i, [128, D], BF16, es2) for i in range(2)]
                stB, stB_b = K.sb("stB", [128, 8, 6], F32, es2)
                mvB, mvB_b = K.sb("mvB", [128, 2], F32, es2)
                rsB, rsB_b = K.sb("rsB", [128, 1], F32, es2)

                def ln_part(t0, tss=(0, 1, 2, 3)):
                    for ts in tss:
                        xt, xt_b = xtB[ts % len(xtB)]
                        xh, xh_b = xh4[ts % 2]
                        K.dma(act, xt[:], x_a[t0 + ts * 128:t0 + (ts + 1) * 128, :], xt_b, w=[xt_b])
                        ln_stats(xt, xt_b, mvB, mvB_b, stB, stB_b, rsB, rsB_b)
                        K.op(dve, lambda e: e.tensor_scalar(out=xh[:], in0=xt[:], scalar1=mvB[:, 0:1], scalar2=rsB[:, 0:1], op0=ALU.subtract, op1=ALU.mult),
                             r=[xt_b, mvB_b, rsB_b], w=[xh_b])

                def tr_part(hT, hT_b, tss=(0, 1, 2, 3)):
                    for ts in tss:
                        xh, xh_b = xh4[ts % 2]
                        for g in range(8):
                            pt, pt_b = pst(4 + g % 2)
                            ptb = pt[:].bitcast(BF16)
                            for j in range(4):
                                kc = g * 4 + j
                                K.op(pe, lambda e, kc=kc, j=j: e.transpose(out=ptb[:, j * 128:(j + 1) * 128], in_=xh[:, kc * 128:(kc + 1) * 128], identity=identb),
                                     r=[xh_b, cmb_b], w=[pt_b], mark=(j == 3))
                            for j in range(4):
                                kc = g * 4 + j
                                if j % 2 == 0:
                                    K.op(act, lambda e, kc=kc, j=j: e.activation(out=hT[:, kc, ts * 128:(ts + 1) * 128], in_=ptb[:, j * 128:(j + 1) * 128], func=AF.Identity,
                                                                                  scale=modT[:, 1, kc:kc + 1], bias=modT[:, 0, kc:kc + 1]), r=[pt_b, modT_b], w=[hT_b])
                                else:
                                    K.op(dve, lambda e, kc=kc, j=j: e.tensor_scalar(out=hT[:, kc, ts * 128:(ts + 1) * 128], in0=ptb[:, j * 128:(j + 1) * 128],
                                                                                     scalar1=modT[:, 1, kc:kc + 1], scalar2=modT[:, 0, kc:kc + 1], op0=ALU.mult, op1=ALU.add),
                                         r=[pt_b, modT_b], w=[hT_b])

                def evq(dst, dst_b, scale):
                    def f(ps_ap, ps_b, col, m):
                        d = dst[:, :, col // 128, :]
                        src = ps_ap.rearrange("p (a b) -> p a b", b=128)
                        ectr[0] += 1
                        if ectr[0] % 2:
                            K.op(act, lambda e: e.activation(out=d, in_=src, func=AF.Copy, scale=scale), r=[ps_b], w=[dst_b])
                        else:
                            K.op(dve, lambda e: e.tensor_scalar(out=d, in0=src, scalar1=scale, scalar2=None, op0=ALU.mult), r=[ps_b], w=[dst_b])
                    return f

                for ts_ in range(4):
                    ln_part(0, (ts_,))
                    tr_part(*hTs[0], tss=(ts_,))
                for tt in range(NT):
                    t0 = tt * 512
                    hT, hT_b = hTs[tt % 2]
                    K.dma(sp, hT_s.ap()[tt], hT[:].rearrange("p a b -> p (a b)"), hT_b, r=[hT_b])
                    proj_S(hT, hT_b, win_a, O_AK, 128, evac_copy(lambda col, m: kT[:, t0:t0 + 512], kT_b))
                    proj_S(hT, hT_b, win_a, O_IK, 64, evac_copy(lambda col, m: kiT[:, t0:t0 + 512], kiT_b), dup=True)
                    proj_M(hT, hT_b, win_a, O_AV, 128, evac_copy(lambda ts, col, w: vaug[:, tt * 4 + ts, 0:128], vaug_b))
                    proj_M(hT, hT_b, win_a, O_IW, 32, evac_copy(lambda ts, col, w: wraw[:, ts, :], wraw_b))
                    K.op(act, lambda e: e.activation(out=wabs[:, tt * 4:tt * 4 + 4, :], in_=wraw[:], func=AF.Abs, scale=I_HEADS ** -0.5), r=[wraw_b], w=[wabs_b])
                    K.op(act, lambda e: e.activation(out=wsgn[:, tt * 4:tt * 4 + 4, :], in_=wraw[:], func=AF.Sign), r=[wraw_b], w=[wsgn_b])
                    proj_S(hT, hT_b, win_a, O_IQ, 2048, evq(qiT, qiT_b, I_D ** -0.5))
                    K.dma(sp, qi_s.ap()[tt], qiT[:].rearrange("p a b c -> p (a b c)"), qiT_b, r=[qiT_b])
                    if tt + 1 < NT:
                        ln_part(t0 + 512, (0, 1))
                    proj_S(hT, hT_b, win_a, O_AQ, 2048, evq(qT, qT_b, A_DH ** -0.5))
                    K.dma(sp, q_s.ap()[tt], qT[:].rearrange("p a b c -> p (a b c)"), qT_b, r=[qT_b])
                    if tt + 1 < NT:
                        nh = hTs[(tt + 1) % 2]
                        tr_part(*nh, tss=(0,))
                        ln_part(t0 + 512, (2,))
                        tr_part(*nh, tss=(1,))
                        ln_part(t0 + 512, (3,))
                        tr_part(*nh, tss=(2, 3))
                K.barrier()

            biasM, biasM_b = K.sb("biasM", [128, A_HEADS, 2, 128], BF16, esB)
            b15, b15_b = K.sb("b15", [128, A_HEADS], F32, esB)
            es3 = ExitStack()
            biasF, biasF_b = K.sb("biasF", [128, A_HEADS, 2, 128], F32, es3)
            t5t, t5_b = K.sb("t5t", [32, A_HEADS], F32, es3)
            oht, oh_b = K.sb("oht", [32, TL], F32, es3)
            gsb, gsb_b = K.sb("gsb", [A_HEADS, TL], F32, es3)
            gsc_b, grep_b = Buf("gsc"), Buf("grep")

            K.dma(sp, t5t[:], t5_d.ap(), t5_b, w=[t5_b])
            K.dma(sp, oht[:], oh_d.ap(), oh_b, w=[oh_b])
            K.dma(sp, b15[:], t5_d.ap()[15:16, :].partition_broadcast(128).rearrange("p a h -> p (a h)"), b15_b, w=[b15_b])
            pt, pt_b = pst(7)
            K.op(pe, lambda e: e.matmul(pt[0:A_HEADS, 0:TL], lhsT=t5t[:, :], rhs=oht[:, :], start=True, stop=True), r=[t5_b, oh_b], w=[pt_b])
            K.op(act, lambda e: e.copy(out=gsb[:], in_=pt[0:A_HEADS, 0:TL]), r=[pt_b], w=[gsb_b])
            K.dma(sp, gsc_s.ap(), gsb[:], gsb_b, r=[gsb_b], w=[gsc_b])
            K.dma(sp, grep_s.ap(), gsc_s.ap().rearrange("h l -> (h l)").partition_broadcast(128), gsb_b, r=[gsc_b], w=[grep_b])
            for hh in range(A_HEADS):
                for a_ in range(2):
                    src = RawAP(grep_s, hh * TL + 127 + 128 * (1 - a_), [[A_HEADS * TL - 1, 128], [1, 128]])
                    K.dma(sp, biasF[:, hh, a_, :], src, biasF_b, r=[grep_b], w=[biasF_b])
            K.op(dve, lambda e: e.tensor_copy(out=biasM[:], in_=biasF[:]), r=[biasF_b], w=[biasM_b])
            K.barrier()
            es3.close()

            es2 = esB
            acc, acc_b = K.sb("acc", [128, SW], F32, es2)
            work, work_b = K.sb("work", [128, SW], F32, es2)
            rts = [K.sb("rt%d" % i, [128, 512], F32, es2) for i in range(4)]
            m8, m8_b = K.sb("m8", [128, 8], F32, es2)
            thr, thr_b = K.sb("thr", [128, 1], F32, es2)
            madd, madd_b = K.sb("madd", [128, S], BF16, es2)
            maskTs = [K.sb("maskT%d" % i, [128, NQ, 128], BF16, es2) for i in range(2)]
            pts = [K.sb("pt%d" % i, [128, 512], BF16, es2) for i in range(3)]
            oa, oa_b = K.sb("oa", [128, 2048], BF16, es2)
            oraw, oraw_b = K.sb("oraw", [128, A_HEADS, 132], F32, es2)
            rinv16, rinv16_b = K.sb("rinv16", [128, A_HEADS], F32, es2)
            qTqs = [K.sb("qTq%d" % i, [128, A_HEADS, 128], BF16, es2) for i in range(2)]
            qiTqs = [K.sb("qiTq%d" % i, [128, 16, 128], BF16, es2) for i in range(2)]

            def load_q(qt):
                tt, ts = qt // 4, qt % 4
                qq, qq_b = qTqs[qt % 2]
                qi, qi_b = qiTqs[qt % 2]
                K.dma(act, qi[:].rearrange("p a b -> p (a b)"), qi_s.ap()[tt][:, ts * 2048:(ts + 1) * 2048], qi_b, w=[qi_b])
                K.dma(act, qq[:].rearrange("p a b -> p (a b)"), q_s.ap()[tt][:, ts * 2048:(ts + 1) * 2048], qq_b, w=[qq_b])

            def S1(qt):
                qiT, qiT_b = qiTqs[qt % 2]
                ts = qt
                sadm = 128 * (qt + 1)
                nkb = (sadm + 511) // 512
                tq = slice(0, 128)
                ri = 0
                for h in range(I_HEADS):
                    rows = slice((h % 2) * 64, (h % 2) * 64 + 64)
                    for kb in range(nkb):
                        kw = min(512, sadm - kb * 512)
                        ks_ = slice(kb * 512, kb * 512 + kw)
                        pt, pt_b = pst(ri % 4)
                        rt, rt_b = rts[ri % 4]
                        ri += 1
                        K.op(pe, lambda e: e.matmul(pt[:, 0:kw], lhsT=qiT[rows, h // 2, tq], rhs=kiT[rows, ks_], start=True, stop=True), r=[qiT_b, kiT_b], w=[pt_b])
                        K.op(act, lambda e: e.activation(out=rt[:, 0:kw], in_=pt[:, 0:kw], func=AF.Relu, scale=wabs[:, ts, h:h + 1]), r=[pt_b, wabs_b], w=[rt_b])
                        if h == 0:
                            K.op(dve, lambda e: e.tensor_scalar(out=acc[:, ks_], in0=rt[:, 0:kw], scalar1=wsgn[:, ts, h:h + 1], scalar2=None, op0=ALU.mult), r=[rt_b, wsgn_b], w=[acc_b])
                        else:
                            K.op(dve, lambda e: e.scalar_tensor_tensor(out=acc[:, ks_], in0=rt[:, 0:kw], scalar=wsgn[:, ts, h:h + 1], in1=acc[:, ks_], op0=ALU.mult, op1=ALU.add),
                                 r=[rt_b, wsgn_b, acc_b], w=[acc_b])
                K.op(dve, lambda e: e.memset(acc[0:64, sadm - 64:sadm], -BIG), w=[acc_b])
                if sadm > TOPK:
                    K.op(act, lambda e: e.copy(out=work[:, 0:sadm], in_=acc[:, 0:sadm]), r=[acc_b], w=[work_b])
                    nr = TOPK // 8
                    for r_ in range(nr):
                        K.op(dve, lambda e: e.max(out=m8[:], in_=work[:, 0:sadm]), r=[work_b], w=[m8_b])
                        if r_ < nr - 1:
                            K.op(dve, lambda e: e.match_replace(out=work[:, 0:sadm], in_to_replace=m8[:], in_values=work[:, 0:sadm], imm_value=-BIG), r=[m8_b, work_b], w=[work_b])
                    K.op(dve, lambda e: e.tensor_copy(out=thr[:], in_=m8[:, 7:8]), r=[m8_b], w=[thr_b])
                else:
                    K.op(dve, lambda e: e.memset(thr[:], -0.5 * BIG), w=[thr_b])
                K.op(dve, lambda e: e.tensor_scalar(out=madd[:, 0:sadm], in0=acc[:, 0:sadm], scalar1=thr[:, 0:1], scalar2=NEG, op0=ALU.is_lt, op1=ALU.mult), r=[acc_b, thr_b], w=[madd_b])

            def S2(qt):
                mT, mT_b = maskTs[qt % 2]
                for g in range((qt + 4) // 4):
                    pt, pt_b = pst(4 + g % 2)
                    ptb = pt[:].bitcast(BF16)
                    n = min(4, qt + 1 - g * 4)
                    for j in range(n):
                        ks = g * 4 + j
                        K.op(pe, lambda e, ks=ks, j=j: e.transpose(out=ptb[:, j * 128:(j + 1) * 128], in_=madd[:, ks * 128:(ks + 1) * 128], identity=identb),
                             r=[madd_b, cmb_b], w=[pt_b], mark=(j == n - 1))
                    K.op(act, lambda e, g=g, n=n: e.copy(out=mT[:, g * 4:g * 4 + n, :], in_=ptb[:, 0:n * 128].rearrange("p (a b) -> p a b", b=128)), r=[pt_b], w=[mT_b])

            def S3(qt):
                qT, qT_b = qTqs[qt % 2]
                tq = slice(0, 128)
                mT, mT_b = maskTs[qt % 2]
                nfar = max(0, qt - 1)
                groups = [(list(range(g, min(g + 4, nfar))), False) for g in range(0, nfar, 4)]
                groups.append(([ks for ks in (qt - 1, qt) if ks >= 0], True))
                pi = 0
                for h in range(A_HEADS):
                    po, po_b = pst(6 + h % 2)
                    for gi, (grp, near) in enumerate(groups):
                        n = len(grp)
                        ks0 = grp[0]
                        ps_, ps_b = pst(4 + pi % 2)
                        pT_, pT_b = pts[pi % 3]
                        pi += 1
                        K.op(pe, lambda e: e.matmul(ps_[:, 0:n * 128], lhsT=identb, rhs=mT[:, ks0:ks0 + n, :].rearrange("p a b -> p (a b)"), start=True, stop=False),
                             r=[cmb_b, mT_b], w=[ps_b], mark=False)
                        if near:
                            brhs = biasM[:, h, 2 - n:2, :].rearrange("p a b -> p (a b)")
                            K.op(pe, lambda e: e.matmul(ps_[:, 0:n * 128], lhsT=identb, rhs=brhs, start=False, stop=False), r=[cmb_b, biasM_b], w=[ps_b], mark=False)
                        for j, ks in enumerate(grp):
                            K.op(pe, lambda e, j=j, ks=ks: e.matmul(ps_[:, j * 128:(j + 1) * 128], lhsT=kT[:, ks * 128:(ks + 1) * 128], rhs=qT[:, h, tq], start=False, stop=(j == n - 1)),
                                 r=[kT_b, qT_b], w=[ps_b], mark=(j == n - 1))
                        if near:
                            K.op(act, lambda e: e.activation(out=pT_[:, 0:n * 128], in_=ps_[:, 0:n * 128], func=AF.Exp), r=[ps_b], w=[pT_b])
                        else:
                            K.op(act, lambda e: e.activation(out=pT_[:, 0:n * 128], in_=ps_[:, 0:n * 128], func=AF.Exp, bias=b15[:, h:h + 1]), r=[ps_b, b15_b], w=[pT_b])
                        for j, ks in enumerate(grp):
                            last = (gi == len(groups) - 1) and (j == n - 1)
                            K.op(pe, lambda e, j=j, ks=ks: e.matmul(po[:, 0:129], lhsT=pT_[:, j * 128:(j + 1) * 128], rhs=vaug[:, ks, 0:129], start=(gi == 0 and j == 0), stop=last),
                                 r=[pT_b, vaug_b], w=[po_b], mark=(j == n - 1))
                    K.op(act, lambda e: e.copy(out=oraw[:, h, 0:129], in_=po[:, 0:129]), r=[po_b], w=[oraw_b])
                K.op(dve, lambda e: e.reciprocal(out=rinv16[:], in_=oraw[:, :, 128]), r=[oraw_b], w=[rinv16_b])
                for h in range(A_HEADS):
                    K.op(act, lambda e: e.activation(out=oa[:, h * 128:(h + 1) * 128], in_=oraw[:, h, 0:128], func=AF.Copy, scale=rinv16[:, h:h + 1]),
                         r=[oraw_b, rinv16_b], w=[oa_b])
                K.dma(sp, o_s.ap()[qt * 128:(qt + 1) * 128, 0:2048], oa[:], oa_b, r=[oa_b])

            with ExitStack() as esS:
                side = adaln_gen(esS, list(range(16, 40)), ldq=pool)
                load_q(0)
                if NQ > 1:
                    load_q(1)
                S1(0)
                S2(0)
                for qt in range(NQ):
                    if qt + 1 < NQ:
                        S1(qt + 1)
                    S3(qt)
                    if qt + 2 < NQ:
                        load_q(qt + 2)
                    if qt + 1 < NQ:
                        S2(qt + 1)
                    for _ in range(2 if qt % 2 == 0 else 1):
                        next(side, None)
                for _ in side:
                    pass
                K.barrier()

        with ExitStack() as es2:
            hT, hT_b = K.sb("hT_C", [128, KC, 512], BF16, es2)
            qTg, qTg_b = K.sb("qTg", [128, 8, 512], BF16, es2)
            kTg, kTg_b = K.sb("kTg", [128, 8, 512], BF16, es2)
            ktok, ktok_b = K.sb("ktok", [128, 4, 1024], BF16, es2)
            vtok, vtok_b = K.sb("vtok", [128, 4, 2048], BF16, es2)
            Gt, Gt_b = K.sb("Gt", [128, 4, 2048], BF16, es2)
            lrT, lrT_b = K.sb("lrT", [GR, 512], BF16, es2)
            wg2, wg2_b = K.sb("wg2", [GR, 1024], BF16, es2)
            bg2, bg2_b = K.sb("bg2", [1, 1024], BF16, es2)
            gnb, gnb_b = K.sb("gnb", [128, B_DV], F32, es2)
            Sf, Sf_b = K.sb("Sf", [128, 8, 512], F32, es2)
            Sb, Sb_b = K.sb("Sb", [128, 8, 512], BF16, es2)
            e1, e1_b = K.sb("e1", [128, 1024], F32, es2)
            spl, spl_b = K.sb("spl", [128, 1024], BF16, es2)
            EbT, EbT_b = K.sb("EbT", [128, 8, 128], F32, es2)
            EnbT, EnbT_b = K.sb("EnbT", [128, 8, 128], BF16, es2)
            qe, qe_b = K.sb("qe", [128, 8, 128], BF16, es2)
            qeP, qeP_b = K.sb("qeP", [128, 8, 2, 128], BF16, es2)
            ke, ke_b = K.sb("ke", [128, 8, 128], BF16, es2)
            Ec, Ec_b = e1, e1_b
            sq, sq_b = e1[:, 0:256].bitcast(BF16), e1_b
            kd, kd_b = K.sb("kd", [128, 1024], BF16, es2)
            Am4, _ = K.sb("Am4", [128, 4, 128], BF16, es2)
            Am_bs = [Buf("Am%d" % i) for i in range(4)]
            pa_bs = [Buf("pa%d" % i) for i in range(4)]
            Sf_bs = [Buf("Sf%d" % i) for i in range(8)]
            Sb_bs = [Buf("Sb%d" % i) for i in range(8)]
            ss_bs = [Buf("ss%d" % i) for i in range(4)]
            rs_bs = [Buf("rs%d" % i) for i in range(4)]
            ssq4, _ = K.sb("ssq4", [128, 4], F32, es2)
            rs4, _ = K.sb("rs4", [128, 4], F32, es2)
            uctr = [0]

            ssq, ssq_b = K.sb("ssq", [128, 1], F32, es2)
            rs2, rs2_b = K.sb("rs2", [128, 1], F32, es2)
            ob, ob_b = K.sb("ob", [128, 2048], BF16, es2)
            silt = [K.sb("silt%d" % i, [128, 256], F32, es2) for i in range(2)]

            es3 = ExitStack()
            K.dma(pool, wg2[:], wg2_d.ap(), wg2_b, w=[wg2_b])
            K.dma(pool, bg2[:], bg2_d.ap(), bg2_b, w=[bg2_b])
            K.dma(sp, gnb[:], gn_d.ap()[0:1, :].partition_broadcast(128).rearrange("p a h -> p (a h)"), gnb_b, w=[gnb_b])
            K.op(dve, lambda e: e.tensor_scalar(out=gnb[:], in0=gnb[:], scalar1=float(B_DV) ** 0.5, scalar2=None, op0=ALU.mult), r=[gnb_b], w=[gnb_b])
            K.op(dve, lambda e: e.memset(Sf[:], 0.0), w=[Sf_b])
            K.op(pool, lambda e: e.memset(Sb[:], 0.0), w=[Sb_b])
            for b_ in Sf_bs:
                b_.w = Sf_b.w
            for b_ in Sb_bs:
                b_.w = Sb_b.w
            K.op(pool, lambda e: e.memset(qeP[:], 0.0), w=[qeP_b])
            K.barrier()
            es3.close()

            sctr = [0]

            def evac_silu(ps_ap, ps_b, ts, col, w):
                for o_ in range(0, w, 256):
                    sl, sl_b = silt[sctr[0] % 2]
                    sctr[0] += 1
                    K.op(act, lambda e: e.activation(out=sl[:, 0:256], in_=ps_ap[:, o_:o_ + 256], func=AF.Silu), r=[ps_b], w=[sl_b])
                    hcol = (col + o_) % B_DV
                    K.op(pool, lambda e: e.tensor_tensor(out=Gt[:, ts, col + o_:col + o_ + 256], in0=sl[:, 0:256], in1=gnb[:, hcol:hcol + 256], op=ALU.mult), r=[sl_b, gnb_b], w=[Gt_b])

            for tt in range(NT):
                t0 = tt * 512
                K.dma(act, hT[:].rearrange("p a b -> p (a b)"), hT_s.ap()[tt], hT_b, w=[hT_b])
                proj_S(hT, hT_b, win_a, O_BQ, 1024, evac_copy(lambda col, m: qTg[:, col // 128, :], qTg_b, scale=B_DK ** -0.5))
                proj_S(hT, hT_b, win_a, O_BK, 1024, evac_copy(lambda col, m: kTg[:, col // 128, :], kTg_b))
                for ts in range(4):
                    for g in range(2):
                        pt, pt_b = pst(4 + g % 2)
                        ptb = pt[:].bitcast(BF16)
                        for j in range(4):
                            c8 = g * 4 + j
                            K.op(pe, lambda e, c8=c8, j=j: e.transpose(out=ptb[:, j * 128:(j + 1) * 128], in_=kTg[:, c8, ts * 128:(ts + 1) * 128], identity=identb),
                                 r=[kTg_b, cmb_b], w=[pt_b], mark=(j == 3))
                        if g % 2:
                            K.op(act, lambda e, g=g: e.copy(out=ktok[:, ts, g * 512:(g + 1) * 512], in_=ptb[:, 0:512]), r=[pt_b], w=[ktok_b])
                        else:
                            K.op(dve, lambda e, g=g: e.tensor_copy(out=ktok[:, ts, g * 512:(g + 1) * 512], in_=ptb[:, 0:512]), r=[pt_b], w=[ktok_b])
                proj_M(hT, hT_b, win_a, O_BV, 2048, evac_copy(lambda ts, col, w: vtok[:, ts, col:col + w], vtok_b))
                proj_S(hT, hT_b, win_a, O_BG, GR, evac_copy(lambda col, m: lrT[:, :], lrT_b))
                proj_M(hT, hT_b, win_a, O_BR, 2048, evac_silu)
                for ts in range(4):
                    tq = slice(ts * 128, (ts + 1) * 128)
                    for hf in range(2):
                        pt, pt_b = pst(4 + hf)
                        cs_ = slice(hf * 512, (hf + 1) * 512)
                        K.op(pe, lambda e: e.matmul(pt[:, :], lhsT=onesb[0:1, :], rhs=bg2[0:1, cs_], start=True, stop=False), r=[onesb_b, bg2_b], w=[pt_b], mark=False)
                        K.op(pe, lambda e: e.matmul(pt[:, :], lhsT=lrT[:, tq], rhs=wg2[:, cs_], start=False, stop=True), r=[lrT_b, wg2_b], w=[pt_b])
                        K.op(act, lambda e: e.activation(out=e1[:, cs_], in_=pt[:, :], func=AF.Exp, scale=-1.0), r=[pt_b], w=[e1_b])
                    K.op(act, lambda e: e.activation(out=spl[:], in_=e1[:], func=AF.Ln, bias=1.0), r=[e1_b], w=[spl_b])
                    for hf in range(2):
                        pt, pt_b = pst(4 + hf)
                        for j in range(4):
                            c8 = hf * 4 + j
                            K.op(pe, lambda e, c8=c8, j=j: e.matmul(pt[:, j * 128:(j + 1) * 128], lhsT=spl[:, c8 * 128:(c8 + 1) * 128], rhs=utri, start=True, stop=True),
                                 r=[spl_b, cmb_b], w=[pt_b], mark=(j == 3))
                        bsl = slice(hf * 4, hf * 4 + 4)
                        K.op(act, lambda e: e.activation(out=EbT[:, bsl, :], in_=pt[:, :].rearrange("p (a b) -> p a b", b=128), func=AF.Exp), r=[pt_b], w=[EbT_b])
                        K.op(act, lambda e: e.activation(out=EnbT[:, bsl, :], in_=pt[:, :].rearrange("p (a b) -> p a b", b=128), func=AF.Exp, scale=-1.0), r=[pt_b], w=[EnbT_b])
                    K.op(dve, lambda e: e.tensor_tensor(out=qe[:], in0=qTg[:, :, tq], in1=EbT[:], op=ALU.mult), r=[qTg_b, EbT_b], w=[qe_b])
                    K.op(pool, lambda e: e.tensor_copy(out=qeP[:, :, 0, 0:64], in_=qe[:, :, 0:64]), r=[qe_b], w=[qeP_b])
                    K.op(pool, lambda e: e.tensor_copy(out=qeP[:, :, 1, 64:128], in_=qe[:, :, 64:128]), r=[qe_b], w=[qeP_b])
                    K.op(dve, lambda e: e.tensor_tensor(out=ke[:], in0=kTg[:, :, tq], in1=EnbT[:], op=ALU.mult), r=[kTg_b, EnbT_b], w=[ke_b])
                    for hf in range(2):
                        pt, pt_b = pst(4 + hf)
                        cs_ = slice(hf * 512, (hf + 1) * 512)
                        K.op(pe, lambda e: e.matmul(pt[:, :], lhsT=ustrict, rhs=spl[:, cs_], start=True, stop=True), r=[spl_b, cmb_b], w=[pt_b])
                        K.op(act, lambda e: e.activation(out=Ec[:, cs_], in_=pt[:, :], func=AF.Exp), r=[pt_b], w=[Ec_b])
                    K.op(dve, lambda e: e.tensor_tensor(out=kd[:], in0=ktok[:, ts, :], in1=Ec[:], op=ALU.mult), r=[ktok_b, Ec_b], w=[kd_b])
                    pa, _ = pst(6)
                    for hh in range(B_HEADS):
                        for c2 in range(2):
                            c8 = hh * 2 + c2
                            K.op(pe, lambda e, c8=c8, c2=c2: e.matmul(pa[:, hh * 128:(hh + 1) * 128], lhsT=ke[:, c8, :], rhs=qe[:, c8, :], start=(c2 == 0), stop=(c2 == 1)),
                                 r=[ke_b, qe_b], w=[pa_bs[hh]], mark=(c2 == 1))
                        K.op(dve, lambda e: e.tensor_tensor(out=Am4[:, hh, :], in0=pa[:, hh * 128:(hh + 1) * 128], in1=trimf, op=ALU.mult), r=[pa_bs[hh], cmf_b], w=[Am_bs[hh]])
                    for hh in range(B_HEADS):
                        vs = slice(hh * B_DV, (hh + 1) * B_DV)
                        po, po_b = pst(hh)
                        K.op(pe, lambda e: e.matmul(po[:, :], lhsT=Am4[:, hh, :], rhs=vtok[:, ts, vs], start=True, stop=False), r=[Am_bs[hh], vtok_b], w=[po_b], mark=False)
                        for c2 in range(2):
                            c8 = hh * 2 + c2
                            K.op(pe, lambda e, c8=c8: e.matmul(po[:, :], lhsT=qeP[:, c8, 0, :], rhs=Sb[:, c8, :], start=False, stop=False),
                                 r=[qeP_b, Sb_bs[c8]], w=[po_b], mark=(c2 == 1))
                    for ck in range(2):
                        rows = slice(ck * 64, ck * 64 + 64)
                        if ck == 1:
                            for hh in range(B_HEADS):
                                po, po_b = pst(hh)
                                for c2 in range(2):
                                    c8 = hh * 2 + c2
                                    K.op(pe, lambda e, c8=c8: e.matmul(po[:, :], lhsT=qeP[:, c8, 1, :], rhs=Sb[:, c8, :], start=False, stop=(c2 == 1)),
                                         r=[qeP_b, Sb_bs[c8]], w=[po_b], mark=(c2 == 1))
                        for hh in range(B_HEADS):
                            vs = slice(hh * B_DV, (hh + 1) * B_DV)
                            for c2 in range(2):
                                c8 = hh * 2 + c2
                                pu, pu_b = pst((4, 5, 7)[uctr[0] % 3])
                                uctr[0] += 1
                                K.op(pe, lambda e, c8=c8: e.matmul(pu[:, :], lhsT=kd[rows, c8 * 128:(c8 + 1) * 128], rhs=vtok[rows, ts, vs], start=True, stop=True),
                                     r=[kd_b, vtok_b], w=[pu_b])
                                K.op(dve, lambda e, c8=c8, ck=ck: e.scalar_tensor_tensor(out=Sf[:, c8, :], in0=Sf[:, c8, :], scalar=EbT[:, c8, ck * 64 + 63:ck * 64 + 64], in1=pu[:, :],
                                                                                          op0=ALU.mult, op1=ALU.add), r=[Sf_bs[c8], EbT_b, pu_b], w=[Sf_bs[c8]])
                                K.op(act, lambda e, c8=c8: e.copy(out=Sb[:, c8, :], in_=Sf[:, c8, :]), r=[Sf_bs[c8]], w=[Sb_bs[c8]])
                    for hh in range(B_HEADS):
                        vs = slice(hh * B_DV, (hh + 1) * B_DV)
                        po, po_b = pst(hh)
                        K.op(act, lambda e: e.activation(out=sq[:], in_=po[:, :], func=AF.Square, accum_out=ssq4[:, hh:hh + 1]), r=[po_b], w=[sq_b, ss_bs[hh]])
                        K.op(act, lambda e: e.activation(out=rs4[:, hh:hh + 1], in_=ssq4[:, hh:hh + 1], func=AF.Sqrt, bias=epsc[:, 1:2]), r=[ss_bs[hh], epsc_b], w=[rs_bs[hh]])
                        K.op(dve, lambda e: e.reciprocal(out=rs4[:, hh:hh + 1], in_=rs4[:, hh:hh + 1]), r=[rs_bs[hh]], w=[rs_bs[hh]])
                        K.op(dve, lambda e: e.scalar_tensor_tensor(out=ob[:, vs], in0=po[:, :], scalar=rs4[:, hh:hh + 1], in1=Gt[:, ts, vs], op0=ALU.mult, op1=ALU.mult),
                             r=[po_b, rs_bs[hh], Gt_b], w=[ob_b])
                    r0 = t0 + ts * 128
                    K.dma(sp, o_s.ap()[r0:r0 + 128, 2048:4096], ob[:], ob_b, r=[ob_b])
            K.barrier()

        with ExitStack() as es2:
            oTs = [K.sb("oT%d" % i, [128, KC, 512], BF16, es2) for i in range(2)]
            ots = [K.sb("ot%d" % i, [128, D], BF16, es2) for i in range(2)]
            g1b, g1b_b = K.sb("g1b", [128, D], F32, es2)
            xbs = [K.sb("xb%d" % i, [128, 512], F32, es2) for i in range(3)]
            tms = [K.sb("tm%d" % i, [128, 512], F32, es2) for i in range(3)]
            K.dma(sp, g1b[:], g1b_s.ap(), g1b_b, w=[g1b_b])
            dctr = [0]

            def prepD(tt):
                t0 = tt * 512
                oT, oT_b = oTs[tt % 2]
                for ts in range(4):
                    ot, ot_b = ots[ts % 2]
                    K.dma(act, ot[:], o_s.ap()[t0 + ts * 128:t0 + (ts + 1) * 128, :], ot_b, w=[ot_b])
                    for g in range(8):
                        pt, pt_b = pst(4 + g % 2)
                        ptb = pt[:].bitcast(BF16)
                        for j in range(4):
                            kc = g * 4 + j
                            K.op(pe, lambda e, kc=kc, j=j: e.transpose(out=ptb[:, j * 128:(j + 1) * 128], in_=ot[:, kc * 128:(kc + 1) * 128], identity=identb),
                                 r=[ot_b, cmb_b], w=[pt_b], mark=(j == 3))
                        if g % 2:
                            K.op(act, lambda e, g=g: e.copy(out=oT[:, g * 4:g * 4 + 4, ts * 128:(ts + 1) * 128], in_=ptb[:, 0:512].rearrange("p (a b) -> p a b", b=128)), r=[pt_b], w=[oT_b])
                        else:
                            K.op(dve, lambda e, g=g: e.tensor_copy(out=oT[:, g * 4:g * 4 + 4, ts * 128:(ts + 1) * 128], in_=ptb[:, 0:512].rearrange("p (a b) -> p a b", b=128)), r=[pt_b], w=[oT_b])

            prepD(0)
            for tt in range(NT):
                t0 = tt * 512
                oT, oT_b = oTs[tt % 2]
                if tt + 1 < NT:
                    prepD(tt + 1)

                def evac_res(ps_ap, ps_b, ts, col, w):
                    i = dctr[0] % 3
                    dctr[0] += 1
                    xb, xb_b = xbs[i]
                    tm, tm_b = tms[i]
                    r0 = t0 + ts * 128
                    K.dma(act, xb[:, 0:w], x_a[r0:r0 + 128, col:col + w], xb_b, w=[xb_b])
                    K.op(dve, lambda e: e.tensor_tensor(out=tm[:, 0:w], in0=ps_ap, in1=g1b[:, col:col + w], op=ALU.mult), r=[ps_b, g1b_b], w=[tm_b])
                    K.op(dve, lambda e: e.scalar_tensor_tensor(out=tm[:, 0:w], in0=xb[:, 0:w], scalar=ALPHA, in1=tm[:, 0:w], op0=ALU.mult, op1=ALU.add), r=[xb_b, tm_b], w=[tm_b])
                    K.dma(sp, xp_s.ap()[r0:r0 + 128, col:col + w], tm[:, 0:w], tm_b, r=[tm_b])

                proj_M(oT, oT_b, wout_a, 0, D, evac_res)
            K.barrier()

        def ln_pass(src_a, dst_a, g_d, b_d, tag, side=None, nside=0):
            with ExitStack() as es2:
                sgen = side(es2) if side is not None else None
                gb, gb_b = K.sb("lng" + tag, [128, D], F32, es2)
                bb, bb_b = K.sb("lnb" + tag, [128, D], F32, es2)
                xts = [K.sb("lx%s%d" % (tag, i), [128, D], F32, es2) for i in range(2)]
                ys = [K.sb("ly%s%d" % (tag, i), [128, D], F32, es2) for i in range(2)]
                sts = [K.sb("lst%s%d" % (tag, i), [128, 8, 6], F32, es2) for i in range(2)]
                mvs = [K.sb("lmv%s%d" % (tag, i), [128, 2], F32, es2) for i in range(2)]
                rss = [K.sb("lrs%s%d" % (tag, i), [128, 1], F32, es2) for i in range(2)]
                K.dma(sp, gb[:], g_d.ap()[0:1, :].partition_broadcast(128).rearrange("p a h -> p (a h)"), gb_b, w=[gb_b])
                K.dma(sp, bb[:], b_d.ap()[0:1, :].partition_broadcast(128).rearrange("p a h -> p (a h)"), bb_b, w=[bb_b])

                def stA(qt):
                    xt, xt_b = xts[qt % 2]
                    K.dma(act, xt[:], src_a[qt * 128:(qt + 1) * 128, :], xt_b, w=[xt_b])
                    ln_stats(xt, xt_b, *mvs[qt % 2], *sts[qt % 2], *rss[qt % 2])

                def stB(qt):
                    xt, xt_b = xts[qt % 2]
                    y, y_b = ys[qt % 2]
                    mv, mv_b = mvs[qt % 2]
                    rstd, rstd_b = rss[qt % 2]
                    K.op(dve, lambda e: e.tensor_scalar(out=y[:], in0=xt[:], scalar1=mv[:, 0:1], scalar2=rstd[:, 0:1], op0=ALU.subtract, op1=ALU.mult), r=[xt_b, mv_b, rstd_b], w=[y_b])
                    K.op(dve, lambda e: e.tensor_tensor(out=y[:], in0=y[:], in1=gb[:], op=ALU.mult), r=[y_b, gb_b], w=[y_b])
                    K.op(pool, lambda e: e.tensor_tensor(out=y[:], in0=y[:], in1=bb[:], op=ALU.add), r=[y_b, bb_b], w=[y_b])
                    K.dma(sp, dst_a[qt * 128:(qt + 1) * 128, :], y[:], y_b, r=[y_b])

                stA(0)
                for qt in range(NQ):
                    if qt + 1 < NQ:
                        stA(qt + 1)
                    stB(qt)
                    if sgen is not None:
                        for _ in range(nside):
                            next(sgen, None)
                if sgen is not None:
                    for _ in sgen:
                        pass
                K.barrier()

        ln_pass(xp_s.ap(), x1_s.ap(), ln1g_d, ln1b_d, "1", side=lambda es_: adaln_gen(es_, list(range(40, 48)), ldq=pool), nside=(8 * 128 + S - 1) // S)

        with ExitStack() as es2:
            hT, hT_b = K.sb("hT_E", [128, KC, 512], BF16, es2)
            fT, fT_b = K.sb("fT", [128, FC, 512], BF16, es2)
            cw, cw_b = K.sb("cw", [128, FC, 3], F32, es2)
            cbias, cbias_b = K.sb("cbias", [128, FC], F32, es2)
            carry, carry_b = K.sb("carry", [128, FC, 2], F32, es2)
            K.dma(sp, cw[:], convw_d.ap(), cw_b, w=[cw_b])
            K.dma(sp, cbias[:], convb_d.ap(), cbias_b, w=[cbias_b])
            K.op(dve, lambda e: e.memset(carry[:], 0.0), w=[carry_b])
            htiles = alloc_ht_tiles(es2, xt_alias=fT[:, 0:16, :].rearrange("p a b -> p (a b)").bitcast(F32), xh_alias=fT[:, 16:24, :].rearrange("p a b -> p (a b)"))
            ubs = [K.sb("ub%d" % i, [128, 516], F32, es2) for i in range(2)]
            a1s = [K.sb("a1%d" % i, [128, 512], F32, es2) for i in range(2)]
            g2b, g2b_b = K.sb("g2b", [128, D], F32, es2)
            xbs = [K.sb("fxb%d" % i, [128, 512], F32, es2) for i in range(2)]
            tms = [K.sb("ftm%d" % i, [128, 512], F32, es2) for i in range(2)]
            K.dma(sp, g2b[:], g2b_s.ap(), g2b_b, w=[g2b_b])
            dctr = [0]
            slabs = [(k0, min(16, FC - k0)) for k0 in range(0, FC, 16)]
            for tt in range(NT):
                t0 = tt * 512
                if tt > 0:
                    K.barrier()
                make_hT(es2, x1_s.ap(), t0, 2, hT, hT_b, htiles)
                for f2 in range(FC // 2):
                    wu, wu_b = load_w(wup_a, f2 * 256, 256)
                    wg, wg_b = load_w(wgate_a, f2 * 256, 256)
                    for j in range(2):
                        fc = f2 * 2 + j
                        pu, pu_b = pst(fc % 4)
                        for kc in range(KC):
                            K.op(pe, lambda e, kc=kc: e.matmul(pu[:, :], lhsT=wu[:, kc, j * 128:(j + 1) * 128], rhs=hT[:, kc, :], start=(kc == 0), stop=(kc == KC - 1)),
                                 r=[wu_b, hT_b], w=[pu_b], mark=(kc == KC - 1))
                        ub, ub_b = ubs[fc % 2]
                        a1, a1_b = a1s[fc % 2]
                        K.op(act, lambda e: e.copy(out=ub[:, 2:514], in_=pu[:, :]), r=[pu_b], w=[ub_b])
                        K.op(pool, lambda e, fc=fc: e.tensor_copy(out=ub[:, 0:2], in_=carry[:, fc, :]), r=[carry_b], w=[ub_b])
                        K.op(pool, lambda e, fc=fc: e.tensor_copy(out=carry[:, fc, :], in_=ub[:, 512:514]), r=[ub_b], w=[carry_b])
                        K.op(dve, lambda e, fc=fc: e.tensor_scalar(out=a1[:], in0=ub[:, 2:514], scalar1=cw[:, fc, 2:3], scalar2=cbias[:, fc:fc + 1], op0=ALU.mult, op1=ALU.add),
                             r=[ub_b, cw_b, cbias_b], w=[a1_b])
                        K.op(dve, lambda e, fc=fc: e.scalar_tensor_tensor(out=a1[:], in0=ub[:, 1:513], scalar=cw[:, fc, 1:2], in1=a1[:], op0=ALU.mult, op1=ALU.add),
                             r=[ub_b, cw_b, a1_b], w=[a1_b])
                        K.op(dve, lambda e, fc=fc: e.scalar_tensor_tensor(out=a1[:], in0=ub[:, 0:512], scalar=cw[:, fc, 0:1], in1=a1[:], op0=ALU.mult, op1=ALU.add),
                             r=[ub_b, cw_b, a1_b], w=[a1_b])
                        K.op(act, lambda e: e.activation(out=a1[:], in_=a1[:], func=AF.Gelu_apprx_tanh), r=[a1_b], w=[a1_b])
                    for j in range(2):
                        fc = f2 * 2 + j
                        pg, pg_b = pst(4 + fc % 4)
                        a1, a1_b = a1s[fc % 2]
                        for kc in range(KC):
                            K.op(pe, lambda e, kc=kc: e.matmul(pg[:, :], lhsT=wg[:, kc, j * 128:(j + 1) * 128], rhs=hT[:, kc, :], start=(kc == 0), stop=(kc == KC - 1)),
                                 r=[wg_b, hT_b], w=[pg_b], mark=(kc == KC - 1))
                        K.op(dve, lambda e, fc=fc: e.tensor_tensor(out=fT[:, fc, :], in0=a1[:], in1=pg[:, :], op=ALU.mult), r=[a1_b, pg_b], w=[fT_b])
                for db in range(D // 512):
                    col = db * 512
                    for (k0, nk) in slabs:
                        wt, wt_b = nextw()
                        wtv = wt[:].rearrange("p a b -> p (a b)")[:, 0:16 * 512].rearrange("p (a b) -> p a b", b=512)
                        K.dma(pool, wtv[:, 0:nk, :], wdown_a[:, k0:k0 + nk, col:col + 512], wt_b, w=[wt_b])
                        for ts in range(4):
                            pt, pt_b = pst((db % 2) * 4 + ts)
                            for kl in range(nk):
                                fc = k0 + kl
                                K.op(pe, lambda e, fc=fc, kl=kl: e.matmul(pt[:, :], lhsT=fT[:, fc, ts * 128:(ts + 1) * 128], rhs=wtv[:, kl, :], start=(fc == 0), stop=(fc == FC - 1)),
                                     r=[fT_b, wt_b], w=[pt_b], mark=(kl == nk - 1))
                    for ts in range(4):
                        pt, pt_b = pst((db % 2) * 4 + ts)
                        i = dctr[0] % 2
                        dctr[0] += 1
                        xb, xb_b = xbs[i]
                        tm, tm_b = tms[i]
                        r0 = t0 + ts * 128
                        K.dma(act, xb[:], x1_s.ap()[r0:r0 + 128, col:col + 512], xb_b, w=[xb_b])
                        K.op(dve, lambda e: e.tensor_tensor(out=tm[:], in0=pt[:, :], in1=g2b[:, col:col + 512], op=ALU.mult), r=[pt_b, g2b_b], w=[tm_b])
                        K.op(dve, lambda e: e.scalar_tensor_tensor(out=tm[:], in0=xb[:], scalar=ALPHA, in1=tm[:], op0=ALU.mult, op1=ALU.add), r=[xb_b, tm_b], w=[tm_b])
                        K.dma(sp, xp_s.ap()[r0:r0 + 128, col:col + 512], tm[:], tm_b, r=[tm_b])
            K.barrier()

        ln_pass(xp_s.ap(), out_a, ln2g_d, ln2b_d, "2")
        K.barrier()
    return nc


def make_in_maps(inp, S):
    cmask, oh = host_consts()
    B = inp["x"].shape[0]
    f = lambda a: np.ascontiguousarray(np.asarray(a, dtype=np.float32))
    shared = {
        "t5": f(inp["t5_table"]),
        "w_ada": f(inp["w_ada"][0]), "b_ada": f(inp["b_ada"][0]).reshape(1, -1),
        "w_in": f(inp["w_in"][0]), "w_g2": f(inp["w_g2"][0]), "b_g2": f(inp["b_g2"][0]).reshape(1, -1),
        "gla_norm": f(inp["gla_norm"][0]).reshape(1, -1), "w_out": f(inp["w_out"][0]),
        "ln1_g": f(inp["ln1_g"][0]).reshape(1, -1), "ln1_b": f(inp["ln1_b"][0]).reshape(1, -1),
        "w_up": f(inp["w_up"][0]), "w_gate": f(inp["w_gate"][0]),
        "conv_wT": f(np.asarray(inp["conv_w"][0]).reshape(3, FC, 128).transpose(2, 1, 0)),
        "conv_bT": f(np.asarray(inp["conv_b"][0]).reshape(FC, 128).T),
        "w_down": f(inp["w_down"][0]),
        "ln2_g": f(inp["ln2_g"][0]).reshape(1, -1), "ln2_b": f(inp["ln2_b"][0]).reshape(1, -1),
        "cmask": f(cmask), "onehot": f(oh),
    }
    maps = []
    for b in range(B):
        m = dict(shared)
        m["x"] = f(inp["x"][b])
        m["cT"] = f(np.asarray(inp["c"][b]).reshape(KC, 128).T)
        maps.append(m)
    return maps


def kernel(**inputs):
    S = inputs["x"].shape[1]
    B = inputs["x"].shape[0]
    nc = build(S)
    maps = make_in_maps(inputs, S)
    res = run_bass_kernel_spmd(nc, maps, core_ids=list(range(B)))
    return np.stack([np.asarray(r["out"], dtype=np.float32) for r in res.results], axis=0)
```
